# Optimizing a Trainium2 kernel written in Bass

```python
import jax, jax.numpy as jnp
from jax import lax
import numpy as np

D_MODEL = 2048
BATCH = 4
SEQ = 2048
DEPTH = 1

N_HEADS = 16
QK_NOPE_DIM = 128
QK_ROPE_DIM = 64
QK_HEAD_DIM = QK_NOPE_DIM + QK_ROPE_DIM
V_HEAD_DIM = 128
Q_LORA_RANK = 512
KV_LORA_RANK = 256
ATTN_WIDTH = N_HEADS * V_HEAD_DIM
CONV_WIDTH = D_MODEL // 2
CONV_K = 3
PLE_DIM = 256
Q_BLOCK = 128
ROPE_THETA = 10000.0
EPS = 1e-6

SPLIT_SIZES = (Q_LORA_RANK, KV_LORA_RANK, QK_ROPE_DIM, ATTN_WIDTH,
               CONV_WIDTH, CONV_WIDTH, CONV_WIDTH, CONV_WIDTH, D_MODEL, D_MODEL)
IN_WIDTH = sum(SPLIT_SIZES)

kernel_name = "hybrid_mla_shortconv_gated_block"


def rmsnorm(x, g):
    xf = x.astype(jnp.float32)
    xf = xf * lax.rsqrt(jnp.mean(xf * xf, axis=-1, keepdims=True) + EPS)
    return xf.astype(x.dtype) * g


def rope_tables(positions, dtype):
    inv_freq = 1.0 / (ROPE_THETA ** (jnp.arange(0, QK_ROPE_DIM, 2, dtype=jnp.float32) / QK_ROPE_DIM))
    ang = positions.astype(jnp.float32)[..., None] * inv_freq
    return jnp.cos(ang)[:, :, None, :].astype(dtype), jnp.sin(ang)[:, :, None, :].astype(dtype)


def apply_rope(x, cos, sin):
    x1, x2 = jnp.split(x, 2, axis=-1)
    return jnp.concatenate([x1 * cos - x2 * sin, x2 * cos + x1 * sin], axis=-1)


def split_columns(z):
    idx, acc = [], 0
    for s in SPLIT_SIZES[:-1]:
        acc += s
        idx.append(acc)
    return jnp.split(z, idx, axis=-1)


def blocked_attention(q, k, v):
    b, s, h, d = q.shape
    scale = QK_HEAD_DIM ** -0.5
    qb = q.reshape(b, s // Q_BLOCK, Q_BLOCK, h, d).transpose(1, 0, 2, 3, 4)

    def attend(q_blk):
        sc = jnp.einsum('bqhd,bkhd->bhqk', q_blk, k).astype(jnp.float32) * scale
        pr = jax.nn.softmax(sc, axis=-1).astype(v.dtype)
        return jnp.einsum('bhqk,bkhv->bqhv', pr, v)

    out = lax.map(attend, qb)
    return out.transpose(1, 0, 2, 3, 4).reshape(b, s, h * v.shape[-1])


def centred_conv3(u, w, bias):
    up = jnp.pad(u, ((0, 0), (1, 1), (0, 0)))
    return up[:, :-2] * w[0] + up[:, 1:-1] * w[1] + up[:, 2:] * w[2] + bias


def setup_inputs(seed: int = 0) -> dict:
    key = jax.random.key(seed)
    ks = jax.random.split(key, 20)
    f32 = jnp.float32

    def w(k, shape, fan_in):
        return jax.random.normal(k, shape, f32) * (fan_in ** -0.5)

    def gain(k, shape):
        return 1.0 + 0.02 * jax.random.normal(k, shape, f32)

    return {
        "x": jax.random.normal(ks[0], (BATCH, SEQ, D_MODEL), f32),
        "p": jax.random.normal(ks[1], (DEPTH, BATCH, SEQ, PLE_DIM), f32),
        "positions": jnp.broadcast_to(jnp.arange(SEQ, dtype=jnp.int32), (BATCH, SEQ)),
        "norm_g": gain(ks[2], (DEPTH, D_MODEL)),
        "w_in": w(ks[3], (DEPTH, D_MODEL, IN_WIDTH), D_MODEL),
        "q_lat_g": gain(ks[4], (DEPTH, Q_LORA_RANK)),
        "kv_lat_g": gain(ks[5], (DEPTH, KV_LORA_RANK)),
        "w_uq": w(ks[6], (DEPTH, Q_LORA_RANK, N_HEADS * QK_HEAD_DIM), Q_LORA_RANK),
        "w_ukv": w(ks[7], (DEPTH, KV_LORA_RANK, N_HEADS * (QK_NOPE_DIM + V_HEAD_DIM)), KV_LORA_RANK),
        "q_norm_g": gain(ks[8], (DEPTH, QK_HEAD_DIM)),
        "k_norm_g": gain(ks[9], (DEPTH, QK_HEAD_DIM)),
        "conv_w": w(ks[10], (DEPTH, CONV_K, CONV_WIDTH), CONV_K),
        "conv_b": 0.02 * jax.random.normal(ks[11], (DEPTH, CONV_WIDTH), f32),
        "w_branch_attn": w(ks[12], (DEPTH, ATTN_WIDTH, D_MODEL), ATTN_WIDTH),
        "w_branch_conv": w(ks[13], (DEPTH, CONV_WIDTH, D_MODEL), CONV_WIDTH),
        "w_out": w(ks[14], (DEPTH, D_MODEL, D_MODEL), D_MODEL),
        "ple_norm_g": gain(ks[15], (DEPTH, D_MODEL)),
        "w_ple_gate": w(ks[16], (DEPTH, D_MODEL, D_MODEL), D_MODEL),
        "w_ple_proj": w(ks[17], (DEPTH, PLE_DIM, D_MODEL), PLE_DIM),
    }


def reference(x, p, positions, norm_g, w_in, q_lat_g, kv_lat_g, w_uq, w_ukv,
              q_norm_g, k_norm_g, conv_w, conv_b, w_branch_attn, w_branch_conv,
              w_out, ple_norm_g, w_ple_gate, w_ple_proj):
    b, s, _ = x.shape
    cos, sin = rope_tables(positions, x.dtype)
    for i in range(DEPTH):
        h = rmsnorm(x, norm_g[i])
        z = jnp.einsum('bsd,de->bse', h, w_in[i])
        (q_a, kv_a, k_pe, gate_a, conv_b_gate, conv_c_gate, conv_x,
         gate_c, merge_a, merge_c) = split_columns(z)

        q = jnp.einsum('bsr,re->bse', rmsnorm(q_a, q_lat_g[i]), w_uq[i])
        q = q.reshape(b, s, N_HEADS, QK_HEAD_DIM)
        kv = jnp.einsum('bsr,re->bse', rmsnorm(kv_a, kv_lat_g[i]), w_ukv[i])
        kv = kv.reshape(b, s, N_HEADS, QK_NOPE_DIM + V_HEAD_DIM)
        k_nope, v = kv[..., :QK_NOPE_DIM], kv[..., QK_NOPE_DIM:]
        k_pe_h = jnp.broadcast_to(k_pe[:, :, None, :], (b, s, N_HEADS, QK_ROPE_DIM))
        k = jnp.concatenate([k_nope, k_pe_h], axis=-1)
        q = rmsnorm(q, q_norm_g[i])
        k = rmsnorm(k, k_norm_g[i])
        q = jnp.concatenate([q[..., :QK_NOPE_DIM], apply_rope(q[..., QK_NOPE_DIM:], cos, sin)], axis=-1)
        k = jnp.concatenate([k[..., :QK_NOPE_DIM], apply_rope(k[..., QK_NOPE_DIM:], cos, sin)], axis=-1)
        attn = blocked_attention(q, k, v)
        y_attn = jnp.einsum('bse,ed->bsd', attn * jax.nn.silu(gate_a), w_branch_attn[i])

        cv = conv_b_gate * centred_conv3(conv_c_gate * conv_x, conv_w[i], conv_b[i])
        y_conv = jnp.einsum('bse,ed->bsd', cv * jax.nn.silu(gate_c), w_branch_conv[i])

        merged = jax.nn.sigmoid(merge_a) * y_attn + jax.nn.sigmoid(merge_c) * y_conv
        x = x + jnp.einsum('bsd,de->bse', merged, w_out[i])

        gate = jax.nn.sigmoid(jnp.einsum('bsd,de->bse', rmsnorm(x, ple_norm_g[i]), w_ple_gate[i]))
        x = x + gate * jnp.einsum('bsp,pd->bsd', p[i], w_ple_proj[i])
    return x
```

```python
import math
from contextlib import ExitStack

import numpy as np
import concourse.bass as bass
import concourse.mybir as mybir
from concourse.bass_utils import run_bass_kernel_spmd

F32 = mybir.dt.float32
BF16 = mybir.dt.bfloat16
I32 = mybir.dt.int32
ALU = mybir.AluOpType
AF = mybir.ActivationFunctionType

D = 2048
S = 2048
T = 1024
NH = 16
EPS = 1e-6
IN_W = 11072
C_GA, C_CB, C_CC, C_CX, C_GC, C_MA, C_MC = 832, 2880, 3904, 4928, 5952, 6976, 9024
NCOL = 48
COL_GQN, COL_GQAUG, COL_GKN, COL_INVF, COL_PH, COL_ML, COL_MR, COL_EPS, COL_LNS, COL_CW, COL_CBIAS = 0, 1, 2, 3, 4, 5, 6, 7, 8, 9, 33

MAGIC = 12582912.0
INV2PI = 0.15915494309189535
CW1 = 6.28125
CW2 = 2.0 * math.pi - 6.28125
PI_LO = 3.1415925


class Res:
    def __init__(self, name, excl=False):
        self.name = name
        self.w = None
        self.r = {}
        self.rd = []
        self.excl = excl


class Slot:
    def __init__(self, name):
        self.name = name
        self.total = 0
        self.sem = None


class Op:
    __slots__ = ("eng", "fn", "deps", "signal", "count", "slot", "val")

    def __init__(self, eng, fn, deps, slot=None):
        self.eng = eng
        self.fn = fn
        self.deps = deps
        self.signal = False
        self.count = None
        self.slot = slot
        self.val = None


class Prog:
    ENGS = ("pe", "act", "dve", "pool", "sp")

    def __init__(self):
        self.ops = {e: [] for e in self.ENGS}
        self.slots = []
        self.last_real = {}
        self.bar = []

    def slot(self, name):
        s = Slot(name)
        self.slots.append(s)
        return s

    def _deps(self, eng, reads, writes, deps):
        d = [x for x in deps if x is not None]
        for r in reads:
            if r.w is not None:
                d.append(r.w)
            if r.excl:
                d.extend(o for e, o in r.r.items() if e != eng)
        for w in writes:
            if w.w is not None:
                d.append(w.w)
            d.extend(w.r.values())
            d.extend(w.rd)
        return d

    def _track(self, o, reads, writes):
        for r in reads:
            if o.slot is not None:
                r.rd.append(o)
            else:
                r.r[o.eng] = o
        for w in writes:
            w.w = o
            w.r = {}
            w.rd = []

    def op(self, eng, fn, reads=(), writes=(), deps=(), real=True):
        o = Op(eng, fn, self._deps(eng, reads, writes, deps))
        self._track(o, reads, writes)
        self.ops[eng].append(o)
        if real:
            self.last_real[eng] = o
        return o

    def dma(self, eng, fn, slot, reads=(), writes=(), deps=()):
        dl = [d for d in self._deps(eng, reads, writes, list(deps) + self.bar) if d.slot is not slot]
        o = Op(eng, fn, dl, slot=slot)
        slot.total += 16
        o.val = slot.total
        self._track(o, reads, writes)
        self.ops[eng].append(o)
        return o

    def barrier(self):
        last = [self.last_real[e] for e in ("pe", "act", "dve", "pool") if e in self.last_real]
        for e in self.ENGS:
            self.op(e, lambda h: h.nop(), deps=last, real=False)
        self.bar = last

    def finalize(self):
        for e in self.ENGS:
            for o in self.ops[e]:
                for d in o.deps:
                    if d.slot is None and not (d.eng == "pe" and e == "pe"):
                        d.signal = True
        for e in self.ENGS:
            c = 0
            for o in self.ops[e]:
                if o.slot is None and o.signal:
                    c += 1
                    o.count = c

    def emit(self, eng, h, sems):
        waited = {}
        cut = getattr(self, "cut", None)
        ops = self.ops[eng] if cut is None else self.ops[eng][:cut[eng]]
        for o in ops:
            for d in o.deps:
                if d.slot is not None:
                    key = id(d.slot)
                    if waited.get(key, 0) < d.val:
                        h.wait_ge(d.slot.sem, d.val)
                        waited[key] = d.val
                else:
                    if d.eng == "pe" and eng == "pe":
                        continue
                    if waited.get(d.eng, 0) < d.count:
                        h.wait_ge(sems[d.eng], d.count)
                        waited[d.eng] = d.count
            inst = o.fn(h)
            if o.slot is not None:
                inst.then_inc(o.slot.sem, 16)
            elif o.signal:
                inst.then_inc(sems[eng], 1)


class Buf:
    def __init__(self, ap, name):
        self.ap = ap
        self.res = Res(name)
        self.rl = None


def build_nc(debug=False, stop=None):
    nc = bass.Bass("TRN2", target_bir_lowering=False)
    P = Prog()

    def din(name, shape, dt=F32):
        return nc.dram_tensor(name, list(shape), dt, kind="ExternalInput").ap()

    x_d = din("x_loc", [S, D])
    p_d = din("p_loc", [T, 256])
    postm_d = din("pos_tm", [128, 16], I32)
    posrow_d = din("pos_row", [128, T], I32)
    cols_d = din("cols", [128, NCOL])
    rowsa_d = din("rows_a", [128, 832])
    normg_d = din("normg_row", [128, D])
    pleg_d = din("pleg_row", [128, D])
    invf4_d = din("invf4", [128, 128])
    phrow_d = din("ph_row", [128, 128])
    w_in_d = din("w_in", [D, IN_W])
    w_uq_d = din("w_uq", [512, 3072])
    w_ukv_d = din("w_ukv", [256, 4096])
    w_ba_d = din("w_ba", [D, D])
    w_bc_d = din("w_bc", [1024, D])
    w_out_d = din("w_out", [D, D])
    w_pg_d = din("w_pg", [D, D])
    w_pp_d = din("w_pp", [256, D])
    out_d = nc.dram_tensor("out", [T, D], F32, kind="ExternalOutput").ap()
    dbg_outs = {}

    w_in_v = w_in_d.rearrange("(c p) n -> p c n", p=128)
    w_uq_v = w_uq_d.rearrange("(c p) n -> p c n", p=128)
    w_ukv_v = w_ukv_d.rearrange("(c p) n -> p c n", p=128)
    w_ba_v = w_ba_d.rearrange("(c p) n -> p c n", p=128)
    w_bc_v = w_bc_d.rearrange("(c p) n -> p c n", p=128)
    w_out_v = w_out_d.rearrange("(c p) n -> p c n", p=128)
    w_pg_v = w_pg_d.rearrange("(c p) n -> p c n", p=128)
    w_pp_v = w_pp_d.rearrange("(c p) n -> p c n", p=128)

    with ExitStack() as es:
        ARENA_BYTES = 204800
        arena = es.enter_context(nc.sbuf_tensor("arena", [128, ARENA_BYTES // 4], F32))

        def view(off, dt, shape, name):
            n = 1
            for s_ in shape:
                n *= s_
            el = 2 if dt == BF16 else 4
            assert off % 4 == 0 and (n * el) % 4 == 0, (name, off, n)
            assert off + n * el <= ARENA_BYTES, (name, off, n * el)
            ap = arena[:, off // 4:(off + n * el) // 4]
            if dt != F32:
                ap = ap.bitcast(dt)
            if len(shape) == 2:
                ap = ap.rearrange("p (a b) -> p a b", a=shape[0])
            elif len(shape) == 3:
                ap = ap.rearrange("p (a b c) -> p a b c", a=shape[0], b=shape[1])
            return Buf(ap, name)

        class Carver:
            def __init__(self, start, end):
                self.off = start
                self.end = end

            def take(self, dt, shape, name):
                n = 1
                for s_ in shape:
                    n *= s_
                nb = n * (2 if dt == BF16 else 4)
                nb_al = (nb + 31) // 32 * 32
                b = view(self.off, dt, shape, name)
                self.off += nb_al
                assert self.off <= self.end, (name, self.off, self.end)
                return b

        def sb(name, shape, dt=F32):
            t = es.enter_context(nc.sbuf_tensor("sb_" + name, list(shape), dt))
            b = Buf(None, name)
            b.t = t
            return b

        cols = sb("cols", [128, NCOL])
        ident = sb("ident", [128, 128], BF16)
        identf = sb("identf", [128, 128])
        ones = sb("ones", [128, 128], BF16)
        ss = sb("ss", [128, 16])
        rs = sb("rs", [128, 16])
        ssq = sb("ssq", [128, 16])
        rq = sb("rq", [128, 16])
        sskv = sb("sskv", [128, 16])
        rkv = sb("rkv", [128, 16])
        sspe = sb("sspe", [128, 16])
        lnt = sb("lnt", [128, 16])
        skt = [sb(f"skt{i}", [128, 16]) for i in range(2)]
        sk = [sb(f"sk{i}", [128, 16]) for i in range(2)]
        ss2 = sb("ss2", [128, 8])
        rs2 = sb("rs2", [128, 8])
        posf = sb("posf", [128, 16])
        postm = sb("postm", [128, 16], I32)
        hT_halo = sb("hT_halo", [128, 16, 2], BF16)
        hcx = sb("hcx", [128, 4])

        def col(i):
            return cols.t[:, i:i + 1]

        banks = []
        for i in range(8):
            t = es.enter_context(nc.psum_tensor(f"bank{i}", [128, 512], F32))
            b = Buf(t[:], f"bank{i}")
            b.res.excl = True
            b.bf = t[:].bitcast(BF16)
            banks.append(b)

        sems = {e: es.enter_context(nc.semaphore(f"s_{e}")) for e in Prog.ENGS}

        hT_own = view(0, BF16, [16, 1024], "hT_own")
        AT = view(32768, BF16, [16, 1024], "AT")
        qaT = view(65536, BF16, [4, 1024], "qaT")
        kvaT = view(73728, BF16, [2, 2048], "kvaT")
        KrT = view(81920, BF16, [2048], "KrT")
        CS_fm = view(86016, F32, [1024], "CS_fm")
        L0 = 90112

        dq = {"sp": P.slot("sp_misc")}

        def maybe_stop(tag):
            if stop == tag and getattr(P, "cut", None) is None:
                P.barrier()
                fd = [o for o in P.ops["sp"] if o.slot is dq["sp"]][-1:]
                P.op("sp", lambda h: h.nop(), deps=fd + P.bar, real=False)
                P.cut = {e: len(P.ops[e]) for e in P.ENGS}
                print("STOP at", tag, {e: len(P.ops[e]) for e in P.ENGS})

        def ld(eng, out_ap, in_ap, wbuf, slot=None, reads=(), deps=()):
            if slot is None:
                if not hasattr(wbuf, "slot"):
                    wbuf.slot = P.slot("ld_" + wbuf.res.name)
                slot = wbuf.slot
            return P.dma(eng, lambda h: h.dma_start(out=out_ap, in_=in_ap), slot,
                         reads=R(*reads), writes=R(wbuf), deps=deps)

        def R(*bufs):
            out_ = []
            for b in bufs:
                out_.extend(b.rl if b.rl is not None else [b.res])
            return out_

        ld("sp", cols.t[:], cols_d, cols)
        ld("sp", postm.t[:], postm_d, postm)

        P.op("dve", lambda h: h.memset(identf.t[:], 1.0), writes=R(identf))
        P.op("pool", lambda h: h.affine_select(out=identf.t[:], in_=identf.t[:], pattern=[[-1, 128]],
                                               compare_op=ALU.is_equal, fill=0.0, base=0, channel_multiplier=1),
             reads=R(identf), writes=R(identf))
        P.op("dve", lambda h: h.tensor_copy(out=ident.t[:], in_=identf.t[:]), reads=R(identf), writes=R(ident))
        P.op("dve", lambda h: h.memset(ones.t[:], 1.0), writes=R(ones))
        for b_ in (ss, ssq, sskv, sspe, ss2):
            P.op("dve", lambda h, b_=b_: h.memset(b_.t[:], 0.0), writes=R(b_))

        W_BASE = 147456
        c0 = Carver(L0, W_BASE)
        xs = [c0.take(F32, [2048], f"xs{i}") for i in range(2)]
        hb = [c0.take(BF16, [2048], f"hb{i}") for i in range(2)]
        hTt = [c0.take(BF16, [16, 128], f"hTt{i}") for i in range(2)]
        normg = c0.take(F32, [2048], "normg")
        CS_tm = c0.take(F32, [16, 128], "CS_tm")
        rows_a = c0.take(F32, [832], "rows_a")
        qa_n = c0.take(BF16, [512], "qa_n")
        kva_n = c0.take(BF16, [256], "kva_n")
        kpe_g = c0.take(F32, [64], "kpe_g")
        tcb = c0.take(F32, [64], "tcb")
        tsb = c0.take(F32, [64], "tsb")
        Kr_tm = c0.take(BF16, [128], "Kr_tm")
        c0 = Carver(W_BASE, ARENA_BYTES)
        trA = c0.take(F32, [2048], "trA")
        trB = c0.take(F32, [2048], "trB")
        trC = c0.take(F32, [2048], "trC")
        posrow = c0.take(I32, [1024], "posrow")
        invf4 = c0.take(F32, [128], "invf4")
        phrow = c0.take(F32, [128], "phrow")
        Wlat = view(32768, BF16, [16, 832], "Wlat")
        junk = view(32768 + 26624, BF16, [2048], "junk")

        ld("sp", rows_a.ap, rowsa_d, rows_a)
        ld("sp", normg.ap, normg_d, normg)
        ld("sp", invf4.ap, invf4_d, invf4)
        ld("sp", phrow.ap, phrow_d, phrow)
        ld("sp", posrow.ap, posrow_d, posrow)
        ld("pool", Wlat.ap, w_in_v[:, :, 0:832], Wlat)

        def range_reduce(ang, kf, n):
            a, k = ang.ap[:, 0:n], kf.ap[:, 0:n]
            P.op("dve", lambda h: h.tensor_scalar(out=k, in0=a, scalar1=INV2PI, scalar2=MAGIC, op0=ALU.mult, op1=ALU.add),
                 reads=R(ang), writes=R(kf))
            P.op("dve", lambda h: h.tensor_scalar(out=k, in0=k, scalar1=-MAGIC, scalar2=None, op0=ALU.add),
                 reads=R(kf), writes=R(kf))
            P.op("dve", lambda h: h.scalar_tensor_tensor(out=a, in0=k, scalar=-CW1, in1=a, op0=ALU.mult, op1=ALU.add),
                 reads=R(kf, ang), writes=R(ang))
            P.op("dve", lambda h: h.scalar_tensor_tensor(out=a, in0=k, scalar=-CW2, in1=a, op0=ALU.mult, op1=ALU.add),
                 reads=R(kf, ang), writes=R(ang))

        def wrap_clamp(ang, m, n):
            a, k = ang.ap[:, 0:n], m.ap[:, 0:n]
            P.op("dve", lambda h: h.tensor_scalar(out=k, in0=a, scalar1=math.pi, scalar2=-2.0 * math.pi, op0=ALU.is_gt, op1=ALU.mult),
                 reads=R(ang), writes=R(m))
            P.op("dve", lambda h: h.tensor_tensor(out=a, in0=a, in1=k, op=ALU.add), reads=R(ang, m), writes=R(ang))
            P.op("dve", lambda h: h.tensor_scalar(out=a, in0=a, scalar1=PI_LO, scalar2=-PI_LO, op0=ALU.min, op1=ALU.max),
                 reads=R(ang), writes=R(ang))

        P.op("dve", lambda h: h.tensor_copy(out=trA.ap[:, 0:1024], in_=posrow.ap), reads=R(posrow), writes=R(trA))
        P.op("dve", lambda h: h.tensor_scalar(out=trA.ap[:, 0:1024], in0=trA.ap[:, 0:1024], scalar1=col(COL_INVF), scalar2=None, op0=ALU.mult),
             reads=R(trA, cols), writes=R(trA))
        range_reduce(trA, trB, 1024)
        P.op("dve", lambda h: h.tensor_scalar(out=trA.ap[:, 0:1024], in0=trA.ap[:, 0:1024], scalar1=col(COL_PH), scalar2=None, op0=ALU.add),
             reads=R(trA, cols), writes=R(trA))
        wrap_clamp(trA, trB, 1024)
        P.op("act", lambda h: h.activation(out=CS_fm.ap, in_=trA.ap[:, 0:1024], func=AF.Sin), reads=R(trA), writes=R(CS_fm))

        P.op("dve", lambda h: h.tensor_copy(out=posf.t[:], in_=postm.t[:]), reads=R(postm), writes=R(posf))
        for t in range(16):
            P.op("dve", lambda h, t=t: h.tensor_scalar(out=trC.ap[:, t * 128:(t + 1) * 128], in0=invf4.ap, scalar1=posf.t[:, t:t + 1],
                                                      scalar2=None, op0=ALU.mult), reads=R(invf4, posf), writes=R(trC))
        range_reduce(trC, trB, 2048)
        for t in range(16):
            P.op("dve", lambda h, t=t: h.tensor_tensor(out=trC.ap[:, t * 128:(t + 1) * 128], in0=trC.ap[:, t * 128:(t + 1) * 128],
                                                      in1=phrow.ap, op=ALU.add), reads=R(trC, phrow), writes=R(trC))
        wrap_clamp(trC, trB, 2048)
        P.op("act", lambda h: h.activation(out=CS_tm.ap.rearrange("p a b -> p (a b)"), in_=trC.ap, func=AF.Sin),
             reads=R(trC), writes=R(CS_tm))

        trig_done = [P.last_real["dve"], P.last_real["act"]]
        wukv = view(W_BASE, BF16, [2, 4096], "wukv")
        wuq = view(W_BASE + 16384, BF16, [4, 3072], "wuq")
        Bmat = view(W_BASE + 40960, BF16, [4, 16, 128], "Bmat")
        ld("pool", wukv.ap, w_ukv_v, wukv, deps=trig_done)
        ld("pool", wuq.ap, w_uq_v, wuq, deps=trig_done)

        def transposes(src_ap, nblk, bank_views, reads, bank_bufs):
            for c in range(nblk):
                bv = bank_views[c]
                P.op("pe", lambda h, c=c, bv=bv: h.transpose(out=bv, in_=src_ap[:, c * 128:(c + 1) * 128], identity=ident.t[:]),
                     reads=reads + R(ident), writes=[bank_bufs[c].res])

        maybe_stop("p0")
        tile_dst = {}

        def stA(t):
            par = t % 2
            xb_, hb_ = xs[par], hb[par]
            ld("sp", xb_.ap, x_d[t * 128:(t + 1) * 128, :], xb_)
            P.op("act", lambda h: h.activation(out=junk.ap, in_=xb_.ap, func=AF.Square, accum_out=ss.t[:, t:t + 1]),
                 reads=R(xb_), writes=R(junk, ss))
            P.op("act", lambda h: h.activation(out=lnt.t[:, 0:1], in_=ss.t[:, t:t + 1], func=AF.Ln, bias=col(COL_EPS), scale=1.0 / D),
                 reads=R(ss, cols), writes=R(lnt))
            P.op("act", lambda h: h.activation(out=rs.t[:, t:t + 1], in_=lnt.t[:, 0:1], func=AF.Exp, scale=-0.5),
                 reads=R(lnt), writes=R(rs))
            P.op("dve", lambda h: h.scalar_tensor_tensor(out=hb_.ap, in0=xb_.ap, scalar=rs.t[:, t:t + 1], in1=normg.ap,
                                                         op0=ALU.mult, op1=ALU.mult),
                 reads=R(xb_, rs, normg), writes=R(hb_))

        def stB(t):
            own = t < 8
            par = t % 2
            hb_ = hb[par]
            bx0, bx1 = banks[0], banks[1]
            bb = [bx0, bx1]
            bviews = [bb[c // 8].bf[:, (c % 8) * 128:(c % 8 + 1) * 128] for c in range(16)]
            transposes(hb_.ap, 16, bviews, R(hb_), [bb[c // 8] for c in range(16)])
            if own:
                dst = hT_own
                d0 = hT_own.ap[:, 0:8, t * 128:(t + 1) * 128]
                d1 = hT_own.ap[:, 8:16, t * 128:(t + 1) * 128]
            else:
                dst = hTt[par]
                d0 = dst.ap[:, 0:8, :]
                d1 = dst.ap[:, 8:16, :]
            tile_dst[t] = dst
            P.op("act", lambda h: h.copy(out=d0, in_=bx0.bf.rearrange("p (a b) -> p a b", a=8)),
                 reads=R(bx0), writes=R(dst))
            P.op("dve", lambda h: h.tensor_copy(out=d1, in_=bx1.bf.rearrange("p (a b) -> p a b", a=8)),
                 reads=R(bx1), writes=R(dst))
            if t == 8:
                P.op("dve", lambda h: h.tensor_scalar(out=hT_halo.t[:, :, 1:2], in0=dst.ap[:, :, 0:1], scalar1=col(COL_MR),
                                                      scalar2=None, op0=ALU.mult), reads=R(dst, cols), writes=R(hT_halo))
            if t == 15:
                P.op("dve", lambda h: h.tensor_scalar(out=hT_halo.t[:, :, 0:1], in0=dst.ap[:, :, 127:128], scalar1=col(COL_ML),
                                                      scalar2=None, op0=ALU.mult), reads=R(dst, cols), writes=R(hT_halo))

        def zbanks(t):
            par = t % 2
            return banks[2 + 2 * par], banks[3 + 2 * par]

        def stC(t):
            own = t < 8
            dst = tile_dst[t]
            bq, bk = zbanks(t)

            def hsrc(kc):
                if own:
                    return dst.ap[:, kc, t * 128:(t + 1) * 128]
                return dst.ap[:, kc, :]

            if own:
                for kc in range(16):
                    P.op("pe", lambda h, kc=kc: h.matmul(bq.ap, lhsT=hsrc(kc), rhs=Wlat.ap[:, kc, 0:512],
                                                        start=(kc == 0), stop=(kc == 15)),
                         reads=R(dst, Wlat), writes=R(bq))
            for kc in range(16):
                P.op("pe", lambda h, kc=kc: h.matmul(bk.ap[:, 0:320], lhsT=hsrc(kc), rhs=Wlat.ap[:, kc, 512:832],
                                                    start=(kc == 0), stop=(kc == 15)),
                     reads=R(dst, Wlat), writes=R(bk))

        qa_n2 = [qa_n, Buf(junk.ap[:, 512:1024], "qa_n_b")]

        def stD(t):
            own = t < 8
            bq, bk = zbanks(t)
            if own:
                P.op("act", lambda h: h.activation(out=junk.ap[:, 0:512], in_=bq.ap, func=AF.Square, accum_out=ssq.t[:, t:t + 1]),
                     reads=R(bq), writes=R(junk, ssq))
            P.op("act", lambda h: h.activation(out=junk.ap[:, 0:256], in_=bk.ap[:, 0:256], func=AF.Square, accum_out=sskv.t[:, t:t + 1]),
                 reads=R(bk), writes=R(junk, sskv))
            P.op("act", lambda h: h.activation(out=junk.ap[:, 0:64], in_=bk.ap[:, 256:320], func=AF.Square, accum_out=sspe.t[:, t:t + 1]),
                 reads=R(bk), writes=R(junk, sspe))
            if own:
                P.op("act", lambda h: h.activation(out=lnt.t[:, 1:2], in_=ssq.t[:, t:t + 1], func=AF.Ln, bias=col(COL_EPS), scale=1.0 / 512),
                     reads=R(ssq, cols), writes=R(lnt))
                P.op("act", lambda h: h.activation(out=rq.t[:, t:t + 1], in_=lnt.t[:, 1:2], func=AF.Exp, scale=-0.5),
                     reads=R(lnt), writes=R(rq))
            P.op("act", lambda h: h.activation(out=lnt.t[:, 2:3], in_=sskv.t[:, t:t + 1], func=AF.Ln, bias=col(COL_EPS), scale=1.0 / 256),
                 reads=R(sskv, cols), writes=R(lnt))
            P.op("act", lambda h: h.activation(out=rkv.t[:, t:t + 1], in_=lnt.t[:, 2:3], func=AF.Exp, scale=-0.5),
                 reads=R(lnt), writes=R(rkv))
            if own:
                P.op("dve", lambda h: h.scalar_tensor_tensor(out=qa_n.ap, in0=bq.ap, scalar=rq.t[:, t:t + 1], in1=rows_a.ap[:, 0:512],
                                                             op0=ALU.mult, op1=ALU.mult), reads=R(bq, rq, rows_a), writes=R(qa_n))
            P.op("dve", lambda h: h.scalar_tensor_tensor(out=kva_n.ap, in0=bk.ap[:, 0:256], scalar=rkv.t[:, t:t + 1],
                                                         in1=rows_a.ap[:, 512:768], op0=ALU.mult, op1=ALU.mult),
                 reads=R(bk, rkv, rows_a), writes=R(kva_n))
            P.op("dve", lambda h: h.tensor_tensor(out=kpe_g.ap, in0=bk.ap[:, 256:320], in1=rows_a.ap[:, 768:832], op=ALU.mult),
                 reads=R(bk, rows_a), writes=R(kpe_g))
            P.op("dve", lambda h: h.tensor_tensor(out=tcb.ap, in0=kpe_g.ap, in1=CS_tm.ap[:, t, 0:64], op=ALU.mult),
                 reads=R(kpe_g, CS_tm), writes=R(tcb))
            P.op("dve", lambda h: h.tensor_tensor(out=tsb.ap, in0=kpe_g.ap, in1=CS_tm.ap[:, t, 64:128], op=ALU.mult),
                 reads=R(kpe_g, CS_tm), writes=R(tsb))
            for rep in range(2):
                o_ = rep * 64
                P.op("dve", lambda h, o_=o_: h.tensor_tensor(out=Kr_tm.ap[:, o_:o_ + 32], in0=tcb.ap[:, 0:32], in1=tsb.ap[:, 32:64], op=ALU.subtract),
                     reads=R(tcb, tsb), writes=R(Kr_tm))
                P.op("dve", lambda h, o_=o_: h.tensor_tensor(out=Kr_tm.ap[:, o_ + 32:o_ + 64], in0=tcb.ap[:, 32:64], in1=tsb.ap[:, 0:32], op=ALU.add),
                     reads=R(tcb, tsb), writes=R(Kr_tm))

        def stE(t):
            own = t < 8
            par = t % 2
            b4 = banks[6 + par]
            if own:
                transposes(qa_n.ap, 4, [b4.bf[:, c * 128:(c + 1) * 128] for c in range(4)], R(qa_n), [b4] * 4)
            transposes(kva_n.ap, 2, [b4.bf[:, (4 + c) * 128:(5 + c) * 128] for c in range(2)], R(kva_n), [b4] * 2)
            transposes(Kr_tm.ap, 1, [b4.bf[:, 768:896]], R(Kr_tm), [b4])
            if own:
                P.op("act", lambda h: h.copy(out=qaT.ap[:, :, t * 128:(t + 1) * 128], in_=b4.bf[:, 0:512].rearrange("p (a b) -> p a b", a=4)),
                     reads=R(b4), writes=R(qaT))
            P.op("dve", lambda h: h.tensor_copy(out=kvaT.ap[:, :, t * 128:(t + 1) * 128], in_=b4.bf[:, 512:768].rearrange("p (a b) -> p a b", a=2)),
                 reads=R(b4), writes=R(kvaT))
            P.op("dve", lambda h: h.tensor_copy(out=KrT.ap[:, t * 128:(t + 1) * 128], in_=b4.bf[:, 768:896]),
                 reads=R(b4), writes=R(KrT))

        stages_1a = [stA, stB, stC, stD, stE]
        for s_ in range(16 + len(stages_1a) - 1):
            for k_ in reversed(range(len(stages_1a))):
                t_ = s_ - k_
                if 0 <= t_ < 16:
                    stages_1a[k_](t_)

        if debug:
            dbg_outs["hT_own"] = (hT_own, [128, 16 * 1024], BF16)
            dbg_outs["qaT"] = (qaT, [128, 4 * 1024], BF16)
            dbg_outs["kvaT"] = (kvaT, [128, 2 * 2048], BF16)
            dbg_outs["KrT"] = (KrT, [128, 2048], BF16)
            dbg_outs["CS_fm"] = (CS_fm, [128, 1024], F32)

        def dump_now(names):
            P.barrier()
            for nm in names:
                buf, shape, dt = dbg_outs.pop(nm)
                dd = nc.dram_tensor("dbg_" + nm, shape, dt, kind="ExternalOutput").ap()
                src = buf.ap
                if len(src.shape) == 3:
                    src = src.rearrange("p a b -> p (a b)")
                P.dma("sp", lambda h, dd=dd, src=src: h.dma_start(out=dd, in_=src), dq["sp"], reads=R(buf))
                dbg_names.append("dbg_" + nm)

        dbg_names = []

        if debug:
            dump_now(["hT_own", "qaT", "kvaT", "KrT", "CS_fm"])
        maybe_stop("p1a")

        P.barrier()
        c1 = Carver(L0, W_BASE)
        KnT = [c1.take(BF16, [2048], f"KnT{i}") for i in range(2)]
        Vsb = [c1.take(BF16, [16, 128], f"V{i}") for i in range(2)]
        QnT = [c1.take(BF16, [1024], f"QnT{i}") for i in range(2)]
        QrT = [c1.take(BF16, [1024], f"QrT{i}") for i in range(2)]
        PT = [c1.take(BF16, [512], f"PT{i}") for i in range(6)]
        sqA = [c1.take(BF16, [512], f"sqA{i}") for i in range(2)]
        sqB = [c1.take(BF16, [512], f"sqB{i}") for i in range(2)]
        sqK = [c1.take(BF16, [512], f"sqK{i}") for i in range(2)]
        fq = [c1.take(F32, [512], f"fq{i}") for i in range(2)]
        accP = [c1.take(F32, [512], f"accP{i}") for i in range(2)]
        tmpB = [c1.take(F32, [512], f"tmpB{i}") for i in range(2)]
        rl = [c1.take(F32, [512], f"rl{i}") for i in range(2)]
        acc = [c1.take(F32, [512], f"acc{i}") for i in range(2)]
        gqk = sb("gqk", [128, 1])
        P.op("dve", lambda h: h.tensor_tensor(out=gqk.t[:], in0=col(COL_GQN), in1=col(COL_GKN), op=ALU.mult), reads=R(cols), writes=R(gqk))
        onesf = identf
        P.op("dve", lambda h: h.memset(onesf.t[:], 1.0), reads=R(ident), writes=R(onesf))

        wuq4 = wuq.ap.rearrange("p c (h d) -> p c h d", h=16)
        P.op("dve", lambda h: h.tensor_copy(out=Bmat.ap[:, :, :, 0:64], in_=wuq4[:, :, :, 128:192]), reads=R(wuq), writes=R(Bmat))
        P.op("dve", lambda h: h.tensor_scalar(out=Bmat.ap[:, :, :, 64:96], in0=wuq4[:, :, :, 160:192], scalar1=-1.0, scalar2=None, op0=ALU.mult),
             reads=R(wuq), writes=R(Bmat))
        P.op("dve", lambda h: h.tensor_copy(out=Bmat.ap[:, :, :, 96:128], in_=wuq4[:, :, :, 128:160]), reads=R(wuq), writes=R(Bmat))

        bS, bO, bL, bB, bSS, bKV = [banks[0], banks[1]], [banks[2], banks[3]], banks[4], banks[5], banks[6], banks[7]
        bA = bKV

        def upproj_stages(hh):
            hp = hh % 2
            Kn, Vv, Qn, Qr = KnT[hp], Vsb[hp], QnT[hp], QrT[hp]
            skt_h, sk_h = skt[hp], sk[hp]
            st = []

            def K_a(tb):
                for kc in range(2):
                    P.op("pe", lambda h, kc=kc: h.matmul(bKV.ap, lhsT=wukv.ap[:, kc, hh * 256:hh * 256 + 128],
                                                        rhs=kvaT.ap[:, kc, tb * 512:(tb + 1) * 512], start=(kc == 0), stop=(kc == 1)),
                         reads=R(wukv, kvaT), writes=R(bKV))

            def K_b(tb):
                sq_ = sqK[tb % 2]
                ks = slice(tb * 512, (tb + 1) * 512)
                P.op("dve", lambda h: h.tensor_copy(out=Kn.ap[:, ks], in_=bKV.ap), reads=R(bKV), writes=R(Kn))
                P.op("pool", lambda h: h.tensor_tensor(out=sq_.ap, in0=Kn.ap[:, ks], in1=Kn.ap[:, ks], op=ALU.mult), reads=R(Kn), writes=R(sq_))

            def K_c(tb):
                sq_ = sqK[tb % 2]
                for j in range(4):
                    kt = tb * 4 + j
                    P.op("pe", lambda h, j=j, kt=kt: h.matmul(bSS.ap[:, kt:kt + 1], lhsT=sq_.ap[:, j * 128:(j + 1) * 128],
                                                             rhs=ones.t[:, 0:1], start=True, stop=True),
                         reads=R(sq_, ones), writes=R(bSS))

            def K_d():
                P.op("dve", lambda h: h.tensor_tensor(out=skt_h.t[:], in0=bSS.ap[:, 0:16], in1=sspe.t[:], op=ALU.add),
                     reads=R(bSS, sspe), writes=R(skt_h))

            def K_e():
                P.op("act", lambda h: h.activation(out=skt_h.t[:], in_=skt_h.t[:], func=AF.Ln, bias=col(COL_EPS), scale=1.0 / 192),
                     reads=R(skt_h, cols), writes=R(skt_h))
                P.op("act", lambda h: h.activation(out=sk_h.t[:], in_=skt_h.t[:], func=AF.Exp, bias=col(COL_LNS), scale=-0.5),
                     reads=R(skt_h, cols), writes=R(sk_h))

            def V_a(rnd):
                for j in range(4):
                    kt = rnd * 4 + j
                    for kc in range(2):
                        P.op("pe", lambda h, kc=kc, kt=kt, j=j: h.matmul(bKV.ap[:, j * 128:(j + 1) * 128], lhsT=kvaT.ap[:, kc, kt * 128:(kt + 1) * 128],
                                                                       rhs=wukv.ap[:, kc, hh * 256 + 128:hh * 256 + 256],
                                                                       start=(kc == 0), stop=(kc == 1), skip_group_check=True),
                             reads=R(kvaT, wukv), writes=R(bKV))

            def V_b(rnd):
                eng = "dve"
                if eng == "act":
                    P.op("act", lambda h: h.copy(out=Vv.ap[:, rnd * 4:(rnd + 1) * 4, :], in_=bKV.ap.rearrange("p (a b) -> p a b", a=4)),
                         reads=R(bKV), writes=R(Vv))
                else:
                    P.op("dve", lambda h: h.tensor_copy(out=Vv.ap[:, rnd * 4:(rnd + 1) * 4, :], in_=bKV.ap.rearrange("p (a b) -> p a b", a=4)),
                         reads=R(bKV), writes=R(Vv))

            def Q_a(qb):
                qs = slice(qb * 512, (qb + 1) * 512)
                for kc in range(4):
                    P.op("pe", lambda h, kc=kc: h.matmul(bA.ap, lhsT=wuq.ap[:, kc, hh * 192:hh * 192 + 128], rhs=qaT.ap[:, kc, qs],
                                                        start=(kc == 0), stop=(kc == 3)), reads=R(wuq, qaT), writes=R(bA))
                for kc in range(4):
                    P.op("pe", lambda h, kc=kc: h.matmul(bB.ap, lhsT=Bmat.ap[:, kc, hh, :], rhs=qaT.ap[:, kc, qs],
                                                        start=(kc == 0), stop=(kc == 3)), reads=R(Bmat, qaT), writes=R(bB))

            def Q_b(qb):
                sa, sb_ = sqA[qb], sqB[qb]
                P.op("act", lambda h: h.activation(out=sa.ap, in_=bA.ap, func=AF.Square), reads=R(bA), writes=R(sa))
                P.op("act", lambda h: h.activation(out=sb_.ap[0:64, :], in_=bB.ap[0:64, :], func=AF.Square), reads=R(bB), writes=R(sb_))

            def Q_c(qb):
                sa, sb_ = sqA[qb], sqB[qb]
                P.op("pe", lambda h: h.matmul(bSS.ap, lhsT=ones.t[:], rhs=sa.ap, start=True, stop=False), reads=R(ones, sa), writes=R(bSS))
                P.op("pe", lambda h: h.matmul(bSS.ap, lhsT=ones.t[0:64, :], rhs=sb_.ap[0:64, :], start=False, stop=True),
                     reads=R(ones, sb_), writes=R(bSS))

            def Q_d(qb):
                f_ = fq[qb]
                P.op("act", lambda h: h.activation(out=f_.ap, in_=bSS.ap, func=AF.Ln, bias=col(COL_EPS), scale=1.0 / 192),
                     reads=R(bSS, cols), writes=R(f_))
                P.op("act", lambda h: h.activation(out=f_.ap, in_=f_.ap, func=AF.Exp, scale=-0.5), reads=R(f_), writes=R(f_))

            def Q_e(qb):
                qs = slice(qb * 512, (qb + 1) * 512)
                f_, tb_ = fq[qb], tmpB[qb]
                P.op("dve", lambda h: h.scalar_tensor_tensor(out=Qn.ap[:, qs], in0=bA.ap, scalar=gqk.t[:, 0:1], in1=f_.ap,
                                                             op0=ALU.mult, op1=ALU.mult), reads=R(bA, gqk, f_), writes=R(Qn))
                P.op("dve", lambda h: h.scalar_tensor_tensor(out=tb_.ap, in0=bB.ap, scalar=col(COL_GQAUG), in1=f_.ap,
                                                             op0=ALU.mult, op1=ALU.mult), reads=R(bB, cols, f_), writes=R(tb_))
                P.op("dve", lambda h: h.tensor_tensor(out=Qr.ap[:, qs], in0=tb_.ap, in1=CS_fm.ap[:, qs], op=ALU.mult),
                     reads=R(tb_, CS_fm), writes=R(Qr))

            for tb in range(4):
                st += [lambda tb=tb: K_a(tb), lambda tb=tb: K_b(tb), lambda tb=tb: K_c(tb)]
            st += [K_d, K_e]
            for rnd in range(4):
                st += [lambda rnd=rnd: V_a(rnd), lambda rnd=rnd: V_b(rnd)]
            for qb in range(2):
                st += [lambda qb=qb: Q_a(qb), lambda qb=qb: Q_b(qb), lambda qb=qb: Q_c(qb), lambda qb=qb: Q_d(qb), lambda qb=qb: Q_e(qb)]
            return st

        pt_i = 0
        for st_ in upproj_stages(0):
            st_()
        for hh in range(NH):
            hp = hh % 2
            Kn, Vv, Qn, Qr = KnT[hp], Vsb[hp], QnT[hp], QrT[hp]
            sk_h = sk[hp]
            nxt = upproj_stages(hh + 1) if hh + 1 < NH else []
            step = 0
            for qb in range(2):
                qs = slice(qb * 512, (qb + 1) * 512)
                bo, ac, acp = bO[qb], acc[qb], accP[qb]

                def emit_S(kt, Kn=Kn, Qn=Qn, Qr=Qr, qs=qs):
                    bs = bS[kt % 2]
                    P.op("pe", lambda h: h.matmul(bs.ap, lhsT=Kn.ap[:, kt * 128:(kt + 1) * 128], rhs=Qn.ap[:, qs], start=True, stop=False),
                         reads=R(Kn, Qn), writes=R(bs))
                    P.op("pe", lambda h: h.matmul(bs.ap, lhsT=KrT.ap[:, kt * 128:(kt + 1) * 128], rhs=Qr.ap[:, qs], start=False, stop=True),
                         reads=R(KrT, Qr), writes=R(bs))

                emit_S(0)
                for kt in range(16):
                    if kt + 1 < 16:
                        emit_S(kt + 1)
                    pt = PT[pt_i % 6]
                    pt_i += 1
                    bs = bS[kt % 2]
                    P.op("act", lambda h, pt=pt, bs=bs, kt=kt, sk_h=sk_h: h.activation(out=pt.ap, in_=bs.ap, func=AF.Exp, scale=sk_h.t[:, kt:kt + 1]),
                         reads=R(bs, sk_h), writes=R(pt))
                    P.op("pe", lambda h, pt=pt, kt=kt, Vv=Vv, bo=bo: h.matmul(bo.ap, lhsT=Vv.ap[:, kt, :], rhs=pt.ap, start=(kt == 0), stop=(kt == 15)),
                         reads=R(Vv, pt), writes=R(bo))
                    le, la = ("pool", acp) if kt % 4 == 3 else ("dve", ac)
                    if kt == 0 or kt == 3:
                        P.op(le, lambda h, pt=pt, la=la: h.tensor_copy(out=la.ap, in_=pt.ap), reads=R(pt), writes=R(la))
                    else:
                        P.op(le, lambda h, pt=pt, la=la: h.tensor_tensor(out=la.ap, in0=la.ap, in1=pt.ap, op=ALU.add), reads=R(la, pt), writes=R(la))
                    if step < len(nxt):
                        nxt[step]()
                    step += 1
                P.op("dve", lambda h, ac=ac, acp=acp: h.tensor_tensor(out=ac.ap, in0=ac.ap, in1=acp.ap, op=ALU.add), reads=R(ac, acp), writes=R(ac))
                P.op("pe", lambda h, ac=ac: h.matmul(bL.ap, lhsT=onesf.t[:], rhs=ac.ap, start=True, stop=True), reads=R(onesf, ac), writes=R(bL))
                r_ = rl[qb]
                P.op("act", lambda h, r_=r_: h.activation(out=r_.ap, in_=bL.ap, func=AF.Ln), reads=R(bL), writes=R(r_))
                P.op("act", lambda h, r_=r_: h.activation(out=r_.ap, in_=r_.ap, func=AF.Exp, scale=-1.0), reads=R(r_), writes=R(r_))
                P.op("dve", lambda h, r_=r_, hh=hh, qs=qs, bo=bo: h.tensor_tensor(out=AT.ap[:, hh, qs], in0=bo.ap, in1=r_.ap, op=ALU.mult),
                     reads=R(bo, r_), writes=R(AT))
            assert step >= len(nxt)

        if debug:
            dbg_outs["AT"] = (AT, [128, 16 * 1024], BF16)
            dump_now(["AT"])
        maybe_stop("p1b")

        P.barrier()
        slab = [view(L0 + i * 28672, BF16, [14336], f"slab{i}") for i in range(2)]
        cvg = view(L0 + 57344, BF16, [8, 1024], "cvg")
        merged = view(L0 + 73728, BF16, [16, 1024], "merged")
        Fs = [view(65536 + i * 2048, F32, [512], f"F{i}") for i in range(8)]
        ubuf = view(65536 + 16384, F32, [1026], "ubuf")
        ybuf = Buf(arena[:, (65536 + 6 * 2048) // 4:(65536 + 8 * 2048) // 4], "ybuf")
        ybuf.rl = [Fs[6].res, Fs[7].res]
        slab_i = 0

        def next_slab():
            nonlocal slab_i
            s_ = slab[slab_i % 2]
            slab_i += 1
            return s_

        for sgrp in range(4):
            sl = next_slab()
            slv = sl.ap[:, 0:8192].rearrange("p (c n) -> p c n", c=16)
            ld("pool", slv, w_in_v[:, :, C_GA + sgrp * 512:C_GA + (sgrp + 1) * 512], sl)
            for mm_ in range(4):
                m = sgrp * 4 + mm_
                for tb in range(2):
                    bk = banks[(m * 2 + tb) % 4]
                    ts_ = slice(tb * 512, (tb + 1) * 512)
                    for kc in range(16):
                        P.op("pe", lambda h, kc=kc, bk=bk, slv=slv, mm_=mm_, ts_=ts_: h.matmul(bk.ap, lhsT=slv[:, kc, mm_ * 128:(mm_ + 1) * 128],
                                                                                           rhs=hT_own.ap[:, kc, ts_], start=(kc == 0), stop=(kc == 15)),
                             reads=R(sl, hT_own), writes=R(bk))
                    sg = Fs[tb]
                    sgv = sg.ap.bitcast(BF16)[:, 0:512]
                    P.op("act", lambda h, bk=bk, sgv=sgv: h.activation(out=sgv, in_=bk.ap, func=AF.Silu), reads=R(bk), writes=R(sg))
                    P.op("dve", lambda h, m=m, ts_=ts_, sgv=sgv: h.tensor_tensor(out=AT.ap[:, m, ts_], in0=AT.ap[:, m, ts_], in1=sgv, op=ALU.mult),
                         reads=R(AT, sg), writes=R(AT))

        for c in range(8):
            sl = next_slab()
            slv = sl.ap[:, 0:8192].rearrange("p (c g n) -> p c g n", c=16, g=4)
            for g_ in range(4):
                c_lo = C_CB + g_ * 1024 + c * 128
                ld("pool", slv[:, :, g_, :], w_in_v[:, :, c_lo:c_lo + 128], sl)

            def proj(bk, g, rhs_fn, ncols, slv=slv, sl=sl):
                for kc in range(16):
                    P.op("pe", lambda h, kc=kc: h.matmul(bk.ap[:, 0:ncols] if ncols < 512 else bk.ap, lhsT=slv[:, kc, g, :], rhs=rhs_fn(kc),
                                                       start=(kc == 0), stop=(kc == 15)), reads=R(sl, hT_own, hT_halo), writes=R(bk))

            for tb in range(2):
                ts_ = slice(tb * 512, (tb + 1) * 512)
                bc_, bx_ = banks[tb], banks[2 + tb]
                proj(bc_, 1, lambda kc, ts_=ts_: hT_own.ap[:, kc, ts_], 512)
                proj(bx_, 2, lambda kc, ts_=ts_: hT_own.ap[:, kc, ts_], 512)
                cc = Fs[tb]
                P.op("act", lambda h, cc=cc, bc_=bc_: h.copy(out=cc.ap, in_=bc_.ap), reads=R(bc_), writes=R(cc))
                P.op("dve", lambda h, cc=cc, bx_=bx_, tb=tb: h.tensor_tensor(out=ubuf.ap[:, 1 + tb * 512:1 + (tb + 1) * 512], in0=bx_.ap, in1=cc.ap, op=ALU.mult),
                     reads=R(bx_, cc), writes=R(ubuf))
            bh = banks[4]
            for kc in range(16):
                P.op("pe", lambda h, kc=kc, slv=slv: h.matmul(bh.ap[:, 0:2], lhsT=slv[:, kc, 1, :], rhs=hT_halo.t[:, kc, :], start=(kc == 0), stop=(kc == 15)),
                     reads=R(sl, hT_halo), writes=R(bh))
            for kc in range(16):
                P.op("pe", lambda h, kc=kc, slv=slv: h.matmul(bh.ap[:, 2:4], lhsT=slv[:, kc, 2, :], rhs=hT_halo.t[:, kc, :], start=(kc == 0), stop=(kc == 15),
                                                            skip_group_check=True), reads=R(sl, hT_halo), writes=R(bh))
            P.op("act", lambda h: h.copy(out=hcx.t[:, 0:2], in_=bh.ap[:, 0:2]), reads=R(bh), writes=R(hcx))
            P.op("dve", lambda h: h.tensor_tensor(out=ubuf.ap[:, 0:1], in0=bh.ap[:, 2:3], in1=hcx.t[:, 0:1], op=ALU.mult),
                 reads=R(bh, hcx), writes=R(ubuf))
            P.op("dve", lambda h: h.tensor_tensor(out=ubuf.ap[:, 1025:1026], in0=bh.ap[:, 3:4], in1=hcx.t[:, 1:2], op=ALU.mult),
                 reads=R(bh, hcx), writes=R(ubuf))
            P.op("dve", lambda h, c=c: h.tensor_scalar(out=ybuf.ap, in0=ubuf.ap[:, 0:1024], scalar1=col(COL_CW + 3 * c), scalar2=col(COL_CBIAS + c),
                                                      op0=ALU.mult, op1=ALU.add), reads=R(ubuf, cols), writes=R(ybuf))
            P.op("dve", lambda h, c=c: h.scalar_tensor_tensor(out=ybuf.ap, in0=ubuf.ap[:, 1:1025], scalar=col(COL_CW + 3 * c + 1), in1=ybuf.ap,
                                                             op0=ALU.mult, op1=ALU.add), reads=R(ubuf, cols, ybuf), writes=R(ybuf))
            P.op("dve", lambda h, c=c: h.scalar_tensor_tensor(out=ybuf.ap, in0=ubuf.ap[:, 2:1026], scalar=col(COL_CW + 3 * c + 2), in1=ybuf.ap,
                                                             op0=ALU.mult, op1=ALU.add), reads=R(ubuf, cols, ybuf), writes=R(ybuf))
            for tb in range(2):
                ts_ = slice(tb * 512, (tb + 1) * 512)
                bb_, bg_ = banks[5 + tb], banks[(7 + tb) if tb == 0 else 4]
                proj(bb_, 0, lambda kc, ts_=ts_: hT_own.ap[:, kc, ts_], 512)
                proj(bg_, 3, lambda kc, ts_=ts_: hT_own.ap[:, kc, ts_], 512)
                sgc, t1 = Fs[2 + tb], Fs[4 + tb]
                P.op("act", lambda h, sgc=sgc, bg_=bg_: h.activation(out=sgc.ap, in_=bg_.ap, func=AF.Silu), reads=R(bg_), writes=R(sgc))
                P.op("dve", lambda h, t1=t1, bb_=bb_, ts_=ts_: h.tensor_tensor(out=t1.ap, in0=bb_.ap, in1=ybuf.ap[:, ts_], op=ALU.mult),
                     reads=R(bb_, ybuf), writes=R(t1))
                P.op("dve", lambda h, t1=t1, sgc=sgc, c=c, ts_=ts_: h.tensor_tensor(out=cvg.ap[:, c, ts_], in0=t1.ap, in1=sgc.ap, op=ALU.mult),
                     reads=R(t1, sgc), writes=R(cvg))

        for g in range(8):
            sl = next_slab()
            s_ma = sl.ap[:, 0:4096].rearrange("p (c n) -> p c n", c=16)
            s_mc = sl.ap[:, 4096:8192].rearrange("p (c n) -> p c n", c=16)
            s_ba = sl.ap[:, 8192:12288].rearrange("p (c n) -> p c n", c=16)
            s_bc = sl.ap[:, 12288:14336].rearrange("p (c n) -> p c n", c=8)
            cs_ = slice(g * 256, (g + 1) * 256)
            sl.slot = getattr(sl, "slot", None) or P.slot("ld_" + sl.res.name)
            ld("pool", s_ma, w_in_v[:, :, C_MA + g * 256:C_MA + (g + 1) * 256], sl)
            ld("pool", s_mc, w_in_v[:, :, C_MC + g * 256:C_MC + (g + 1) * 256], sl)
            ld("pool", s_ba, w_ba_v[:, :, cs_], sl)
            ld("pool", s_bc, w_bc_v[:, :, cs_], sl)
            for jj in range(2):
                j = g * 2 + jj
                js = slice(jj * 128, (jj + 1) * 128)
                for tb in range(2):
                    ts_ = slice(tb * 512, (tb + 1) * 512)
                    par = (j * 2 + tb) % 2
                    b_ma, b_mc, b_ya, b_yc = banks[par * 4], banks[par * 4 + 1], banks[par * 4 + 2], banks[par * 4 + 3]
                    for kc in range(16):
                        P.op("pe", lambda h, kc=kc, b_ma=b_ma, s_ma=s_ma, js=js, ts_=ts_: h.matmul(b_ma.ap, lhsT=s_ma[:, kc, js], rhs=hT_own.ap[:, kc, ts_],
                                                                                                start=(kc == 0), stop=(kc == 15)), reads=R(sl, hT_own), writes=R(b_ma))
                    for kc in range(16):
                        P.op("pe", lambda h, kc=kc, b_mc=b_mc, s_mc=s_mc, js=js, ts_=ts_: h.matmul(b_mc.ap, lhsT=s_mc[:, kc, js], rhs=hT_own.ap[:, kc, ts_],
                                                                                                start=(kc == 0), stop=(kc == 15)), reads=R(sl, hT_own), writes=R(b_mc))
                    for kc in range(16):
                        P.op("pe", lambda h, kc=kc, b_ya=b_ya, s_ba=s_ba, js=js, ts_=ts_: h.matmul(b_ya.ap, lhsT=s_ba[:, kc, js], rhs=AT.ap[:, kc, ts_],
                                                                                                start=(kc == 0), stop=(kc == 15)), reads=R(sl, AT), writes=R(b_ya))
                    for kc in range(8):
                        P.op("pe", lambda h, kc=kc, b_yc=b_yc, s_bc=s_bc, js=js, ts_=ts_: h.matmul(b_yc.ap, lhsT=s_bc[:, kc, js], rhs=cvg.ap[:, kc, ts_],
                                                                                                start=(kc == 0), stop=(kc == 7)), reads=R(sl, cvg), writes=R(b_yc))
                    ta, tcg, m1, m2 = Fs[par], Fs[2 + par], Fs[4 + par], Fs[6 + par]
                    P.op("act", lambda h, ta=ta, b_ma=b_ma: h.activation(out=ta.ap, in_=b_ma.ap, func=AF.Tanh, scale=0.5), reads=R(b_ma), writes=R(ta))
                    P.op("act", lambda h, tcg=tcg, b_mc=b_mc: h.activation(out=tcg.ap, in_=b_mc.ap, func=AF.Tanh, scale=0.5), reads=R(b_mc), writes=R(tcg))
                    P.op("dve", lambda h, ta=ta, m1=m1, b_ya=b_ya: h.scalar_tensor_tensor(out=m1.ap, in0=ta.ap, scalar=1.0, in1=b_ya.ap, op0=ALU.add, op1=ALU.mult),
                         reads=R(ta, b_ya), writes=R(m1))
                    P.op("dve", lambda h, tcg=tcg, m2=m2, b_yc=b_yc: h.scalar_tensor_tensor(out=m2.ap, in0=tcg.ap, scalar=1.0, in1=b_yc.ap, op0=ALU.add, op1=ALU.mult),
                         reads=R(tcg, b_yc), writes=R(m2))
                    P.op("dve", lambda h, m1=m1, m2=m2, j=j, ts_=ts_: h.tensor_tensor(out=merged.ap[:, j, ts_], in0=m1.ap, in1=m2.ap, op=ALU.add),
                         reads=R(m1, m2), writes=R(merged))

        if debug:
            dbg_outs["cvg"] = (cvg, [128, 8 * 1024], BF16)
            dbg_outs["merged"] = (merged, [128, 16 * 1024], BF16)
            dbg_outs["AT2"] = (AT, [128, 16 * 1024], BF16)
            dump_now(["cvg", "merged", "AT2"])
        maybe_stop("p2c")

        x2 = view(0, F32, [8, 2048], "x2")
        x2t = []
        for t in range(8):
            x2t.append(Buf(x2.ap[:, t, :], f"x2_{t}"))

        def res_deps(res):
            return ([res.w] if res.w is not None else []) + list(res.r.values()) + list(res.rd)
        S_ALL = [f_.res for f_ in Fs] + [ubuf.res]
        h2T = view(L0 + 73728, BF16, [16, 1024], "h2T")
        h2T.rl = [h2T.res, merged.res]
        pleg = view(L0 + 57344, F32, [2048], "pleg")
        wpp = view(196608, BF16, [2, 2048], "wpp")
        c3 = Carver(65536, 90112)

        def take_s(dt, shape, name):
            b_ = c3.take(dt, shape, name)
            b_.rl = [b_.res] + S_ALL
            return b_

        pT = take_s(BF16, [2, 1024], "pT")
        ptm = [take_s(F32, [256], f"ptm{i}") for i in range(2)]
        pbb = [take_s(BF16, [256], f"pbb{i}") for i in range(2)]
        h2b = [take_s(BF16, [2048], f"h2b{i}") for i in range(2)]
        tg = [take_s(F32, [512], f"tg{i}") for i in range(2)]
        e1 = [take_s(F32, [512], f"e1{i}") for i in range(2)]

        ld("pool", wpp.ap, w_pp_v, wpp)
        old_deps = res_deps(hT_own.res) + res_deps(AT.res)
        for t in range(8):
            ld("sp", x2t[t].ap, x_d[t * 128:(t + 1) * 128, :], x2t[t], deps=old_deps)
        ld("sp", pleg.ap, pleg_d, pleg, deps=res_deps(cvg.res))

        def norm2_stats(t):
            hb2 = h2b[t % 2]
            P.op("act", lambda h: h.activation(out=hb2.ap, in_=x2t[t].ap, func=AF.Square, accum_out=ss2.t[:, t:t + 1]),
                 reads=R(x2t[t]), writes=R(hb2, ss2))
            P.op("act", lambda h: h.activation(out=lnt.t[:, t:t + 1], in_=ss2.t[:, t:t + 1], func=AF.Ln, bias=col(COL_EPS), scale=1.0 / D),
                 reads=R(ss2, cols), writes=R(lnt))
            P.op("act", lambda h: h.activation(out=rs2.t[:, t:t + 1], in_=lnt.t[:, t:t + 1], func=AF.Exp, scale=-0.5), reads=R(lnt), writes=R(rs2))
            P.op("dve", lambda h: h.scalar_tensor_tensor(out=hb2.ap, in0=x2t[t].ap, scalar=rs2.t[:, t:t + 1], in1=pleg.ap, op0=ALU.mult, op1=ALU.mult),
                 reads=R(x2t[t], rs2, pleg), writes=R(hb2))
            pm, pb_ = ptm[t % 2], pbb[t % 2]
            ld("sp", pm.ap, p_d[t * 128:(t + 1) * 128, :], pm)
            P.op("dve", lambda h: h.tensor_copy(out=pb_.ap, in_=pm.ap), reads=R(pm), writes=R(pb_))

        def norm2_transposes(t):
            hb2, pb_ = h2b[t % 2], pbb[t % 2]
            bviews = [banks[4 + c // 8].bf[:, (c % 8) * 128:(c % 8 + 1) * 128] for c in range(16)]
            transposes(hb2.ap, 16, bviews, R(hb2), [banks[4 + c // 8] for c in range(16)])
            P.op("act", lambda h: h.copy(out=h2T.ap[:, 0:8, t * 128:(t + 1) * 128], in_=banks[4].bf.rearrange("p (a b) -> p a b", a=8)),
                 reads=R(banks[4]), writes=R(h2T))
            P.op("dve", lambda h: h.tensor_copy(out=h2T.ap[:, 8:16, t * 128:(t + 1) * 128], in_=banks[5].bf.rearrange("p (a b) -> p a b", a=8)),
                 reads=R(banks[5]), writes=R(h2T))
            transposes(pb_.ap, 2, [banks[6].bf[:, c * 128:(c + 1) * 128] for c in range(2)], R(pb_), [banks[6]] * 2)
            P.op("act", lambda h: h.copy(out=pT.ap[:, :, t * 128:(t + 1) * 128], in_=banks[6].bf[:, 0:256].rearrange("p (a b) -> p a b", a=2)),
                 reads=R(banks[6]), writes=R(pT))

        for n in range(4):
            sl = next_slab()
            w_ap = sl.ap[:, 0:8192].rearrange("p (c n) -> p c n", c=16)
            ns = slice(n * 512, (n + 1) * 512)
            ld("pool", w_ap, w_out_v[:, :, ns], sl)
            for t in range(8):
                bk = banks[(n * 8 + t) % 4]
                for kc in range(16):
                    P.op("pe", lambda h, kc=kc, bk=bk, w_ap=w_ap, t=t: h.matmul(bk.ap, lhsT=merged.ap[:, kc, t * 128:(t + 1) * 128], rhs=w_ap[:, kc, :],
                                                                            start=(kc == 0), stop=(kc == 15)), reads=R(merged, sl), writes=R(bk))
                P.op("dve", lambda h, bk=bk, t=t, ns=ns: h.scalar_tensor_tensor(out=x2t[t].ap[:, ns], in0=bk.ap, scalar=0.5, in1=x2t[t].ap[:, ns],
                                                                             op0=ALU.mult, op1=ALU.add), reads=R(bk, x2t[t]), writes=R(x2t[t]))
                if n == 3:
                    if t >= 2:
                        norm2_transposes(t - 2)
                    norm2_stats(t)
        norm2_transposes(6)
        norm2_transposes(7)
        so = P.slot("out")
        for n in range(4):
            sl = next_slab()
            w_ap = sl.ap[:, 0:8192].rearrange("p (c n) -> p c n", c=16)
            ns = slice(n * 512, (n + 1) * 512)
            ld("pool", w_ap, w_pg_v[:, :, ns], sl)
            for t in range(8):
                par = (n * 8 + t) % 2
                bg_, bp_ = banks[par * 2], banks[par * 2 + 1]
                for kc in range(16):
                    P.op("pe", lambda h, kc=kc, bg_=bg_, w_ap=w_ap, t=t: h.matmul(bg_.ap, lhsT=h2T.ap[:, kc, t * 128:(t + 1) * 128], rhs=w_ap[:, kc, :],
                                                                              start=(kc == 0), stop=(kc == 15)), reads=R(h2T, sl), writes=R(bg_))
                for kc in range(2):
                    P.op("pe", lambda h, kc=kc, bp_=bp_, t=t, ns=ns: h.matmul(bp_.ap, lhsT=pT.ap[:, kc, t * 128:(t + 1) * 128], rhs=wpp.ap[:, kc, ns],
                                                                          start=(kc == 0), stop=(kc == 1)), reads=R(pT, wpp), writes=R(bp_))
                tg_, e1_ = tg[par], e1[par]
                P.op("act", lambda h, tg_=tg_, bg_=bg_: h.activation(out=tg_.ap, in_=bg_.ap, func=AF.Tanh, scale=0.5), reads=R(bg_), writes=R(tg_))
                P.op("dve", lambda h, tg_=tg_, e1_=e1_, bp_=bp_: h.scalar_tensor_tensor(out=e1_.ap, in0=tg_.ap, scalar=1.0, in1=bp_.ap, op0=ALU.add, op1=ALU.mult),
                     reads=R(tg_, bp_), writes=R(e1_))
                P.op("dve", lambda h, e1_=e1_, t=t, ns=ns: h.scalar_tensor_tensor(out=x2t[t].ap[:, ns], in0=e1_.ap, scalar=0.5, in1=x2t[t].ap[:, ns],
                                                                               op0=ALU.mult, op1=ALU.add), reads=R(e1_, x2t[t]), writes=R(x2t[t]))
                if n == 3:
                    P.dma("sp", lambda h, t=t: h.dma_start(out=out_d[t * 128:(t + 1) * 128, :], in_=x2t[t].ap), so, reads=R(x2t[t]))
        fin_deps = [o for o in P.ops["sp"] if o.slot is so][-1:]
        if dbg_names:
            fin_deps += [o for o in P.ops["sp"] if o.slot is dq["sp"]][-1:]
        P.op("sp", lambda h: h.nop(), deps=fin_deps, real=False)

        P.finalize()
        print("ops", {e: len(P.ops[e]) for e in P.ENGS}, "signals", {e: sum(1 for o in P.ops[e] if o.signal) for e in P.ENGS}, "slots", len(P.slots))
        for s_ in P.slots:
            s_.sem = es.enter_context(nc.semaphore("d_" + s_.name))
        with nc.Block() as block:
            @block.tensor
            def _(h):
                P.emit("pe", h, sems)

            @block.scalar
            def _(h):
                P.emit("act", h, sems)

            @block.vector
            def _(h):
                P.emit("dve", h, sems)

            @block.gpsimd
            def _(h):
                P.emit("pool", h, sems)

            @block.sync
            def _(h):
                P.emit("sp", h, sems)
    nc._dbg_names = dbg_names
    return nc


def _host_inputs(x, p, positions, norm_g, w_in, q_lat_g, kv_lat_g, w_uq, w_ukv, q_norm_g, k_norm_g,
                 conv_w, conv_b, w_branch_attn, w_branch_conv, w_out, ple_norm_g, w_ple_gate, w_ple_proj):
    f32 = np.float32
    x = np.asarray(x, f32)
    p = np.asarray(p, f32)[0]
    positions = np.asarray(positions, np.int32)
    qg = np.asarray(q_norm_g, f32)[0]
    kg = np.asarray(k_norm_g, f32)[0]
    cw = np.asarray(conv_w, f32)[0]
    cb = np.asarray(conv_b, f32)[0]
    inv_freq = (1.0 / (10000.0 ** (np.arange(0, 64, 2, dtype=np.float32) / 64.0))).astype(f32)

    def rep(v):
        return np.ascontiguousarray(np.broadcast_to(np.asarray(v, f32)[None, :], (128, len(v))))

    shared = {
        "rows_a": np.ascontiguousarray(np.concatenate([rep(np.asarray(q_lat_g, f32)[0]), rep(np.asarray(kv_lat_g, f32)[0]), rep(kg[128:192])], axis=1)),
        "normg_row": rep(np.asarray(norm_g, f32)[0]),
        "pleg_row": rep(np.asarray(ple_norm_g, f32)[0]),
        "invf4": rep(np.tile(inv_freq, 4)),
        "ph_row": rep(np.concatenate([np.full(64, np.pi / 2, f32), np.zeros(64, f32)])),
        "w_in": np.ascontiguousarray(np.asarray(w_in, f32)[0]),
        "w_uq": np.ascontiguousarray(np.asarray(w_uq, f32)[0]),
        "w_ukv": np.ascontiguousarray(np.asarray(w_ukv, f32)[0]),
        "w_ba": np.ascontiguousarray(np.asarray(w_branch_attn, f32)[0]),
        "w_bc": np.ascontiguousarray(np.asarray(w_branch_conv, f32)[0]),
        "w_out": np.ascontiguousarray(np.asarray(w_out, f32)[0]),
        "w_pg": np.ascontiguousarray(np.asarray(w_ple_gate, f32)[0]),
        "w_pp": np.ascontiguousarray(np.asarray(w_ple_proj, f32)[0]),
    }
    cols0 = np.zeros((128, NCOL), f32)
    cols0[:, COL_GQN] = qg[0:128]
    cols0[:, COL_GQAUG] = np.concatenate([qg[128:192], qg[160:192], qg[128:160]])
    cols0[:, COL_GKN] = kg[0:128]
    cols0[:, COL_INVF] = np.tile(inv_freq, 4)
    cols0[:, COL_PH] = np.concatenate([np.full(64, np.pi / 2, f32), np.zeros(64, f32)])
    cols0[:, COL_EPS] = EPS
    cols0[:, COL_LNS] = -0.5 * math.log(192.0)
    for c in range(8):
        for j in range(3):
            cols0[:, COL_CW + 3 * c + j] = cw[j, c * 128:(c + 1) * 128]
        cols0[:, COL_CBIAS + c] = cb[c * 128:(c + 1) * 128]
    in_maps = []
    for core in range(8):
        b, hq = core // 2, core % 2
        own = slice(hq * T, (hq + 1) * T)
        oth = slice((1 - hq) * T, (2 - hq) * T)
        cols = cols0.copy()
        cols[:, COL_ML] = 1.0 if hq == 1 else 0.0
        cols[:, COL_MR] = 1.0 if hq == 0 else 0.0
        pos_loc = np.concatenate([positions[b, own], positions[b, oth]])
        m = dict(shared)
        m["x_loc"] = np.ascontiguousarray(np.concatenate([x[b, own], x[b, oth]], axis=0))
        m["p_loc"] = np.ascontiguousarray(p[b, own])
        m["pos_tm"] = np.ascontiguousarray(pos_loc.reshape(16, 128).T)
        m["pos_row"] = np.ascontiguousarray(np.broadcast_to(positions[b, own][None, :], (128, T)))
        m["cols"] = cols
        in_maps.append(m)
    return in_maps


_NC_CACHE = {}


def kernel(**inputs):
    in_maps = _host_inputs(**inputs)
    if "nc" not in _NC_CACHE:
        _NC_CACHE["nc"] = build_nc(debug=False)
    nc = _NC_CACHE["nc"]
    res = run_bass_kernel_spmd(nc, in_maps, core_ids=list(range(8)))
    out = np.empty((4, S, D), np.float32)
    for core in range(8):
        b, hq = core // 2, core % 2
        out[b, hq * T:(hq + 1) * T, :] = res.results[core]["out"]
    return out
```

```python
import math
from contextlib import ExitStack

import numpy as np
import concourse.bass as bass
import concourse.mybir as mybir
from concourse.bass_utils import run_bass_kernel_spmd

F32 = mybir.dt.float32
BF16 = mybir.dt.bfloat16
I32 = mybir.dt.int32
ALU = mybir.AluOpType
AF = mybir.ActivationFunctionType

D = 2048
S = 2048
T = 1024
NH = 16
EPS = 1e-6
IN_W = 11072
C_GA, C_CB, C_CC, C_CX, C_GC, C_MA, C_MC = 832, 2880, 3904, 4928, 5952, 6976, 9024
NCOL = 48
COL_GQN, COL_GQAUG, COL_GKN, COL_INVF, COL_PH, COL_ML, COL_MR, COL_EPS, COL_LNS, COL_CW, COL_CBIAS = 0, 1, 2, 3, 4, 5, 6, 7, 8, 9, 33

MAGIC = 12582912.0
INV2PI = 0.15915494309189535
CW1 = 6.28125
CW2 = 2.0 * math.pi - 6.28125
PI_LO = 3.1415925


class Res:
    def __init__(self, name, excl=False):
        self.name = name
        self.w = None
        self.r = {}
        self.rd = []
        self.excl = excl


class Slot:
    def __init__(self, name):
        self.name = name
        self.total = 0
        self.sem = None


class Op:
    __slots__ = ("eng", "fn", "deps", "signal", "count", "slot", "val")

    def __init__(self, eng, fn, deps, slot=None):
        self.eng = eng
        self.fn = fn
        self.deps = deps
        self.signal = False
        self.count = None
        self.slot = slot
        self.val = None


class Prog:
    ENGS = ("pe", "act", "dve", "pool", "sp")

    def __init__(self):
        self.ops = {e: [] for e in self.ENGS}
        self.slots = []
        self.last_real = {}
        self.bar = []

    def slot(self, name):
        s = Slot(name)
        self.slots.append(s)
        return s

    def _deps(self, eng, reads, writes, deps):
        d = [x for x in deps if x is not None]
        for r in reads:
            if r.w is not None:
                d.append(r.w)
            if r.excl:
                d.extend(o for e, o in r.r.items() if e != eng)
        for w in writes:
            if w.w is not None:
                d.append(w.w)
            d.extend(w.r.values())
            d.extend(w.rd)
        return d

    def _track(self, o, reads, writes):
        for r in reads:
            if o.slot is not None:
                r.rd.append(o)
            else:
                r.r[o.eng] = o
        for w in writes:
            w.w = o
            w.r = {}
            w.rd = []

    def op(self, eng, fn, reads=(), writes=(), deps=(), real=True):
        o = Op(eng, fn, self._deps(eng, reads, writes, deps))
        self._track(o, reads, writes)
        self.ops[eng].append(o)
        if real:
            self.last_real[eng] = o
        return o

    def dma(self, eng, fn, slot, reads=(), writes=(), deps=()):
        dl = [d for d in self._deps(eng, reads, writes, list(deps) + self.bar) if d.slot is not slot]
        o = Op(eng, fn, dl, slot=slot)
        slot.total += 16
        o.val = slot.total
        self._track(o, reads, writes)
        self.ops[eng].append(o)
        return o

    def barrier(self):
        last = [self.last_real[e] for e in ("pe", "act", "dve", "pool") if e in self.last_real]
        for e in self.ENGS:
            self.op(e, lambda h: h.nop(), deps=last, real=False)
        self.bar = last

    def finalize(self):
        for e in self.ENGS:
            for o in self.ops[e]:
                for d in o.deps:
                    if d.slot is None and not (d.eng == "pe" and e == "pe"):
                        d.signal = True
        for e in self.ENGS:
            c = 0
            for o in self.ops[e]:
                if o.slot is None and o.signal:
                    c += 1
                    o.count = c

    def emit(self, eng, h, sems):
        waited = {}
        cut = getattr(self, "cut", None)
        ops = self.ops[eng] if cut is None else self.ops[eng][:cut[eng]]
        for o in ops:
            for d in o.deps:
                if d.slot is not None:
                    key = id(d.slot)
                    if waited.get(key, 0) < d.val:
                        h.wait_ge(d.slot.sem, d.val)
                        waited[key] = d.val
                else:
                    if d.eng == "pe" and eng == "pe":
                        continue
                    if waited.get(d.eng, 0) < d.count:
                        h.wait_ge(sems[d.eng], d.count)
                        waited[d.eng] = d.count
            inst = o.fn(h)
            if o.slot is not None:
                inst.then_inc(o.slot.sem, 16)
            elif o.signal:
                inst.then_inc(sems[eng], 1)


class Buf:
    def __init__(self, ap, name):
        self.ap = ap
        self.res = Res(name)
        self.rl = None


def build_nc(debug=False, stop=None):
    nc = bass.Bass("TRN2", target_bir_lowering=False)
    P = Prog()

    def din(name, shape, dt=F32):
        return nc.dram_tensor(name, list(shape), dt, kind="ExternalInput").ap()

    x_d = din("x_loc", [S, D])
    p_d = din("p_loc", [T, 256])
    postm_d = din("pos_tm", [128, 16], I32)
    posrow_d = din("pos_row", [128, T], I32)
    cols_d = din("cols", [128, NCOL])
    rowsa_d = din("rows_a", [128, 832])
    normg_d = din("normg_row", [128, D])
    pleg_d = din("pleg_row", [128, D])
    invf4_d = din("invf4", [128, 128])
    phrow_d = din("ph_row", [128, 128])
    w_in_d = din("w_in", [D, IN_W])
    w_uq_d = din("w_uq", [512, 3072])
    w_ukv_d = din("w_ukv", [256, 4096])
    w_ba_d = din("w_ba", [D, D])
    w_bc_d = din("w_bc", [1024, D])
    w_out_d = din("w_out", [D, D])
    w_pg_d = din("w_pg", [D, D])
    w_pp_d = din("w_pp", [256, D])
    out_d = nc.dram_tensor("out", [T, D], F32, kind="ExternalOutput").ap()
    dbg_outs = {}

    w_in_v = w_in_d.rearrange("(c p) n -> p c n", p=128)
    w_uq_v = w_uq_d.rearrange("(c p) n -> p c n", p=128)
    w_ukv_v = w_ukv_d.rearrange("(c p) n -> p c n", p=128)
    w_ba_v = w_ba_d.rearrange("(c p) n -> p c n", p=128)
    w_bc_v = w_bc_d.rearrange("(c p) n -> p c n", p=128)
    w_out_v = w_out_d.rearrange("(c p) n -> p c n", p=128)
    w_pg_v = w_pg_d.rearrange("(c p) n -> p c n", p=128)
    w_pp_v = w_pp_d.rearrange("(c p) n -> p c n", p=128)

    with ExitStack() as es:
        ARENA_BYTES = 204800
        arena = es.enter_context(nc.sbuf_tensor("arena", [128, ARENA_BYTES // 4], F32))

        def view(off, dt, shape, name):
            n = 1
            for s_ in shape:
                n *= s_
            el = 2 if dt == BF16 else 4
            assert off % 4 == 0 and (n * el) % 4 == 0, (name, off, n)
            assert off + n * el <= ARENA_BYTES, (name, off, n * el)
            ap = arena[:, off // 4:(off + n * el) // 4]
            if dt != F32:
                ap = ap.bitcast(dt)
            if len(shape) == 2:
                ap = ap.rearrange("p (a b) -> p a b", a=shape[0])
            elif len(shape) == 3:
                ap = ap.rearrange("p (a b c) -> p a b c", a=shape[0], b=shape[1])
            return Buf(ap, name)

        class Carver:
            def __init__(self, start, end):
                self.off = start
                self.end = end

            def take(self, dt, shape, name):
                n = 1
                for s_ in shape:
                    n *= s_
                nb = n * (2 if dt == BF16 else 4)
                nb_al = (nb + 31) // 32 * 32
                b = view(self.off, dt, shape, name)
                self.off += nb_al
                assert self.off <= self.end, (name, self.off, self.end)
                return b

        def sb(name, shape, dt=F32):
            t = es.enter_context(nc.sbuf_tensor("sb_" + name, list(shape), dt))
            b = Buf(None, name)
            b.t = t
            return b

        cols = sb("cols", [128, NCOL])
        ident = sb("ident", [128, 128], BF16)
        identf = sb("identf", [128, 128])
        ones = sb("ones", [128, 128], BF16)
        ss = sb("ss", [128, 16])
        rs = sb("rs", [128, 16])
        ssq = sb("ssq", [128, 16])
        rq = sb("rq", [128, 16])
        sskv = sb("sskv", [128, 16])
        rkv = sb("rkv", [128, 16])
        sspe = sb("sspe", [128, 16])
        lnt = sb("lnt", [128, 16])
        skt = [sb(f"skt{i}", [128, 16]) for i in range(2)]
        sk = [sb(f"sk{i}", [128, 16]) for i in range(2)]
        ss2 = sb("ss2", [128, 8])
        rs2 = sb("rs2", [128, 8])
        posf = sb("posf", [128, 16])
        postm = sb("postm", [128, 16], I32)
        hT_halo = sb("hT_halo", [128, 16, 2], BF16)
        hcx = sb("hcx", [128, 4])

        def col(i):
            return cols.t[:, i:i + 1]

        banks = []
        for i in range(8):
            t = es.enter_context(nc.psum_tensor(f"bank{i}", [128, 512], F32))
            b = Buf(t[:], f"bank{i}")
            b.res.excl = True
            b.bf = t[:].bitcast(BF16)
            banks.append(b)

        sems = {e: es.enter_context(nc.semaphore(f"s_{e}")) for e in Prog.ENGS}

        hT_own = view(0, BF16, [16, 1024], "hT_own")
        AT = view(32768, BF16, [16, 1024], "AT")
        qaT = view(65536, BF16, [4, 1024], "qaT")
        kvaT = view(73728, BF16, [2, 2048], "kvaT")
        KrT = view(81920, BF16, [2048], "KrT")
        CS_fm = view(86016, F32, [1024], "CS_fm")
        L0 = 90112

        dq = {"sp": P.slot("sp_misc")}

        def maybe_stop(tag):
            if stop == tag and getattr(P, "cut", None) is None:
                P.barrier()
                fd = [o for o in P.ops["sp"] if o.slot is dq["sp"]][-1:]
                P.op("sp", lambda h: h.nop(), deps=fd + P.bar, real=False)
                P.cut = {e: len(P.ops[e]) for e in P.ENGS}
                print("STOP at", tag, {e: len(P.ops[e]) for e in P.ENGS})

        def ld(eng, out_ap, in_ap, wbuf, slot=None, reads=(), deps=()):
            if slot is None:
                if not hasattr(wbuf, "slot"):
                    wbuf.slot = P.slot("ld_" + wbuf.res.name)
                slot = wbuf.slot
            return P.dma(eng, lambda h: h.dma_start(out=out_ap, in_=in_ap), slot,
                         reads=R(*reads), writes=R(wbuf), deps=deps)

        def R(*bufs):
            out_ = []
            for b in bufs:
                out_.extend(b.rl if b.rl is not None else [b.res])
            return out_

        ld("sp", cols.t[:], cols_d, cols)
        ld("sp", postm.t[:], postm_d, postm)

        P.op("dve", lambda h: h.memset(identf.t[:], 1.0), writes=R(identf))
        P.op("pool", lambda h: h.affine_select(out=identf.t[:], in_=identf.t[:], pattern=[[-1, 128]],
                                               compare_op=ALU.is_equal, fill=0.0, base=0, channel_multiplier=1),
             reads=R(identf), writes=R(identf))
        P.op("dve", lambda h: h.tensor_copy(out=ident.t[:], in_=identf.t[:]), reads=R(identf), writes=R(ident))
        P.op("dve", lambda h: h.memset(ones.t[:], 1.0), writes=R(ones))
        for b_ in (ss, ssq, sskv, sspe, ss2):
            P.op("dve", lambda h, b_=b_: h.memset(b_.t[:], 0.0), writes=R(b_))

        W_BASE = 147456
        c0 = Carver(L0, W_BASE)
        xs = [c0.take(F32, [2048], f"xs{i}") for i in range(2)]
        hb = [c0.take(BF16, [2048], f"hb{i}") for i in range(2)]
        hTt = [c0.take(BF16, [16, 128], f"hTt{i}") for i in range(2)]
        normg = c0.take(F32, [2048], "normg")
        CS_tm = c0.take(F32, [16, 128], "CS_tm")
        rows_a = c0.take(F32, [832], "rows_a")
        qa_n = c0.take(BF16, [512], "qa_n")
        kva_n = c0.take(BF16, [256], "kva_n")
        kpe_g = c0.take(F32, [64], "kpe_g")
        tcb = c0.take(F32, [64], "tcb")
        tsb = c0.take(F32, [64], "tsb")
        Kr_tm = c0.take(BF16, [128], "Kr_tm")
        c0 = Carver(W_BASE, ARENA_BYTES)
        trA = c0.take(F32, [2048], "trA")
        trB = c0.take(F32, [2048], "trB")
        trC = c0.take(F32, [2048], "trC")
        posrow = c0.take(I32, [1024], "posrow")
        invf4 = c0.take(F32, [128], "invf4")
        phrow = c0.take(F32, [128], "phrow")
        Wlat = view(32768, BF16, [16, 832], "Wlat")
        junk = view(32768 + 26624, BF16, [2048], "junk")

        ld("sp", rows_a.ap, rowsa_d, rows_a)
        ld("sp", normg.ap, normg_d, normg)
        ld("sp", invf4.ap, invf4_d, invf4)
        ld("sp", phrow.ap, phrow_d, phrow)
        ld("sp", posrow.ap, posrow_d, posrow)
        ld("pool", Wlat.ap, w_in_v[:, :, 0:832], Wlat)

        def range_reduce(ang, kf, n):
            a, k = ang.ap[:, 0:n], kf.ap[:, 0:n]
            P.op("dve", lambda h: h.tensor_scalar(out=k, in0=a, scalar1=INV2PI, scalar2=MAGIC, op0=ALU.mult, op1=ALU.add),
                 reads=R(ang), writes=R(kf))
            P.op("dve", lambda h: h.tensor_scalar(out=k, in0=k, scalar1=-MAGIC, scalar2=None, op0=ALU.add),
                 reads=R(kf), writes=R(kf))
            P.op("dve", lambda h: h.scalar_tensor_tensor(out=a, in0=k, scalar=-CW1, in1=a, op0=ALU.mult, op1=ALU.add),
                 reads=R(kf, ang), writes=R(ang))
            P.op("dve", lambda h: h.scalar_tensor_tensor(out=a, in0=k, scalar=-CW2, in1=a, op0=ALU.mult, op1=ALU.add),
                 reads=R(kf, ang), writes=R(ang))

        def wrap_clamp(ang, m, n):
            a, k = ang.ap[:, 0:n], m.ap[:, 0:n]
            P.op("dve", lambda h: h.tensor_scalar(out=k, in0=a, scalar1=math.pi, scalar2=-2.0 * math.pi, op0=ALU.is_gt, op1=ALU.mult),
                 reads=R(ang), writes=R(m))
            P.op("dve", lambda h: h.tensor_tensor(out=a, in0=a, in1=k, op=ALU.add), reads=R(ang, m), writes=R(ang))
            P.op("dve", lambda h: h.tensor_scalar(out=a, in0=a, scalar1=PI_LO, scalar2=-PI_LO, op0=ALU.min, op1=ALU.max),
                 reads=R(ang), writes=R(ang))

        P.op("dve", lambda h: h.tensor_copy(out=trA.ap[:, 0:1024], in_=posrow.ap), reads=R(posrow), writes=R(trA))
        P.op("dve", lambda h: h.tensor_scalar(out=trA.ap[:, 0:1024], in0=trA.ap[:, 0:1024], scalar1=col(COL_INVF), scalar2=None, op0=ALU.mult),
             reads=R(trA, cols), writes=R(trA))
        range_reduce(trA, trB, 1024)
        P.op("dve", lambda h: h.tensor_scalar(out=trA.ap[:, 0:1024], in0=trA.ap[:, 0:1024], scalar1=col(COL_PH), scalar2=None, op0=ALU.add),
             reads=R(trA, cols), writes=R(trA))
        wrap_clamp(trA, trB, 1024)
        P.op("act", lambda h: h.activation(out=CS_fm.ap, in_=trA.ap[:, 0:1024], func=AF.Sin), reads=R(trA), writes=R(CS_fm))

        P.op("dve", lambda h: h.tensor_copy(out=posf.t[:], in_=postm.t[:]), reads=R(postm), writes=R(posf))
        for t in range(16):
            P.op("dve", lambda h, t=t: h.tensor_scalar(out=trC.ap[:, t * 128:(t + 1) * 128], in0=invf4.ap, scalar1=posf.t[:, t:t + 1],
                                                      scalar2=None, op0=ALU.mult), reads=R(invf4, posf), writes=R(trC))
        range_reduce(trC, trB, 2048)
        for t in range(16):
            P.op("dve", lambda h, t=t: h.tensor_tensor(out=trC.ap[:, t * 128:(t + 1) * 128], in0=trC.ap[:, t * 128:(t + 1) * 128],
                                                      in1=phrow.ap, op=ALU.add), reads=R(trC, phrow), writes=R(trC))
        wrap_clamp(trC, trB, 2048)
        P.op("act", lambda h: h.activation(out=CS_tm.ap.rearrange("p a b -> p (a b)"), in_=trC.ap, func=AF.Sin),
             reads=R(trC), writes=R(CS_tm))

        trig_done = [P.last_real["dve"], P.last_real["act"]]
        wukv = view(W_BASE, BF16, [2, 4096], "wukv")
        wuq = view(W_BASE + 16384, BF16, [4, 3072], "wuq")
        Bmat = view(W_BASE + 40960, BF16, [4, 16, 128], "Bmat")
        ld("pool", wukv.ap, w_ukv_v, wukv, deps=trig_done)
        ld("pool", wuq.ap, w_uq_v, wuq, deps=trig_done)

        def transposes(src_ap, nblk, bank_views, reads, bank_bufs):
            for c in range(nblk):
                bv = bank_views[c]
                P.op("pe", lambda h, c=c, bv=bv: h.transpose(out=bv, in_=src_ap[:, c * 128:(c + 1) * 128], identity=ident.t[:]),
                     reads=reads + R(ident), writes=[bank_bufs[c].res])

        maybe_stop("p0")
        tile_dst = {}

        def stA(t):
            par = t % 2
            xb_, hb_ = xs[par], hb[par]
            ld("sp", xb_.ap, x_d[t * 128:(t + 1) * 128, :], xb_)
            P.op("act", lambda h: h.activation(out=junk.ap, in_=xb_.ap, func=AF.Square, accum_out=ss.t[:, t:t + 1]),
                 reads=R(xb_), writes=R(junk, ss))
            P.op("act", lambda h: h.activation(out=lnt.t[:, 0:1], in_=ss.t[:, t:t + 1], func=AF.Ln, bias=col(COL_EPS), scale=1.0 / D),
                 reads=R(ss, cols), writes=R(lnt))
            P.op("act", lambda h: h.activation(out=rs.t[:, t:t + 1], in_=lnt.t[:, 0:1], func=AF.Exp, scale=-0.5),
                 reads=R(lnt), writes=R(rs))
            P.op("dve", lambda h: h.scalar_tensor_tensor(out=hb_.ap, in0=xb_.ap, scalar=rs.t[:, t:t + 1], in1=normg.ap,
                                                         op0=ALU.mult, op1=ALU.mult),
                 reads=R(xb_, rs, normg), writes=R(hb_))

        def stB(t):
            own = t < 8
            par = t % 2
            hb_ = hb[par]
            bx0, bx1 = banks[0], banks[1]
            bb = [bx0, bx1]
            bviews = [bb[c // 8].bf[:, (c % 8) * 128:(c % 8 + 1) * 128] for c in range(16)]
            transposes(hb_.ap, 16, bviews, R(hb_), [bb[c // 8] for c in range(16)])
            if own:
                dst = hT_own
                d0 = hT_own.ap[:, 0:8, t * 128:(t + 1) * 128]
                d1 = hT_own.ap[:, 8:16, t * 128:(t + 1) * 128]
            else:
                dst = hTt[par]
                d0 = dst.ap[:, 0:8, :]
                d1 = dst.ap[:, 8:16, :]
            tile_dst[t] = dst
            P.op("act", lambda h: h.copy(out=d0, in_=bx0.bf.rearrange("p (a b) -> p a b", a=8)),
                 reads=R(bx0), writes=R(dst))
            P.op("dve", lambda h: h.tensor_copy(out=d1, in_=bx1.bf.rearrange("p (a b) -> p a b", a=8)),
                 reads=R(bx1), writes=R(dst))
            if t == 8:
                P.op("dve", lambda h: h.tensor_scalar(out=hT_halo.t[:, :, 1:2], in0=dst.ap[:, :, 0:1], scalar1=col(COL_MR),
                                                      scalar2=None, op0=ALU.mult), reads=R(dst, cols), writes=R(hT_halo))
            if t == 15:
                P.op("dve", lambda h: h.tensor_scalar(out=hT_halo.t[:, :, 0:1], in0=dst.ap[:, :, 127:128], scalar1=col(COL_ML),
                                                      scalar2=None, op0=ALU.mult), reads=R(dst, cols), writes=R(hT_halo))

        def zbanks(t):
            par = t % 2
            return banks[2 + 2 * par], banks[3 + 2 * par]

        def stC(t):
            own = t < 8
            dst = tile_dst[t]
            bq, bk = zbanks(t)

            def hsrc(kc):
                if own:
                    return dst.ap[:, kc, t * 128:(t + 1) * 128]
                return dst.ap[:, kc, :]

            if own:
                for kc in range(16):
                    P.op("pe", lambda h, kc=kc: h.matmul(bq.ap, lhsT=hsrc(kc), rhs=Wlat.ap[:, kc, 0:512],
                                                        start=(kc == 0), stop=(kc == 15)),
                         reads=R(dst, Wlat), writes=R(bq))
            for kc in range(16):
                P.op("pe", lambda h, kc=kc: h.matmul(bk.ap[:, 0:320], lhsT=hsrc(kc), rhs=Wlat.ap[:, kc, 512:832],
                                                    start=(kc == 0), stop=(kc == 15)),
                     reads=R(dst, Wlat), writes=R(bk))

        qa_n2 = [qa_n, Buf(junk.ap[:, 512:1024], "qa_n_b")]

        def stD(t):
            own = t < 8
            bq, bk = zbanks(t)
            if own:
                P.op("act", lambda h: h.activation(out=junk.ap[:, 0:512], in_=bq.ap, func=AF.Square, accum_out=ssq.t[:, t:t + 1]),
                     reads=R(bq), writes=R(junk, ssq))
            P.op("act", lambda h: h.activation(out=junk.ap[:, 0:256], in_=bk.ap[:, 0:256], func=AF.Square, accum_out=sskv.t[:, t:t + 1]),
                 reads=R(bk), writes=R(junk, sskv))
            P.op("act", lambda h: h.activation(out=junk.ap[:, 0:64], in_=bk.ap[:, 256:320], func=AF.Square, accum_out=sspe.t[:, t:t + 1]),
                 reads=R(bk), writes=R(junk, sspe))
            if own:
                P.op("act", lambda h: h.activation(out=lnt.t[:, 1:2], in_=ssq.t[:, t:t + 1], func=AF.Ln, bias=col(COL_EPS), scale=1.0 / 512),
                     reads=R(ssq, cols), writes=R(lnt))
                P.op("act", lambda h: h.activation(out=rq.t[:, t:t + 1], in_=lnt.t[:, 1:2], func=AF.Exp, scale=-0.5),
                     reads=R(lnt), writes=R(rq))
            P.op("act", lambda h: h.activation(out=lnt.t[:, 2:3], in_=sskv.t[:, t:t + 1], func=AF.Ln, bias=col(COL_EPS), scale=1.0 / 256),
                 reads=R(sskv, cols), writes=R(lnt))
            P.op("act", lambda h: h.activation(out=rkv.t[:, t:t + 1], in_=lnt.t[:, 2:3], func=AF.Exp, scale=-0.5),
                 reads=R(lnt), writes=R(rkv))
            if own:
                P.op("dve", lambda h: h.scalar_tensor_tensor(out=qa_n.ap, in0=bq.ap, scalar=rq.t[:, t:t + 1], in1=rows_a.ap[:, 0:512],
                                                             op0=ALU.mult, op1=ALU.mult), reads=R(bq, rq, rows_a), writes=R(qa_n))
            P.op("dve", lambda h: h.scalar_tensor_tensor(out=kva_n.ap, in0=bk.ap[:, 0:256], scalar=rkv.t[:, t:t + 1],
                                                         in1=rows_a.ap[:, 512:768], op0=ALU.mult, op1=ALU.mult),
                 reads=R(bk, rkv, rows_a), writes=R(kva_n))
            P.op("dve", lambda h: h.tensor_tensor(out=kpe_g.ap, in0=bk.ap[:, 256:320], in1=rows_a.ap[:, 768:832], op=ALU.mult),
                 reads=R(bk, rows_a), writes=R(kpe_g))
            P.op("dve", lambda h: h.tensor_tensor(out=tcb.ap, in0=kpe_g.ap, in1=CS_tm.ap[:, t, 0:64], op=ALU.mult),
                 reads=R(kpe_g, CS_tm), writes=R(tcb))
            P.op("dve", lambda h: h.tensor_tensor(out=tsb.ap, in0=kpe_g.ap, in1=CS_tm.ap[:, t, 64:128], op=ALU.mult),
                 reads=R(kpe_g, CS_tm), writes=R(tsb))
            for rep in range(2):
                o_ = rep * 64
                P.op("dve", lambda h, o_=o_: h.tensor_tensor(out=Kr_tm.ap[:, o_:o_ + 32], in0=tcb.ap[:, 0:32], in1=tsb.ap[:, 32:64], op=ALU.subtract),
                     reads=R(tcb, tsb), writes=R(Kr_tm))
                P.op("dve", lambda h, o_=o_: h.tensor_tensor(out=Kr_tm.ap[:, o_ + 32:o_ + 64], in0=tcb.ap[:, 32:64], in1=tsb.ap[:, 0:32], op=ALU.add),
                     reads=R(tcb, tsb), writes=R(Kr_tm))

        def stE(t):
            own = t < 8
            par = t % 2
            b4 = banks[6 + par]
            if own:
                transposes(qa_n.ap, 4, [b4.bf[:, c * 128:(c + 1) * 128] for c in range(4)], R(qa_n), [b4] * 4)
            transposes(kva_n.ap, 2, [b4.bf[:, (4 + c) * 128:(5 + c) * 128] for c in range(2)], R(kva_n), [b4] * 2)
            transposes(Kr_tm.ap, 1, [b4.bf[:, 768:896]], R(Kr_tm), [b4])
            if own:
                P.op("act", lambda h: h.copy(out=qaT.ap[:, :, t * 128:(t + 1) * 128], in_=b4.bf[:, 0:512].rearrange("p (a b) -> p a b", a=4)),
                     reads=R(b4), writes=R(qaT))
            P.op("dve", lambda h: h.tensor_copy(out=kvaT.ap[:, :, t * 128:(t + 1) * 128], in_=b4.bf[:, 512:768].rearrange("p (a b) -> p a b", a=2)),
                 reads=R(b4), writes=R(kvaT))
            P.op("dve", lambda h: h.tensor_copy(out=KrT.ap[:, t * 128:(t + 1) * 128], in_=b4.bf[:, 768:896]),
                 reads=R(b4), writes=R(KrT))

        stages_1a = [stA, stB, stC, stD, stE]
        for s_ in range(16 + len(stages_1a) - 1):
            for k_ in reversed(range(len(stages_1a))):
                t_ = s_ - k_
                if 0 <= t_ < 16:
                    stages_1a[k_](t_)

        if debug:
            dbg_outs["hT_own"] = (hT_own, [128, 16 * 1024], BF16)
            dbg_outs["qaT"] = (qaT, [128, 4 * 1024], BF16)
            dbg_outs["kvaT"] = (kvaT, [128, 2 * 2048], BF16)
            dbg_outs["KrT"] = (KrT, [128, 2048], BF16)
            dbg_outs["CS_fm"] = (CS_fm, [128, 1024], F32)

        def dump_now(names):
            P.barrier()
            for nm in names:
                buf, shape, dt = dbg_outs.pop(nm)
                dd = nc.dram_tensor("dbg_" + nm, shape, dt, kind="ExternalOutput").ap()
                src = buf.ap
                if len(src.shape) == 3:
                    src = src.rearrange("p a b -> p (a b)")
                P.dma("sp", lambda h, dd=dd, src=src: h.dma_start(out=dd, in_=src), dq["sp"], reads=R(buf))
                dbg_names.append("dbg_" + nm)

        dbg_names = []

        if debug:
            dump_now(["hT_own", "qaT", "kvaT", "KrT", "CS_fm"])
        maybe_stop("p1a")

        P.barrier()
        c1 = Carver(L0, W_BASE)
        KnT = [c1.take(BF16, [2048], f"KnT{i}") for i in range(2)]
        Vsb = [c1.take(BF16, [16, 128], f"V{i}") for i in range(2)]
        QnT = [c1.take(BF16, [1024], f"QnT{i}") for i in range(2)]
        QrT = [c1.take(BF16, [1024], f"QrT{i}") for i in range(2)]
        PT = [c1.take(BF16, [512], f"PT{i}") for i in range(4)]
        sqA = [c1.take(BF16, [512], f"sqA{i}") for i in range(2)]
        sqB = [c1.take(BF16, [512], f"sqB{i}") for i in range(2)]
        sqK = [c1.take(BF16, [512], f"sqK{i}") for i in range(2)]
        lnv = [c1.take(F32, [512], f"lnv{i}") for i in range(2)]
        fq = [c1.take(F32, [512], f"fq{i}") for i in range(2)]
        tmpB = [c1.take(F32, [512], f"tmpB{i}") for i in range(2)]
        rl = [c1.take(F32, [512], f"rl{i}") for i in range(2)]
        acc = [c1.take(F32, [512], f"acc{i}") for i in range(2)]
        onesf = identf
        P.op("dve", lambda h: h.memset(onesf.t[:], 1.0), reads=R(ident), writes=R(onesf))

        wuq4 = wuq.ap.rearrange("p c (h d) -> p c h d", h=16)
        P.op("dve", lambda h: h.tensor_copy(out=Bmat.ap[:, :, :, 0:64], in_=wuq4[:, :, :, 128:192]), reads=R(wuq), writes=R(Bmat))
        P.op("dve", lambda h: h.tensor_scalar(out=Bmat.ap[:, :, :, 64:96], in0=wuq4[:, :, :, 160:192], scalar1=-1.0, scalar2=None, op0=ALU.mult),
             reads=R(wuq), writes=R(Bmat))
        P.op("dve", lambda h: h.tensor_copy(out=Bmat.ap[:, :, :, 96:128], in_=wuq4[:, :, :, 128:160]), reads=R(wuq), writes=R(Bmat))

        bS, bO, bL, bB, bSS, bKV = [banks[0], banks[1]], [banks[2], banks[3]], banks[4], banks[5], banks[6], banks[7]
        bA = bKV

        def upproj_stages(hh):
            hp = hh % 2
            Kn, Vv, Qn, Qr = KnT[hp], Vsb[hp], QnT[hp], QrT[hp]
            skt_h, sk_h = skt[hp], sk[hp]
            st = []

            def K_a(tb):
                for kc in range(2):
                    P.op("pe", lambda h, kc=kc: h.matmul(bKV.ap, lhsT=wukv.ap[:, kc, hh * 256:hh * 256 + 128],
                                                        rhs=kvaT.ap[:, kc, tb * 512:(tb + 1) * 512], start=(kc == 0), stop=(kc == 1)),
                         reads=R(wukv, kvaT), writes=R(bKV))

            def K_b(tb):
                sq_ = sqK[tb % 2]
                P.op("act", lambda h: h.activation(out=sq_.ap, in_=bKV.ap, func=AF.Square), reads=R(bKV), writes=R(sq_))
                P.op("dve", lambda h: h.tensor_scalar(out=Kn.ap[:, tb * 512:(tb + 1) * 512], in0=bKV.ap, scalar1=col(COL_GKN),
                                                      scalar2=None, op0=ALU.mult), reads=R(bKV, cols), writes=R(Kn))

            def K_c(tb):
                sq_ = sqK[tb % 2]
                for j in range(4):
                    kt = tb * 4 + j
                    P.op("pe", lambda h, j=j, kt=kt: h.matmul(bSS.ap[:, kt:kt + 1], lhsT=sq_.ap[:, j * 128:(j + 1) * 128],
                                                             rhs=ones.t[:, 0:1], start=True, stop=True),
                         reads=R(sq_, ones), writes=R(bSS))

            def K_d():
                P.op("dve", lambda h: h.tensor_tensor(out=skt_h.t[:], in0=bSS.ap[:, 0:16], in1=sspe.t[:], op=ALU.add),
                     reads=R(bSS, sspe), writes=R(skt_h))

            def K_e():
                P.op("act", lambda h: h.activation(out=skt_h.t[:], in_=skt_h.t[:], func=AF.Ln, bias=col(COL_EPS), scale=1.0 / 192),
                     reads=R(skt_h, cols), writes=R(skt_h))
                P.op("act", lambda h: h.activation(out=sk_h.t[:], in_=skt_h.t[:], func=AF.Exp, bias=col(COL_LNS), scale=-0.5),
                     reads=R(skt_h, cols), writes=R(sk_h))

            def V_a(rnd):
                for j in range(4):
                    kt = rnd * 4 + j
                    for kc in range(2):
                        P.op("pe", lambda h, kc=kc, kt=kt, j=j: h.matmul(bKV.ap[:, j * 128:(j + 1) * 128], lhsT=kvaT.ap[:, kc, kt * 128:(kt + 1) * 128],
                                                                       rhs=wukv.ap[:, kc, hh * 256 + 128:hh * 256 + 256],
                                                                       start=(kc == 0), stop=(kc == 1), skip_group_check=True),
                             reads=R(kvaT, wukv), writes=R(bKV))

            def V_b(rnd):
                eng = "act" if rnd % 2 == 0 else "dve"
                if eng == "act":
                    P.op("act", lambda h: h.copy(out=Vv.ap[:, rnd * 4:(rnd + 1) * 4, :], in_=bKV.ap.rearrange("p (a b) -> p a b", a=4)),
                         reads=R(bKV), writes=R(Vv))
                else:
                    P.op("dve", lambda h: h.tensor_copy(out=Vv.ap[:, rnd * 4:(rnd + 1) * 4, :], in_=bKV.ap.rearrange("p (a b) -> p a b", a=4)),
                         reads=R(bKV), writes=R(Vv))

            def Q_a(qb):
                qs = slice(qb * 512, (qb + 1) * 512)
                for kc in range(4):
                    P.op("pe", lambda h, kc=kc: h.matmul(bA.ap, lhsT=wuq.ap[:, kc, hh * 192:hh * 192 + 128], rhs=qaT.ap[:, kc, qs],
                                                        start=(kc == 0), stop=(kc == 3)), reads=R(wuq, qaT), writes=R(bA))
                for kc in range(4):
                    P.op("pe", lambda h, kc=kc: h.matmul(bB.ap, lhsT=Bmat.ap[:, kc, hh, :], rhs=qaT.ap[:, kc, qs],
                                                        start=(kc == 0), stop=(kc == 3)), reads=R(Bmat, qaT), writes=R(bB))

            def Q_b(qb):
                sa, sb_ = sqA[qb], sqB[qb]
                P.op("act", lambda h: h.activation(out=sa.ap, in_=bA.ap, func=AF.Square), reads=R(bA), writes=R(sa))
                P.op("act", lambda h: h.activation(out=sb_.ap[0:64, :], in_=bB.ap[0:64, :], func=AF.Square), reads=R(bB), writes=R(sb_))

            def Q_c(qb):
                sa, sb_ = sqA[qb], sqB[qb]
                P.op("pe", lambda h: h.matmul(bSS.ap, lhsT=ones.t[:], rhs=sa.ap, start=True, stop=False), reads=R(ones, sa), writes=R(bSS))
                P.op("pe", lambda h: h.matmul(bSS.ap, lhsT=ones.t[0:64, :], rhs=sb_.ap[0:64, :], start=False, stop=True),
                     reads=R(ones, sb_), writes=R(bSS))

            def Q_d(qb):
                lv, f_ = lnv[qb], fq[qb]
                P.op("act", lambda h: h.activation(out=lv.ap, in_=bSS.ap, func=AF.Ln, bias=col(COL_EPS), scale=1.0 / 192),
                     reads=R(bSS, cols), writes=R(lv))
                P.op("act", lambda h: h.activation(out=f_.ap, in_=lv.ap, func=AF.Exp, scale=-0.5), reads=R(lv), writes=R(f_))

            def Q_e(qb):
                qs = slice(qb * 512, (qb + 1) * 512)
                f_, tb_ = fq[qb], tmpB[qb]
                P.op("dve", lambda h: h.scalar_tensor_tensor(out=Qn.ap[:, qs], in0=bA.ap, scalar=col(COL_GQN), in1=f_.ap,
                                                             op0=ALU.mult, op1=ALU.mult), reads=R(bA, cols, f_), writes=R(Qn))
                P.op("dve", lambda h: h.scalar_tensor_tensor(out=tb_.ap, in0=bB.ap, scalar=col(COL_GQAUG), in1=f_.ap,
                                                             op0=ALU.mult, op1=ALU.mult), reads=R(bB, cols, f_), writes=R(tb_))
                P.op("dve", lambda h: h.tensor_tensor(out=Qr.ap[:, qs], in0=tb_.ap, in1=CS_fm.ap[:, qs], op=ALU.mult),
                     reads=R(tb_, CS_fm), writes=R(Qr))

            for tb in range(4):
                st += [lambda tb=tb: K_a(tb), lambda tb=tb: K_b(tb), lambda tb=tb: K_c(tb)]
            st += [K_d, K_e]
            for rnd in range(4):
                st += [lambda rnd=rnd: V_a(rnd), lambda rnd=rnd: V_b(rnd)]
            for qb in range(2):
                st += [lambda qb=qb: Q_a(qb), lambda qb=qb: Q_b(qb), lambda qb=qb: Q_c(qb), lambda qb=qb: Q_d(qb), lambda qb=qb: Q_e(qb)]
            return st

        pt_i = 0
        gstep = 0
        pending = []
        for st_ in upproj_stages(0):
            st_()
        for hh in range(NH):
            hp = hh % 2
            Kn, Vv, Qn, Qr = KnT[hp], Vsb[hp], QnT[hp], QrT[hp]
            sk_h = sk[hp]
            nxt = upproj_stages(hh + 1) if hh + 1 < NH else []
            step = 0
            for qb in range(2):
                qs = slice(qb * 512, (qb + 1) * 512)
                bo, ac = bO[qb], acc[qb]

                def emit_S(kt, Kn=Kn, Qn=Qn, Qr=Qr, qs=qs):
                    bs = bS[kt % 2]
                    P.op("pe", lambda h: h.matmul(bs.ap, lhsT=Kn.ap[:, kt * 128:(kt + 1) * 128], rhs=Qn.ap[:, qs], start=True, stop=False),
                         reads=R(Kn, Qn), writes=R(bs))
                    P.op("pe", lambda h: h.matmul(bs.ap, lhsT=KrT.ap[:, kt * 128:(kt + 1) * 128], rhs=Qr.ap[:, qs], start=False, stop=True),
                         reads=R(KrT, Qr), writes=R(bs))

                emit_S(0)
                for kt in range(16):
                    for due_, fn_ in [p_ for p_ in pending if p_[0] <= gstep]:
                        fn_()
                    pending[:] = [p_ for p_ in pending if p_[0] > gstep]
                    gstep += 1
                    if kt + 1 < 16:
                        emit_S(kt + 1)
                    pt = PT[pt_i % 4]
                    pt_i += 1
                    bs = bS[kt % 2]
                    P.op("act", lambda h, pt=pt, bs=bs, kt=kt, sk_h=sk_h: h.activation(out=pt.ap, in_=bs.ap, func=AF.Exp, scale=sk_h.t[:, kt:kt + 1]),
                         reads=R(bs, sk_h), writes=R(pt))
                    P.op("pe", lambda h, pt=pt, kt=kt, Vv=Vv, bo=bo: h.matmul(bo.ap, lhsT=Vv.ap[:, kt, :], rhs=pt.ap, start=(kt == 0), stop=(kt == 15)),
                         reads=R(Vv, pt), writes=R(bo))
                    if kt == 0:
                        P.op("dve", lambda h, pt=pt, ac=ac: h.tensor_copy(out=ac.ap, in_=pt.ap), reads=R(pt), writes=R(ac))
                    else:
                        P.op("dve", lambda h, pt=pt, ac=ac: h.tensor_tensor(out=ac.ap, in0=ac.ap, in1=pt.ap, op=ALU.add), reads=R(ac, pt), writes=R(ac))
                    if step < len(nxt):
                        nxt[step]()
                    step += 1
                r_ = rl[qb]

                def tail1(ac=ac):
                    P.op("pe", lambda h: h.matmul(bL.ap, lhsT=onesf.t[:], rhs=ac.ap, start=True, stop=True), reads=R(onesf, ac), writes=R(bL))

                def tail2(r_=r_):
                    P.op("act", lambda h: h.activation(out=r_.ap, in_=bL.ap, func=AF.Ln), reads=R(bL), writes=R(r_))
                    P.op("act", lambda h: h.activation(out=r_.ap, in_=r_.ap, func=AF.Exp, scale=-1.0), reads=R(r_), writes=R(r_))

                def tail3(r_=r_, hh=hh, qs=qs, bo=bo):
                    P.op("dve", lambda h: h.tensor_tensor(out=AT.ap[:, hh, qs], in0=bo.ap, in1=r_.ap, op=ALU.mult),
                         reads=R(bo, r_), writes=R(AT))

                pending.extend([(gstep + 1, tail1), (gstep + 2, tail2), (gstep + 4, tail3)])
            assert step >= len(nxt)
        for due_, fn_ in sorted(pending, key=lambda p_: p_[0]):
            fn_()

        if debug:
            dbg_outs["AT"] = (AT, [128, 16 * 1024], BF16)
            dump_now(["AT"])
        maybe_stop("p1b")

        P.barrier()
        slab = [view(L0 + i * 28672, BF16, [14336], f"slab{i}") for i in range(2)]
        cvg = view(L0 + 57344, BF16, [8, 1024], "cvg")
        merged = view(L0 + 73728, BF16, [16, 1024], "merged")
        Fs = [view(65536 + i * 2048, F32, [512], f"F{i}") for i in range(8)]
        ubuf = view(65536 + 16384, F32, [1026], "ubuf")
        ybuf = Buf(arena[:, (65536 + 6 * 2048) // 4:(65536 + 8 * 2048) // 4], "ybuf")
        ybuf.rl = [Fs[6].res, Fs[7].res]
        slab_i = 0

        def next_slab():
            nonlocal slab_i
            s_ = slab[slab_i % 2]
            slab_i += 1
            return s_

        for sgrp in range(4):
            sl = next_slab()
            slv = sl.ap[:, 0:8192].rearrange("p (c n) -> p c n", c=16)
            ld("pool", slv, w_in_v[:, :, C_GA + sgrp * 512:C_GA + (sgrp + 1) * 512], sl)
            for mm_ in range(4):
                m = sgrp * 4 + mm_
                for tb in range(2):
                    bk = banks[(m * 2 + tb) % 4]
                    ts_ = slice(tb * 512, (tb + 1) * 512)
                    for kc in range(16):
                        P.op("pe", lambda h, kc=kc, bk=bk, slv=slv, mm_=mm_, ts_=ts_: h.matmul(bk.ap, lhsT=slv[:, kc, mm_ * 128:(mm_ + 1) * 128],
                                                                                           rhs=hT_own.ap[:, kc, ts_], start=(kc == 0), stop=(kc == 15)),
                             reads=R(sl, hT_own), writes=R(bk))
                    sg = Fs[tb]
                    sgv = sg.ap.bitcast(BF16)[:, 0:512]
                    P.op("act", lambda h, bk=bk, sgv=sgv: h.activation(out=sgv, in_=bk.ap, func=AF.Silu), reads=R(bk), writes=R(sg))
                    P.op("dve", lambda h, m=m, ts_=ts_, sgv=sgv: h.tensor_tensor(out=AT.ap[:, m, ts_], in0=AT.ap[:, m, ts_], in1=sgv, op=ALU.mult),
                         reads=R(AT, sg), writes=R(AT))

        for c in range(8):
            sl = next_slab()
            slv = sl.ap[:, 0:8192].rearrange("p (c g n) -> p c g n", c=16, g=4)
            for g_ in range(4):
                c_lo = C_CB + g_ * 1024 + c * 128
                ld("pool", slv[:, :, g_, :], w_in_v[:, :, c_lo:c_lo + 128], sl)

            def proj(bk, g, rhs_fn, ncols, slv=slv, sl=sl):
                for kc in range(16):
                    P.op("pe", lambda h, kc=kc: h.matmul(bk.ap[:, 0:ncols] if ncols < 512 else bk.ap, lhsT=slv[:, kc, g, :], rhs=rhs_fn(kc),
                                                       start=(kc == 0), stop=(kc == 15)), reads=R(sl, hT_own, hT_halo), writes=R(bk))

            for tb in range(2):
                ts_ = slice(tb * 512, (tb + 1) * 512)
                bc_, bx_ = banks[tb], banks[2 + tb]
                proj(bc_, 1, lambda kc, ts_=ts_: hT_own.ap[:, kc, ts_], 512)
                proj(bx_, 2, lambda kc, ts_=ts_: hT_own.ap[:, kc, ts_], 512)
                cc = Fs[tb]
                P.op("act", lambda h, cc=cc, bc_=bc_: h.copy(out=cc.ap, in_=bc_.ap), reads=R(bc_), writes=R(cc))
                P.op("dve", lambda h, cc=cc, bx_=bx_, tb=tb: h.tensor_tensor(out=ubuf.ap[:, 1 + tb * 512:1 + (tb + 1) * 512], in0=bx_.ap, in1=cc.ap, op=ALU.mult),
                     reads=R(bx_, cc), writes=R(ubuf))
            bh = banks[4]
            for kc in range(16):
                P.op("pe", lambda h, kc=kc, slv=slv: h.matmul(bh.ap[:, 0:2], lhsT=slv[:, kc, 1, :], rhs=hT_halo.t[:, kc, :], start=(kc == 0), stop=(kc == 15)),
                     reads=R(sl, hT_halo), writes=R(bh))
            for kc in range(16):
                P.op("pe", lambda h, kc=kc, slv=slv: h.matmul(bh.ap[:, 2:4], lhsT=slv[:, kc, 2, :], rhs=hT_halo.t[:, kc, :], start=(kc == 0), stop=(kc == 15),
                                                            skip_group_check=True), reads=R(sl, hT_halo), writes=R(bh))
            P.op("act", lambda h: h.copy(out=hcx.t[:, 0:2], in_=bh.ap[:, 0:2]), reads=R(bh), writes=R(hcx))
            P.op("dve", lambda h: h.tensor_tensor(out=ubuf.ap[:, 0:1], in0=bh.ap[:, 2:3], in1=hcx.t[:, 0:1], op=ALU.mult),
                 reads=R(bh, hcx), writes=R(ubuf))
            P.op("dve", lambda h: h.tensor_tensor(out=ubuf.ap[:, 1025:1026], in0=bh.ap[:, 3:4], in1=hcx.t[:, 1:2], op=ALU.mult),
                 reads=R(bh, hcx), writes=R(ubuf))
            P.op("dve", lambda h, c=c: h.tensor_scalar(out=ybuf.ap, in0=ubuf.ap[:, 0:1024], scalar1=col(COL_CW + 3 * c), scalar2=col(COL_CBIAS + c),
                                                      op0=ALU.mult, op1=ALU.add), reads=R(ubuf, cols), writes=R(ybuf))
            P.op("dve", lambda h, c=c: h.scalar_tensor_tensor(out=ybuf.ap, in0=ubuf.ap[:, 1:1025], scalar=col(COL_CW + 3 * c + 1), in1=ybuf.ap,
                                                             op0=ALU.mult, op1=ALU.add), reads=R(ubuf, cols, ybuf), writes=R(ybuf))
            P.op("dve", lambda h, c=c: h.scalar_tensor_tensor(out=ybuf.ap, in0=ubuf.ap[:, 2:1026], scalar=col(COL_CW + 3 * c + 2), in1=ybuf.ap,
                                                             op0=ALU.mult, op1=ALU.add), reads=R(ubuf, cols, ybuf), writes=R(ybuf))
            for tb in range(2):
                ts_ = slice(tb * 512, (tb + 1) * 512)
                bb_, bg_ = banks[5 + tb], banks[(7 + tb) if tb == 0 else 4]
                proj(bb_, 0, lambda kc, ts_=ts_: hT_own.ap[:, kc, ts_], 512)
                proj(bg_, 3, lambda kc, ts_=ts_: hT_own.ap[:, kc, ts_], 512)
                sgc, t1 = Fs[2 + tb], Fs[4 + tb]
                P.op("act", lambda h, sgc=sgc, bg_=bg_: h.activation(out=sgc.ap, in_=bg_.ap, func=AF.Silu), reads=R(bg_), writes=R(sgc))
                P.op("dve", lambda h, t1=t1, bb_=bb_, ts_=ts_: h.tensor_tensor(out=t1.ap, in0=bb_.ap, in1=ybuf.ap[:, ts_], op=ALU.mult),
                     reads=R(bb_, ybuf), writes=R(t1))
                P.op("dve", lambda h, t1=t1, sgc=sgc, c=c, ts_=ts_: h.tensor_tensor(out=cvg.ap[:, c, ts_], in0=t1.ap, in1=sgc.ap, op=ALU.mult),
                     reads=R(t1, sgc), writes=R(cvg))

        for g in range(8):
            sl = next_slab()
            s_ma = sl.ap[:, 0:4096].rearrange("p (c n) -> p c n", c=16)
            s_mc = sl.ap[:, 4096:8192].rearrange("p (c n) -> p c n", c=16)
            s_ba = sl.ap[:, 8192:12288].rearrange("p (c n) -> p c n", c=16)
            s_bc = sl.ap[:, 12288:14336].rearrange("p (c n) -> p c n", c=8)
            cs_ = slice(g * 256, (g + 1) * 256)
            sl.slot = getattr(sl, "slot", None) or P.slot("ld_" + sl.res.name)
            ld("pool", s_ma, w_in_v[:, :, C_MA + g * 256:C_MA + (g + 1) * 256], sl)
            ld("pool", s_mc, w_in_v[:, :, C_MC + g * 256:C_MC + (g + 1) * 256], sl)
            ld("pool", s_ba, w_ba_v[:, :, cs_], sl)
            ld("pool", s_bc, w_bc_v[:, :, cs_], sl)
            for jj in range(2):
                j = g * 2 + jj
                js = slice(jj * 128, (jj + 1) * 128)
                for tb in range(2):
                    ts_ = slice(tb * 512, (tb + 1) * 512)
                    par = (j * 2 + tb) % 2
                    b_ma, b_mc, b_ya, b_yc = banks[par * 4], banks[par * 4 + 1], banks[par * 4 + 2], banks[par * 4 + 3]
                    for kc in range(16):
                        P.op("pe", lambda h, kc=kc, b_ma=b_ma, s_ma=s_ma, js=js, ts_=ts_: h.matmul(b_ma.ap, lhsT=s_ma[:, kc, js], rhs=hT_own.ap[:, kc, ts_],
                                                                                                start=(kc == 0), stop=(kc == 15)), reads=R(sl, hT_own), writes=R(b_ma))
                    for kc in range(16):
                        P.op("pe", lambda h, kc=kc, b_mc=b_mc, s_mc=s_mc, js=js, ts_=ts_: h.matmul(b_mc.ap, lhsT=s_mc[:, kc, js], rhs=hT_own.ap[:, kc, ts_],
                                                                                                start=(kc == 0), stop=(kc == 15)), reads=R(sl, hT_own), writes=R(b_mc))
                    for kc in range(16):
                        P.op("pe", lambda h, kc=kc, b_ya=b_ya, s_ba=s_ba, js=js, ts_=ts_: h.matmul(b_ya.ap, lhsT=s_ba[:, kc, js], rhs=AT.ap[:, kc, ts_],
                                                                                                start=(kc == 0), stop=(kc == 15)), reads=R(sl, AT), writes=R(b_ya))
                    for kc in range(8):
                        P.op("pe", lambda h, kc=kc, b_yc=b_yc, s_bc=s_bc, js=js, ts_=ts_: h.matmul(b_yc.ap, lhsT=s_bc[:, kc, js], rhs=cvg.ap[:, kc, ts_],
                                                                                                start=(kc == 0), stop=(kc == 7)), reads=R(sl, cvg), writes=R(b_yc))
                    ta, tcg, m1, m2 = Fs[par], Fs[2 + par], Fs[4 + par], Fs[6 + par]
                    P.op("act", lambda h, ta=ta, b_ma=b_ma: h.activation(out=ta.ap, in_=b_ma.ap, func=AF.Tanh, scale=0.5), reads=R(b_ma), writes=R(ta))
                    P.op("act", lambda h, tcg=tcg, b_mc=b_mc: h.activation(out=tcg.ap, in_=b_mc.ap, func=AF.Tanh, scale=0.5), reads=R(b_mc), writes=R(tcg))
                    P.op("dve", lambda h, ta=ta, m1=m1, b_ya=b_ya: h.scalar_tensor_tensor(out=m1.ap, in0=ta.ap, scalar=1.0, in1=b_ya.ap, op0=ALU.add, op1=ALU.mult),
                         reads=R(ta, b_ya), writes=R(m1))
                    P.op("dve", lambda h, tcg=tcg, m2=m2, b_yc=b_yc: h.scalar_tensor_tensor(out=m2.ap, in0=tcg.ap, scalar=1.0, in1=b_yc.ap, op0=ALU.add, op1=ALU.mult),
                         reads=R(tcg, b_yc), writes=R(m2))
                    P.op("dve", lambda h, m1=m1, m2=m2, j=j, ts_=ts_: h.tensor_tensor(out=merged.ap[:, j, ts_], in0=m1.ap, in1=m2.ap, op=ALU.add),
                         reads=R(m1, m2), writes=R(merged))

        if debug:
            dbg_outs["cvg"] = (cvg, [128, 8 * 1024], BF16)
            dbg_outs["merged"] = (merged, [128, 16 * 1024], BF16)
            dbg_outs["AT2"] = (AT, [128, 16 * 1024], BF16)
            dump_now(["cvg", "merged", "AT2"])
        maybe_stop("p2c")

        x2 = view(0, F32, [8, 2048], "x2")
        x2t = []
        for t in range(8):
            x2t.append(Buf(x2.ap[:, t, :], f"x2_{t}"))

        def res_deps(res):
            return ([res.w] if res.w is not None else []) + list(res.r.values()) + list(res.rd)
        S_ALL = [f_.res for f_ in Fs] + [ubuf.res]
        h2T = view(L0 + 73728, BF16, [16, 1024], "h2T")
        h2T.rl = [h2T.res, merged.res]
        pleg = view(L0 + 57344, F32, [2048], "pleg")
        wpp = view(196608, BF16, [2, 2048], "wpp")
        c3 = Carver(65536, 90112)

        def take_s(dt, shape, name):
            b_ = c3.take(dt, shape, name)
            b_.rl = [b_.res] + S_ALL
            return b_

        pT = take_s(BF16, [2, 1024], "pT")
        ptm = [take_s(F32, [256], f"ptm{i}") for i in range(2)]
        pbb = [take_s(BF16, [256], f"pbb{i}") for i in range(2)]
        h2b = [take_s(BF16, [2048], f"h2b{i}") for i in range(2)]
        tg = [take_s(F32, [512], f"tg{i}") for i in range(2)]
        e1 = [take_s(F32, [512], f"e1{i}") for i in range(2)]

        ld("pool", wpp.ap, w_pp_v, wpp)
        old_deps = res_deps(hT_own.res) + res_deps(AT.res)
        for t in range(8):
            ld("sp", x2t[t].ap, x_d[t * 128:(t + 1) * 128, :], x2t[t], deps=old_deps)
        ld("sp", pleg.ap, pleg_d, pleg, deps=res_deps(cvg.res))

        def norm2_stats(t):
            hb2 = h2b[t % 2]
            P.op("act", lambda h: h.activation(out=hb2.ap, in_=x2t[t].ap, func=AF.Square, accum_out=ss2.t[:, t:t + 1]),
                 reads=R(x2t[t]), writes=R(hb2, ss2))
            P.op("act", lambda h: h.activation(out=lnt.t[:, t:t + 1], in_=ss2.t[:, t:t + 1], func=AF.Ln, bias=col(COL_EPS), scale=1.0 / D),
                 reads=R(ss2, cols), writes=R(lnt))
            P.op("act", lambda h: h.activation(out=rs2.t[:, t:t + 1], in_=lnt.t[:, t:t + 1], func=AF.Exp, scale=-0.5), reads=R(lnt), writes=R(rs2))
            P.op("dve", lambda h: h.scalar_tensor_tensor(out=hb2.ap, in0=x2t[t].ap, scalar=rs2.t[:, t:t + 1], in1=pleg.ap, op0=ALU.mult, op1=ALU.mult),
                 reads=R(x2t[t], rs2, pleg), writes=R(hb2))
            pm, pb_ = ptm[t % 2], pbb[t % 2]
            ld("sp", pm.ap, p_d[t * 128:(t + 1) * 128, :], pm)
            P.op("dve", lambda h: h.tensor_copy(out=pb_.ap, in_=pm.ap), reads=R(pm), writes=R(pb_))

        def norm2_transposes(t):
            hb2, pb_ = h2b[t % 2], pbb[t % 2]
            bviews = [banks[4 + c // 8].bf[:, (c % 8) * 128:(c % 8 + 1) * 128] for c in range(16)]
            transposes(hb2.ap, 16, bviews, R(hb2), [banks[4 + c // 8] for c in range(16)])
            P.op("act", lambda h: h.copy(out=h2T.ap[:, 0:8, t * 128:(t + 1) * 128], in_=banks[4].bf.rearrange("p (a b) -> p a b", a=8)),
                 reads=R(banks[4]), writes=R(h2T))
            P.op("dve", lambda h: h.tensor_copy(out=h2T.ap[:, 8:16, t * 128:(t + 1) * 128], in_=banks[5].bf.rearrange("p (a b) -> p a b", a=8)),
                 reads=R(banks[5]), writes=R(h2T))
            transposes(pb_.ap, 2, [banks[6].bf[:, c * 128:(c + 1) * 128] for c in range(2)], R(pb_), [banks[6]] * 2)
            P.op("act", lambda h: h.copy(out=pT.ap[:, :, t * 128:(t + 1) * 128], in_=banks[6].bf[:, 0:256].rearrange("p (a b) -> p a b", a=2)),
                 reads=R(banks[6]), writes=R(pT))

        for n in range(4):
            sl = next_slab()
            w_ap = sl.ap[:, 0:8192].rearrange("p (c n) -> p c n", c=16)
            ns = slice(n * 512, (n + 1) * 512)
            ld("pool", w_ap, w_out_v[:, :, ns], sl)
            for t in range(8):
                bk = banks[(n * 8 + t) % 4]
                for kc in range(16):
                    P.op("pe", lambda h, kc=kc, bk=bk, w_ap=w_ap, t=t: h.matmul(bk.ap, lhsT=merged.ap[:, kc, t * 128:(t + 1) * 128], rhs=w_ap[:, kc, :],
                                                                            start=(kc == 0), stop=(kc == 15)), reads=R(merged, sl), writes=R(bk))
                P.op("dve", lambda h, bk=bk, t=t, ns=ns: h.scalar_tensor_tensor(out=x2t[t].ap[:, ns], in0=bk.ap, scalar=0.5, in1=x2t[t].ap[:, ns],
                                                                             op0=ALU.mult, op1=ALU.add), reads=R(bk, x2t[t]), writes=R(x2t[t]))
                if n == 3:
                    if t >= 2:
                        norm2_transposes(t - 2)
                    norm2_stats(t)
        norm2_transposes(6)
        norm2_transposes(7)
        so = P.slot("out")
        for n in range(4):
            sl = next_slab()
            w_ap = sl.ap[:, 0:8192].rearrange("p (c n) -> p c n", c=16)
            ns = slice(n * 512, (n + 1) * 512)
            ld("pool", w_ap, w_pg_v[:, :, ns], sl)
            for t in range(8):
                par = (n * 8 + t) % 2
                bg_, bp_ = banks[par * 2], banks[par * 2 + 1]
                for kc in range(16):
                    P.op("pe", lambda h, kc=kc, bg_=bg_, w_ap=w_ap, t=t: h.matmul(bg_.ap, lhsT=h2T.ap[:, kc, t * 128:(t + 1) * 128], rhs=w_ap[:, kc, :],
                                                                              start=(kc == 0), stop=(kc == 15)), reads=R(h2T, sl), writes=R(bg_))
                for kc in range(2):
                    P.op("pe", lambda h, kc=kc, bp_=bp_, t=t, ns=ns: h.matmul(bp_.ap, lhsT=pT.ap[:, kc, t * 128:(t + 1) * 128], rhs=wpp.ap[:, kc, ns],
                                                                          start=(kc == 0), stop=(kc == 1)), reads=R(pT, wpp), writes=R(bp_))
                tg_, e1_ = tg[par], e1[par]
                P.op("act", lambda h, tg_=tg_, bg_=bg_: h.activation(out=tg_.ap, in_=bg_.ap, func=AF.Tanh, scale=0.5), reads=R(bg_), writes=R(tg_))
                P.op("dve", lambda h, tg_=tg_, e1_=e1_, bp_=bp_: h.scalar_tensor_tensor(out=e1_.ap, in0=tg_.ap, scalar=1.0, in1=bp_.ap, op0=ALU.add, op1=ALU.mult),
                     reads=R(tg_, bp_), writes=R(e1_))
                P.op("dve", lambda h, e1_=e1_, t=t, ns=ns: h.scalar_tensor_tensor(out=x2t[t].ap[:, ns], in0=e1_.ap, scalar=0.5, in1=x2t[t].ap[:, ns],
                                                                               op0=ALU.mult, op1=ALU.add), reads=R(e1_, x2t[t]), writes=R(x2t[t]))
                if n == 3:
                    P.dma("sp", lambda h, t=t: h.dma_start(out=out_d[t * 128:(t + 1) * 128, :], in_=x2t[t].ap), so, reads=R(x2t[t]))
        fin_deps = [o for o in P.ops["sp"] if o.slot is so][-1:]
        if dbg_names:
            fin_deps += [o for o in P.ops["sp"] if o.slot is dq["sp"]][-1:]
        P.op("sp", lambda h: h.nop(), deps=fin_deps, real=False)

        P.finalize()
        print("ops", {e: len(P.ops[e]) for e in P.ENGS}, "signals", {e: sum(1 for o in P.ops[e] if o.signal) for e in P.ENGS}, "slots", len(P.slots))
        for s_ in P.slots:
            s_.sem = es.enter_context(nc.semaphore("d_" + s_.name))
        with nc.Block() as block:
            @block.tensor
            def _(h):
                P.emit("pe", h, sems)

            @block.scalar
            def _(h):
                P.emit("act", h, sems)

            @block.vector
            def _(h):
                P.emit("dve", h, sems)

            @block.gpsimd
            def _(h):
                P.emit("pool", h, sems)

            @block.sync
            def _(h):
                P.emit("sp", h, sems)
    nc._dbg_names = dbg_names
    return nc


def _host_inputs(x, p, positions, norm_g, w_in, q_lat_g, kv_lat_g, w_uq, w_ukv, q_norm_g, k_norm_g,
                 conv_w, conv_b, w_branch_attn, w_branch_conv, w_out, ple_norm_g, w_ple_gate, w_ple_proj):
    f32 = np.float32
    x = np.asarray(x, f32)
    p = np.asarray(p, f32)[0]
    positions = np.asarray(positions, np.int32)
    qg = np.asarray(q_norm_g, f32)[0]
    kg = np.asarray(k_norm_g, f32)[0]
    cw = np.asarray(conv_w, f32)[0]
    cb = np.asarray(conv_b, f32)[0]
    inv_freq = (1.0 / (10000.0 ** (np.arange(0, 64, 2, dtype=np.float32) / 64.0))).astype(f32)

    def rep(v):
        return np.ascontiguousarray(np.broadcast_to(np.asarray(v, f32)[None, :], (128, len(v))))

    shared = {
        "rows_a": np.ascontiguousarray(np.concatenate([rep(np.asarray(q_lat_g, f32)[0]), rep(np.asarray(kv_lat_g, f32)[0]), rep(kg[128:192])], axis=1)),
        "normg_row": rep(np.asarray(norm_g, f32)[0]),
        "pleg_row": rep(np.asarray(ple_norm_g, f32)[0]),
        "invf4": rep(np.tile(inv_freq, 4)),
        "ph_row": rep(np.concatenate([np.full(64, np.pi / 2, f32), np.zeros(64, f32)])),
        "w_in": np.ascontiguousarray(np.asarray(w_in, f32)[0]),
        "w_uq": np.ascontiguousarray(np.asarray(w_uq, f32)[0]),
        "w_ukv": np.ascontiguousarray(np.asarray(w_ukv, f32)[0]),
        "w_ba": np.ascontiguousarray(np.asarray(w_branch_attn, f32)[0]),
        "w_bc": np.ascontiguousarray(np.asarray(w_branch_conv, f32)[0]),
        "w_out": np.ascontiguousarray(np.asarray(w_out, f32)[0]),
        "w_pg": np.ascontiguousarray(np.asarray(w_ple_gate, f32)[0]),
        "w_pp": np.ascontiguousarray(np.asarray(w_ple_proj, f32)[0]),
    }
    cols0 = np.zeros((128, NCOL), f32)
    cols0[:, COL_GQN] = qg[0:128]
    cols0[:, COL_GQAUG] = np.concatenate([qg[128:192], qg[160:192], qg[128:160]])
    cols0[:, COL_GKN] = kg[0:128]
    cols0[:, COL_INVF] = np.tile(inv_freq, 4)
    cols0[:, COL_PH] = np.concatenate([np.full(64, np.pi / 2, f32), np.zeros(64, f32)])
    cols0[:, COL_EPS] = EPS
    cols0[:, COL_LNS] = -0.5 * math.log(192.0)
    for c in range(8):
        for j in range(3):
            cols0[:, COL_CW + 3 * c + j] = cw[j, c * 128:(c + 1) * 128]
        cols0[:, COL_CBIAS + c] = cb[c * 128:(c + 1) * 128]
    in_maps = []
    for core in range(8):
        b, hq = core // 2, core % 2
        own = slice(hq * T, (hq + 1) * T)
        oth = slice((1 - hq) * T, (2 - hq) * T)
        cols = cols0.copy()
        cols[:, COL_ML] = 1.0 if hq == 1 else 0.0
        cols[:, COL_MR] = 1.0 if hq == 0 else 0.0
        pos_loc = np.concatenate([positions[b, own], positions[b, oth]])
        m = dict(shared)
        m["x_loc"] = np.ascontiguousarray(np.concatenate([x[b, own], x[b, oth]], axis=0))
        m["p_loc"] = np.ascontiguousarray(p[b, own])
        m["pos_tm"] = np.ascontiguousarray(pos_loc.reshape(16, 128).T)
        m["pos_row"] = np.ascontiguousarray(np.broadcast_to(positions[b, own][None, :], (128, T)))
        m["cols"] = cols
        in_maps.append(m)
    return in_maps


_NC_CACHE = {}


def kernel(**inputs):
    in_maps = _host_inputs(**inputs)
    if "nc" not in _NC_CACHE:
        _NC_CACHE["nc"] = build_nc(debug=False)
    nc = _NC_CACHE["nc"]
    res = run_bass_kernel_spmd(nc, in_maps, core_ids=list(range(8)))
    out = np.empty((4, S, D), np.float32)
    for core in range(8):
        b, hq = core // 2, core % 2
        out[b, hq * T:(hq + 1) * T, :] = res.results[core]["out"]
    return out
```

```python
import math
from contextlib import ExitStack

import numpy as np
import concourse.bass as bass
import concourse.mybir as mybir
from concourse.bass_utils import run_bass_kernel_spmd

F32 = mybir.dt.float32
BF16 = mybir.dt.bfloat16
I32 = mybir.dt.int32
ALU = mybir.AluOpType
AF = mybir.ActivationFunctionType

D = 2048
S = 2048
T = 1024
NH = 16
EPS = 1e-6
IN_W = 11072
C_GA, C_CB, C_CC, C_CX, C_GC, C_MA, C_MC = 832, 2880, 3904, 4928, 5952, 6976, 9024
NCOL = 48
COL_GQN, COL_GQAUG, COL_GKN, COL_INVF, COL_PH, COL_ML, COL_MR, COL_EPS, COL_LNS, COL_CW, COL_CBIAS = 0, 1, 2, 3, 4, 5, 6, 7, 8, 9, 33

MAGIC = 12582912.0
INV2PI = 0.15915494309189535
CW1 = 6.28125
CW2 = 2.0 * math.pi - 6.28125
PI_LO = 3.1415925


class Res:
    def __init__(self, name, excl=False):
        self.name = name
        self.w = None
        self.r = {}
        self.rd = []
        self.excl = excl


class Slot:
    def __init__(self, name):
        self.name = name
        self.total = 0
        self.sem = None


class Op:
    __slots__ = ("eng", "fn", "deps", "signal", "count", "slot", "val")

    def __init__(self, eng, fn, deps, slot=None):
        self.eng = eng
        self.fn = fn
        self.deps = deps
        self.signal = False
        self.count = None
        self.slot = slot
        self.val = None


class Prog:
    ENGS = ("pe", "act", "dve", "pool", "sp")

    def __init__(self):
        self.ops = {e: [] for e in self.ENGS}
        self.slots = []
        self.last_real = {}
        self.bar = []

    def slot(self, name):
        s = Slot(name)
        self.slots.append(s)
        return s

    def _deps(self, eng, reads, writes, deps):
        d = [x for x in deps if x is not None]
        for r in reads:
            if r.w is not None:
                d.append(r.w)
            if r.excl:
                d.extend(o for e, o in r.r.items() if e != eng)
        for w in writes:
            if w.w is not None:
                d.append(w.w)
            d.extend(w.r.values())
            d.extend(w.rd)
        return d

    def _track(self, o, reads, writes):
        for r in reads:
            if o.slot is not None:
                r.rd.append(o)
            else:
                r.r[o.eng] = o
        for w in writes:
            w.w = o
            w.r = {}
            w.rd = []

    def op(self, eng, fn, reads=(), writes=(), deps=(), real=True):
        o = Op(eng, fn, self._deps(eng, reads, writes, deps))
        self._track(o, reads, writes)
        self.ops[eng].append(o)
        if real:
            self.last_real[eng] = o
        return o

    def dma(self, eng, fn, slot, reads=(), writes=(), deps=()):
        dl = [d for d in self._deps(eng, reads, writes, list(deps) + self.bar) if d.slot is not slot]
        o = Op(eng, fn, dl, slot=slot)
        slot.total += 16
        o.val = slot.total
        self._track(o, reads, writes)
        self.ops[eng].append(o)
        return o

    def barrier(self):
        last = [self.last_real[e] for e in ("pe", "act", "dve", "pool") if e in self.last_real]
        for e in self.ENGS:
            self.op(e, lambda h: h.nop(), deps=last, real=False)
        self.bar = last

    def finalize(self):
        for e in self.ENGS:
            for o in self.ops[e]:
                for d in o.deps:
                    if d.slot is None and not (d.eng == "pe" and e == "pe"):
                        d.signal = True
        for e in self.ENGS:
            c = 0
            for o in self.ops[e]:
                if o.slot is None and o.signal:
                    c += 1
                    o.count = c

    def emit(self, eng, h, sems):
        waited = {}
        cut = getattr(self, "cut", None)
        ops = self.ops[eng] if cut is None else self.ops[eng][:cut[eng]]
        for o in ops:
            for d in o.deps:
                if d.slot is not None:
                    key = id(d.slot)
                    if waited.get(key, 0) < d.val:
                        h.wait_ge(d.slot.sem, d.val)
                        waited[key] = d.val
                else:
                    if d.eng == "pe" and eng == "pe":
                        continue
                    if waited.get(d.eng, 0) < d.count:
                        h.wait_ge(sems[d.eng], d.count)
                        waited[d.eng] = d.count
            inst = o.fn(h)
            if o.slot is not None:
                inst.then_inc(o.slot.sem, 16)
            elif o.signal:
                inst.then_inc(sems[eng], 1)


class Buf:
    def __init__(self, ap, name):
        self.ap = ap
        self.res = Res(name)
        self.rl = None


def build_nc(debug=False, stop=None):
    nc = bass.Bass("TRN2", target_bir_lowering=False)
    P = Prog()

    def din(name, shape, dt=F32):
        return nc.dram_tensor(name, list(shape), dt, kind="ExternalInput").ap()

    x_d = din("x_loc", [S, D])
    p_d = din("p_loc", [T, 256])
    postm_d = din("pos_tm", [128, 16], I32)
    posrow_d = din("pos_row", [128, T], I32)
    cols_d = din("cols", [128, NCOL])
    rowsa_d = din("rows_a", [128, 832])
    normg_d = din("normg_row", [128, D])
    pleg_d = din("pleg_row", [128, D])
    invf4_d = din("invf4", [128, 128])
    phrow_d = din("ph_row", [128, 128])
    w_in_d = din("w_in", [D, IN_W])
    w_uq_d = din("w_uq", [512, 3072])
    w_ukv_d = din("w_ukv", [256, 4096])
    w_ba_d = din("w_ba", [D, D])
    w_bc_d = din("w_bc", [1024, D])
    w_out_d = din("w_out", [D, D])
    w_pg_d = din("w_pg", [D, D])
    w_pp_d = din("w_pp", [256, D])
    out_d = nc.dram_tensor("out", [T, D], F32, kind="ExternalOutput").ap()
    dbg_outs = {}

    w_in_v = w_in_d.rearrange("(c p) n -> p c n", p=128)
    w_uq_v = w_uq_d.rearrange("(c p) n -> p c n", p=128)
    w_ukv_v = w_ukv_d.rearrange("(c p) n -> p c n", p=128)
    w_ba_v = w_ba_d.rearrange("(c p) n -> p c n", p=128)
    w_bc_v = w_bc_d.rearrange("(c p) n -> p c n", p=128)
    w_out_v = w_out_d.rearrange("(c p) n -> p c n", p=128)
    w_pg_v = w_pg_d.rearrange("(c p) n -> p c n", p=128)
    w_pp_v = w_pp_d.rearrange("(c p) n -> p c n", p=128)

    with ExitStack() as es:
        ARENA_BYTES = 204800
        arena = es.enter_context(nc.sbuf_tensor("arena", [128, ARENA_BYTES // 4], F32))

        def view(off, dt, shape, name):
            n = 1
            for s_ in shape:
                n *= s_
            el = 2 if dt == BF16 else 4
            assert off % 4 == 0 and (n * el) % 4 == 0, (name, off, n)
            assert off + n * el <= ARENA_BYTES, (name, off, n * el)
            ap = arena[:, off // 4:(off + n * el) // 4]
            if dt != F32:
                ap = ap.bitcast(dt)
            if len(shape) == 2:
                ap = ap.rearrange("p (a b) -> p a b", a=shape[0])
            elif len(shape) == 3:
                ap = ap.rearrange("p (a b c) -> p a b c", a=shape[0], b=shape[1])
            return Buf(ap, name)

        class Carver:
            def __init__(self, start, end):
                self.off = start
                self.end = end

            def take(self, dt, shape, name):
                n = 1
                for s_ in shape:
                    n *= s_
                nb = n * (2 if dt == BF16 else 4)
                nb_al = (nb + 31) // 32 * 32
                b = view(self.off, dt, shape, name)
                self.off += nb_al
                assert self.off <= self.end, (name, self.off, self.end)
                return b

        def sb(name, shape, dt=F32):
            t = es.enter_context(nc.sbuf_tensor("sb_" + name, list(shape), dt))
            b = Buf(None, name)
            b.t = t
            return b

        cols = sb("cols", [128, NCOL])
        ident = sb("ident", [128, 128], BF16)
        identf = sb("identf", [128, 128])
        ones = sb("ones", [128, 128], BF16)
        ss = sb("ss", [128, 16])
        rs = sb("rs", [128, 16])
        ssq = sb("ssq", [128, 16])
        rq = sb("rq", [128, 16])
        sskv = sb("sskv", [128, 16])
        rkv = sb("rkv", [128, 16])
        sspe = sb("sspe", [128, 16])
        lnt = sb("lnt", [128, 16])
        skt = [sb(f"skt{i}", [128, 16]) for i in range(2)]
        sk = [sb(f"sk{i}", [128, 16]) for i in range(2)]
        ss2 = sb("ss2", [128, 8])
        rs2 = sb("rs2", [128, 8])
        posf = sb("posf", [128, 16])
        postm = sb("postm", [128, 16], I32)
        hT_halo = sb("hT_halo", [128, 16, 2], BF16)
        hcx = sb("hcx", [128, 4])

        def col(i):
            return cols.t[:, i:i + 1]

        banks = []
        for i in range(8):
            t = es.enter_context(nc.psum_tensor(f"bank{i}", [128, 512], F32))
            b = Buf(t[:], f"bank{i}")
            b.res.excl = True
            b.bf = t[:].bitcast(BF16)
            banks.append(b)

        sems = {e: es.enter_context(nc.semaphore(f"s_{e}")) for e in Prog.ENGS}

        hT_own = view(0, BF16, [16, 1024], "hT_own")
        AT = view(32768, BF16, [16, 1024], "AT")
        qaT = view(65536, BF16, [4, 1024], "qaT")
        kvaT = view(73728, BF16, [2, 2048], "kvaT")
        KrT = view(81920, BF16, [2048], "KrT")
        CS_fm = view(86016, F32, [1024], "CS_fm")
        L0 = 90112

        dq = {"sp": P.slot("sp_misc")}

        def maybe_stop(tag):
            if stop == tag and getattr(P, "cut", None) is None:
                P.barrier()
                fd = [o for o in P.ops["sp"] if o.slot is dq["sp"]][-1:]
                P.op("sp", lambda h: h.nop(), deps=fd + P.bar, real=False)
                P.cut = {e: len(P.ops[e]) for e in P.ENGS}
                print("STOP at", tag, {e: len(P.ops[e]) for e in P.ENGS})

        def ld(eng, out_ap, in_ap, wbuf, slot=None, reads=(), deps=()):
            if slot is None:
                if not hasattr(wbuf, "slot"):
                    wbuf.slot = P.slot("ld_" + wbuf.res.name)
                slot = wbuf.slot
            return P.dma(eng, lambda h: h.dma_start(out=out_ap, in_=in_ap), slot,
                         reads=R(*reads), writes=R(wbuf), deps=deps)

        def R(*bufs):
            out_ = []
            for b in bufs:
                out_.extend(b.rl if b.rl is not None else [b.res])
            return out_

        ld("sp", cols.t[:], cols_d, cols)
        ld("sp", postm.t[:], postm_d, postm)

        P.op("dve", lambda h: h.memset(identf.t[:], 1.0), writes=R(identf))
        P.op("pool", lambda h: h.affine_select(out=identf.t[:], in_=identf.t[:], pattern=[[-1, 128]],
                                               compare_op=ALU.is_equal, fill=0.0, base=0, channel_multiplier=1),
             reads=R(identf), writes=R(identf))
        P.op("dve", lambda h: h.tensor_copy(out=ident.t[:], in_=identf.t[:]), reads=R(identf), writes=R(ident))
        P.op("dve", lambda h: h.memset(ones.t[:], 1.0), writes=R(ones))
        for b_ in (ss, ssq, sskv, sspe, ss2):
            P.op("dve", lambda h, b_=b_: h.memset(b_.t[:], 0.0), writes=R(b_))

        W_BASE = 147456
        c0 = Carver(L0, W_BASE)
        xs = [c0.take(F32, [2048], f"xs{i}") for i in range(2)]
        hb = [c0.take(BF16, [2048], f"hb{i}") for i in range(2)]
        hTt = [c0.take(BF16, [16, 128], f"hTt{i}") for i in range(2)]
        normg = c0.take(F32, [2048], "normg")
        CS_tm = c0.take(F32, [16, 128], "CS_tm")
        rows_a = c0.take(F32, [832], "rows_a")
        qa_n = c0.take(BF16, [512], "qa_n")
        kva_n = c0.take(BF16, [256], "kva_n")
        kpe_g = c0.take(F32, [64], "kpe_g")
        tcb = c0.take(F32, [64], "tcb")
        tsb = c0.take(F32, [64], "tsb")
        Kr_tm = c0.take(BF16, [128], "Kr_tm")
        c0 = Carver(W_BASE, ARENA_BYTES)
        trA = c0.take(F32, [2048], "trA")
        trB = c0.take(F32, [2048], "trB")
        trC = c0.take(F32, [2048], "trC")
        posrow = c0.take(I32, [1024], "posrow")
        invf4 = c0.take(F32, [128], "invf4")
        phrow = c0.take(F32, [128], "phrow")
        Wlat = view(32768, BF16, [16, 832], "Wlat")
        junk = view(32768 + 26624, BF16, [2048], "junk")

        ld("sp", rows_a.ap, rowsa_d, rows_a)
        ld("sp", normg.ap, normg_d, normg)
        ld("sp", invf4.ap, invf4_d, invf4)
        ld("sp", phrow.ap, phrow_d, phrow)
        ld("sp", posrow.ap, posrow_d, posrow)
        ld("pool", Wlat.ap, w_in_v[:, :, 0:832], Wlat)

        def range_reduce(ang, kf, n):
            a, k = ang.ap[:, 0:n], kf.ap[:, 0:n]
            P.op("dve", lambda h: h.tensor_scalar(out=k, in0=a, scalar1=INV2PI, scalar2=MAGIC, op0=ALU.mult, op1=ALU.add),
                 reads=R(ang), writes=R(kf))
            P.op("dve", lambda h: h.tensor_scalar(out=k, in0=k, scalar1=-MAGIC, scalar2=None, op0=ALU.add),
                 reads=R(kf), writes=R(kf))
            P.op("dve", lambda h: h.scalar_tensor_tensor(out=a, in0=k, scalar=-CW1, in1=a, op0=ALU.mult, op1=ALU.add),
                 reads=R(kf, ang), writes=R(ang))
            P.op("dve", lambda h: h.scalar_tensor_tensor(out=a, in0=k, scalar=-CW2, in1=a, op0=ALU.mult, op1=ALU.add),
                 reads=R(kf, ang), writes=R(ang))

        def wrap_clamp(ang, m, n):
            a, k = ang.ap[:, 0:n], m.ap[:, 0:n]
            P.op("dve", lambda h: h.tensor_scalar(out=k, in0=a, scalar1=math.pi, scalar2=-2.0 * math.pi, op0=ALU.is_gt, op1=ALU.mult),
                 reads=R(ang), writes=R(m))
            P.op("dve", lambda h: h.tensor_tensor(out=a, in0=a, in1=k, op=ALU.add), reads=R(ang, m), writes=R(ang))
            P.op("dve", lambda h: h.tensor_scalar(out=a, in0=a, scalar1=PI_LO, scalar2=-PI_LO, op0=ALU.min, op1=ALU.max),
                 reads=R(ang), writes=R(ang))

        P.op("dve", lambda h: h.tensor_copy(out=trA.ap[:, 0:1024], in_=posrow.ap), reads=R(posrow), writes=R(trA))
        P.op("dve", lambda h: h.tensor_scalar(out=trA.ap[:, 0:1024], in0=trA.ap[:, 0:1024], scalar1=col(COL_INVF), scalar2=None, op0=ALU.mult),
             reads=R(trA, cols), writes=R(trA))
        range_reduce(trA, trB, 1024)
        P.op("dve", lambda h: h.tensor_scalar(out=trA.ap[:, 0:1024], in0=trA.ap[:, 0:1024], scalar1=col(COL_PH), scalar2=None, op0=ALU.add),
             reads=R(trA, cols), writes=R(trA))
        wrap_clamp(trA, trB, 1024)
        P.op("act", lambda h: h.activation(out=CS_fm.ap, in_=trA.ap[:, 0:1024], func=AF.Sin), reads=R(trA), writes=R(CS_fm))

        P.op("dve", lambda h: h.tensor_copy(out=posf.t[:], in_=postm.t[:]), reads=R(postm), writes=R(posf))
        for t in range(16):
            P.op("dve", lambda h, t=t: h.tensor_scalar(out=trC.ap[:, t * 128:(t + 1) * 128], in0=invf4.ap, scalar1=posf.t[:, t:t + 1],
                                                      scalar2=None, op0=ALU.mult), reads=R(invf4, posf), writes=R(trC))
        range_reduce(trC, trB, 2048)
        for t in range(16):
            P.op("dve", lambda h, t=t: h.tensor_tensor(out=trC.ap[:, t * 128:(t + 1) * 128], in0=trC.ap[:, t * 128:(t + 1) * 128],
                                                      in1=phrow.ap, op=ALU.add), reads=R(trC, phrow), writes=R(trC))
        wrap_clamp(trC, trB, 2048)
        P.op("act", lambda h: h.activation(out=CS_tm.ap.rearrange("p a b -> p (a b)"), in_=trC.ap, func=AF.Sin),
             reads=R(trC), writes=R(CS_tm))

        trig_done = [P.last_real["dve"], P.last_real["act"]]
        wukv = view(W_BASE, BF16, [2, 4096], "wukv")
        wuq = view(W_BASE + 16384, BF16, [4, 3072], "wuq")
        Bmat = view(W_BASE + 40960, BF16, [4, 16, 128], "Bmat")
        ld("pool", wukv.ap, w_ukv_v, wukv, deps=trig_done)
        ld("pool", wuq.ap, w_uq_v, wuq, deps=trig_done)

        def transposes(src_ap, nblk, bank_views, reads, bank_bufs):
            for c in range(nblk):
                bv = bank_views[c]
                P.op("pe", lambda h, c=c, bv=bv: h.transpose(out=bv, in_=src_ap[:, c * 128:(c + 1) * 128], identity=ident.t[:]),
                     reads=reads + R(ident), writes=[bank_bufs[c].res])

        maybe_stop("p0")
        tile_dst = {}

        def stA(t):
            par = t % 2
            xb_, hb_ = xs[par], hb[par]
            ld("sp", xb_.ap, x_d[t * 128:(t + 1) * 128, :], xb_)
            P.op("act", lambda h: h.activation(out=junk.ap, in_=xb_.ap, func=AF.Square, accum_out=ss.t[:, t:t + 1]),
                 reads=R(xb_), writes=R(junk, ss))
            P.op("act", lambda h: h.activation(out=lnt.t[:, 0:1], in_=ss.t[:, t:t + 1], func=AF.Ln, bias=col(COL_EPS), scale=1.0 / D),
                 reads=R(ss, cols), writes=R(lnt))
            P.op("act", lambda h: h.activation(out=rs.t[:, t:t + 1], in_=lnt.t[:, 0:1], func=AF.Exp, scale=-0.5),
                 reads=R(lnt), writes=R(rs))
            P.op("dve", lambda h: h.scalar_tensor_tensor(out=hb_.ap, in0=xb_.ap, scalar=rs.t[:, t:t + 1], in1=normg.ap,
                                                         op0=ALU.mult, op1=ALU.mult),
                 reads=R(xb_, rs, normg), writes=R(hb_))

        def stB(t):
            own = t < 8
            par = t % 2
            hb_ = hb[par]
            bx0, bx1 = banks[0], banks[1]
            bb = [bx0, bx1]
            bviews = [bb[c // 8].bf[:, (c % 8) * 128:(c % 8 + 1) * 128] for c in range(16)]
            transposes(hb_.ap, 16, bviews, R(hb_), [bb[c // 8] for c in range(16)])
            if own:
                dst = hT_own
                d0 = hT_own.ap[:, 0:8, t * 128:(t + 1) * 128]
                d1 = hT_own.ap[:, 8:16, t * 128:(t + 1) * 128]
            else:
                dst = hTt[par]
                d0 = dst.ap[:, 0:8, :]
                d1 = dst.ap[:, 8:16, :]
            tile_dst[t] = dst
            P.op("act", lambda h: h.copy(out=d0, in_=bx0.bf.rearrange("p (a b) -> p a b", a=8)),
                 reads=R(bx0), writes=R(dst))
            P.op("dve", lambda h: h.tensor_copy(out=d1, in_=bx1.bf.rearrange("p (a b) -> p a b", a=8)),
                 reads=R(bx1), writes=R(dst))
            if t == 8:
                P.op("dve", lambda h: h.tensor_scalar(out=hT_halo.t[:, :, 1:2], in0=dst.ap[:, :, 0:1], scalar1=col(COL_MR),
                                                      scalar2=None, op0=ALU.mult), reads=R(dst, cols), writes=R(hT_halo))
            if t == 15:
                P.op("dve", lambda h: h.tensor_scalar(out=hT_halo.t[:, :, 0:1], in0=dst.ap[:, :, 127:128], scalar1=col(COL_ML),
                                                      scalar2=None, op0=ALU.mult), reads=R(dst, cols), writes=R(hT_halo))

        def zbanks(t):
            par = t % 2
            return banks[2 + 2 * par], banks[3 + 2 * par]

        def stC(t):
            own = t < 8
            dst = tile_dst[t]
            bq, bk = zbanks(t)

            def hsrc(kc):
                if own:
                    return dst.ap[:, kc, t * 128:(t + 1) * 128]
                return dst.ap[:, kc, :]

            if own:
                for kc in range(16):
                    P.op("pe", lambda h, kc=kc: h.matmul(bq.ap, lhsT=hsrc(kc), rhs=Wlat.ap[:, kc, 0:512],
                                                        start=(kc == 0), stop=(kc == 15)),
                         reads=R(dst, Wlat), writes=R(bq))
            for kc in range(16):
                P.op("pe", lambda h, kc=kc: h.matmul(bk.ap[:, 0:320], lhsT=hsrc(kc), rhs=Wlat.ap[:, kc, 512:832],
                                                    start=(kc == 0), stop=(kc == 15)),
                     reads=R(dst, Wlat), writes=R(bk))

        qa_n2 = [qa_n, Buf(junk.ap[:, 512:1024], "qa_n_b")]

        def stD(t):
            own = t < 8
            bq, bk = zbanks(t)
            if own:
                P.op("act", lambda h: h.activation(out=junk.ap[:, 0:512], in_=bq.ap, func=AF.Square, accum_out=ssq.t[:, t:t + 1]),
                     reads=R(bq), writes=R(junk, ssq))
            P.op("act", lambda h: h.activation(out=junk.ap[:, 0:256], in_=bk.ap[:, 0:256], func=AF.Square, accum_out=sskv.t[:, t:t + 1]),
                 reads=R(bk), writes=R(junk, sskv))
            P.op("act", lambda h: h.activation(out=junk.ap[:, 0:64], in_=bk.ap[:, 256:320], func=AF.Square, accum_out=sspe.t[:, t:t + 1]),
                 reads=R(bk), writes=R(junk, sspe))
            if own:
                P.op("act", lambda h: h.activation(out=lnt.t[:, 1:2], in_=ssq.t[:, t:t + 1], func=AF.Ln, bias=col(COL_EPS), scale=1.0 / 512),
                     reads=R(ssq, cols), writes=R(lnt))
                P.op("act", lambda h: h.activation(out=rq.t[:, t:t + 1], in_=lnt.t[:, 1:2], func=AF.Exp, scale=-0.5),
                     reads=R(lnt), writes=R(rq))
            P.op("act", lambda h: h.activation(out=lnt.t[:, 2:3], in_=sskv.t[:, t:t + 1], func=AF.Ln, bias=col(COL_EPS), scale=1.0 / 256),
                 reads=R(sskv, cols), writes=R(lnt))
            P.op("act", lambda h: h.activation(out=rkv.t[:, t:t + 1], in_=lnt.t[:, 2:3], func=AF.Exp, scale=-0.5),
                 reads=R(lnt), writes=R(rkv))
            if own:
                P.op("dve", lambda h: h.scalar_tensor_tensor(out=qa_n.ap, in0=bq.ap, scalar=rq.t[:, t:t + 1], in1=rows_a.ap[:, 0:512],
                                                             op0=ALU.mult, op1=ALU.mult), reads=R(bq, rq, rows_a), writes=R(qa_n))
            P.op("dve", lambda h: h.scalar_tensor_tensor(out=kva_n.ap, in0=bk.ap[:, 0:256], scalar=rkv.t[:, t:t + 1],
                                                         in1=rows_a.ap[:, 512:768], op0=ALU.mult, op1=ALU.mult),
                 reads=R(bk, rkv, rows_a), writes=R(kva_n))
            P.op("dve", lambda h: h.tensor_tensor(out=kpe_g.ap, in0=bk.ap[:, 256:320], in1=rows_a.ap[:, 768:832], op=ALU.mult),
                 reads=R(bk, rows_a), writes=R(kpe_g))
            P.op("dve", lambda h: h.tensor_tensor(out=tcb.ap, in0=kpe_g.ap, in1=CS_tm.ap[:, t, 0:64], op=ALU.mult),
                 reads=R(kpe_g, CS_tm), writes=R(tcb))
            P.op("dve", lambda h: h.tensor_tensor(out=tsb.ap, in0=kpe_g.ap, in1=CS_tm.ap[:, t, 64:128], op=ALU.mult),
                 reads=R(kpe_g, CS_tm), writes=R(tsb))
            for rep in range(2):
                o_ = rep * 64
                P.op("dve", lambda h, o_=o_: h.tensor_tensor(out=Kr_tm.ap[:, o_:o_ + 32], in0=tcb.ap[:, 0:32], in1=tsb.ap[:, 32:64], op=ALU.subtract),
                     reads=R(tcb, tsb), writes=R(Kr_tm))
                P.op("dve", lambda h, o_=o_: h.tensor_tensor(out=Kr_tm.ap[:, o_ + 32:o_ + 64], in0=tcb.ap[:, 32:64], in1=tsb.ap[:, 0:32], op=ALU.add),
                     reads=R(tcb, tsb), writes=R(Kr_tm))

        def stE(t):
            own = t < 8
            par = t % 2
            b4 = banks[6 + par]
            if own:
                transposes(qa_n.ap, 4, [b4.bf[:, c * 128:(c + 1) * 128] for c in range(4)], R(qa_n), [b4] * 4)
            transposes(kva_n.ap, 2, [b4.bf[:, (4 + c) * 128:(5 + c) * 128] for c in range(2)], R(kva_n), [b4] * 2)
            transposes(Kr_tm.ap, 1, [b4.bf[:, 768:896]], R(Kr_tm), [b4])
            if own:
                P.op("act", lambda h: h.copy(out=qaT.ap[:, :, t * 128:(t + 1) * 128], in_=b4.bf[:, 0:512].rearrange("p (a b) -> p a b", a=4)),
                     reads=R(b4), writes=R(qaT))
            P.op("dve", lambda h: h.tensor_copy(out=kvaT.ap[:, :, t * 128:(t + 1) * 128], in_=b4.bf[:, 512:768].rearrange("p (a b) -> p a b", a=2)),
                 reads=R(b4), writes=R(kvaT))
            P.op("dve", lambda h: h.tensor_copy(out=KrT.ap[:, t * 128:(t + 1) * 128], in_=b4.bf[:, 768:896]),
                 reads=R(b4), writes=R(KrT))

        stages_1a = [stA, stB, stC, stD, stE]
        for s_ in range(16 + len(stages_1a) - 1):
            for k_ in reversed(range(len(stages_1a))):
                t_ = s_ - k_
                if 0 <= t_ < 16:
                    stages_1a[k_](t_)

        if debug:
            dbg_outs["hT_own"] = (hT_own, [128, 16 * 1024], BF16)
            dbg_outs["qaT"] = (qaT, [128, 4 * 1024], BF16)
            dbg_outs["kvaT"] = (kvaT, [128, 2 * 2048], BF16)
            dbg_outs["KrT"] = (KrT, [128, 2048], BF16)
            dbg_outs["CS_fm"] = (CS_fm, [128, 1024], F32)

        def dump_now(names):
            P.barrier()
            for nm in names:
                buf, shape, dt = dbg_outs.pop(nm)
                dd = nc.dram_tensor("dbg_" + nm, shape, dt, kind="ExternalOutput").ap()
                src = buf.ap
                if len(src.shape) == 3:
                    src = src.rearrange("p a b -> p (a b)")
                P.dma("sp", lambda h, dd=dd, src=src: h.dma_start(out=dd, in_=src), dq["sp"], reads=R(buf))
                dbg_names.append("dbg_" + nm)

        dbg_names = []

        if debug:
            dump_now(["hT_own", "qaT", "kvaT", "KrT", "CS_fm"])
        maybe_stop("p1a")

        P.barrier()
        c1 = Carver(L0, W_BASE)
        KnT = [c1.take(BF16, [2048], f"KnT{i}") for i in range(2)]
        Vsb = [c1.take(BF16, [16, 128], f"V{i}") for i in range(2)]
        QnT = [c1.take(BF16, [1024], f"QnT{i}") for i in range(2)]
        QrT = [c1.take(BF16, [1024], f"QrT{i}") for i in range(2)]
        PT = [c1.take(BF16, [512], f"PT{i}") for i in range(4)]
        sqA = [c1.take(BF16, [512], f"sqA{i}") for i in range(2)]
        sqB = [c1.take(BF16, [512], f"sqB{i}") for i in range(2)]
        sqK = [c1.take(BF16, [512], f"sqK{i}") for i in range(2)]
        lnv = [c1.take(F32, [512], f"lnv{i}") for i in range(2)]
        fq = [c1.take(F32, [512], f"fq{i}") for i in range(2)]
        tmpB = [c1.take(F32, [512], f"tmpB{i}") for i in range(2)]
        rl = [c1.take(F32, [512], f"rl{i}") for i in range(2)]
        acc = [c1.take(F32, [512], f"acc{i}") for i in range(2)]
        onesf = identf
        P.op("dve", lambda h: h.memset(onesf.t[:], 1.0), reads=R(ident), writes=R(onesf))

        wuq4 = wuq.ap.rearrange("p c (h d) -> p c h d", h=16)
        P.op("dve", lambda h: h.tensor_copy(out=Bmat.ap[:, :, :, 0:64], in_=wuq4[:, :, :, 128:192]), reads=R(wuq), writes=R(Bmat))
        P.op("dve", lambda h: h.tensor_scalar(out=Bmat.ap[:, :, :, 64:96], in0=wuq4[:, :, :, 160:192], scalar1=-1.0, scalar2=None, op0=ALU.mult),
             reads=R(wuq), writes=R(Bmat))
        P.op("dve", lambda h: h.tensor_copy(out=Bmat.ap[:, :, :, 96:128], in_=wuq4[:, :, :, 128:160]), reads=R(wuq), writes=R(Bmat))

        bS, bO, bL, bB, bSS, bKV = [banks[0], banks[1]], [banks[2], banks[3]], banks[4], banks[5], banks[6], banks[7]
        bA = bKV

        def upproj_stages(hh):
            hp = hh % 2
            Kn, Vv, Qn, Qr = KnT[hp], Vsb[hp], QnT[hp], QrT[hp]
            skt_h, sk_h = skt[hp], sk[hp]
            st = []

            def K_a(tb):
                for kc in range(2):
                    P.op("pe", lambda h, kc=kc: h.matmul(bKV.ap, lhsT=wukv.ap[:, kc, hh * 256:hh * 256 + 128],
                                                        rhs=kvaT.ap[:, kc, tb * 512:(tb + 1) * 512], start=(kc == 0), stop=(kc == 1)),
                         reads=R(wukv, kvaT), writes=R(bKV))

            def K_b(tb):
                sq_ = sqK[tb % 2]
                P.op("act", lambda h: h.activation(out=sq_.ap, in_=bKV.ap, func=AF.Square), reads=R(bKV), writes=R(sq_))
                P.op("dve", lambda h: h.tensor_scalar(out=Kn.ap[:, tb * 512:(tb + 1) * 512], in0=bKV.ap, scalar1=col(COL_GKN),
                                                      scalar2=None, op0=ALU.mult), reads=R(bKV, cols), writes=R(Kn))

            def K_c(tb):
                sq_ = sqK[tb % 2]
                for j in range(4):
                    kt = tb * 4 + j
                    P.op("pe", lambda h, j=j, kt=kt: h.matmul(bSS.ap[:, kt:kt + 1], lhsT=sq_.ap[:, j * 128:(j + 1) * 128],
                                                             rhs=ones.t[:, 0:1], start=True, stop=True),
                         reads=R(sq_, ones), writes=R(bSS))

            def K_d():
                P.op("dve", lambda h: h.tensor_tensor(out=skt_h.t[:], in0=bSS.ap[:, 0:16], in1=sspe.t[:], op=ALU.add),
                     reads=R(bSS, sspe), writes=R(skt_h))

            def K_e():
                P.op("act", lambda h: h.activation(out=skt_h.t[:], in_=skt_h.t[:], func=AF.Ln, bias=col(COL_EPS), scale=1.0 / 192),
                     reads=R(skt_h, cols), writes=R(skt_h))
                P.op("act", lambda h: h.activation(out=sk_h.t[:], in_=skt_h.t[:], func=AF.Exp, bias=col(COL_LNS), scale=-0.5),
                     reads=R(skt_h, cols), writes=R(sk_h))

            def V_a(rnd):
                for j in range(4):
                    kt = rnd * 4 + j
                    for kc in range(2):
                        P.op("pe", lambda h, kc=kc, kt=kt, j=j: h.matmul(bKV.ap[:, j * 128:(j + 1) * 128], lhsT=kvaT.ap[:, kc, kt * 128:(kt + 1) * 128],
                                                                       rhs=wukv.ap[:, kc, hh * 256 + 128:hh * 256 + 256],
                                                                       start=(kc == 0), stop=(kc == 1), skip_group_check=True),
                             reads=R(kvaT, wukv), writes=R(bKV))

            def V_b(rnd):
                eng = "act" if rnd % 2 == 0 else "dve"
                if eng == "act":
                    P.op("act", lambda h: h.copy(out=Vv.ap[:, rnd * 4:(rnd + 1) * 4, :], in_=bKV.ap.rearrange("p (a b) -> p a b", a=4)),
                         reads=R(bKV), writes=R(Vv))
                else:
                    P.op("dve", lambda h: h.tensor_copy(out=Vv.ap[:, rnd * 4:(rnd + 1) * 4, :], in_=bKV.ap.rearrange("p (a b) -> p a b", a=4)),
                         reads=R(bKV), writes=R(Vv))

            def Q_a(qb):
                qs = slice(qb * 512, (qb + 1) * 512)
                for kc in range(4):
                    P.op("pe", lambda h, kc=kc: h.matmul(bA.ap, lhsT=wuq.ap[:, kc, hh * 192:hh * 192 + 128], rhs=qaT.ap[:, kc, qs],
                                                        start=(kc == 0), stop=(kc == 3)), reads=R(wuq, qaT), writes=R(bA))
                for kc in range(4):
                    P.op("pe", lambda h, kc=kc: h.matmul(bB.ap, lhsT=Bmat.ap[:, kc, hh, :], rhs=qaT.ap[:, kc, qs],
                                                        start=(kc == 0), stop=(kc == 3)), reads=R(Bmat, qaT), writes=R(bB))

            def Q_b(qb):
                sa, sb_ = sqA[qb], sqB[qb]
                P.op("act", lambda h: h.activation(out=sa.ap, in_=bA.ap, func=AF.Square), reads=R(bA), writes=R(sa))
                P.op("act", lambda h: h.activation(out=sb_.ap[0:64, :], in_=bB.ap[0:64, :], func=AF.Square), reads=R(bB), writes=R(sb_))

            def Q_c(qb):
                sa, sb_ = sqA[qb], sqB[qb]
                P.op("pe", lambda h: h.matmul(bSS.ap, lhsT=ones.t[:], rhs=sa.ap, start=True, stop=False), reads=R(ones, sa), writes=R(bSS))
                P.op("pe", lambda h: h.matmul(bSS.ap, lhsT=ones.t[0:64, :], rhs=sb_.ap[0:64, :], start=False, stop=True),
                     reads=R(ones, sb_), writes=R(bSS))

            def Q_d(qb):
                lv, f_ = lnv[qb], fq[qb]
                P.op("act", lambda h: h.activation(out=lv.ap, in_=bSS.ap, func=AF.Ln, bias=col(COL_EPS), scale=1.0 / 192),
                     reads=R(bSS, cols), writes=R(lv))
                P.op("act", lambda h: h.activation(out=f_.ap, in_=lv.ap, func=AF.Exp, scale=-0.5), reads=R(lv), writes=R(f_))

            def Q_e(qb):
                qs = slice(qb * 512, (qb + 1) * 512)
                f_, tb_ = fq[qb], tmpB[qb]
                P.op("dve", lambda h: h.scalar_tensor_tensor(out=Qn.ap[:, qs], in0=bA.ap, scalar=col(COL_GQN), in1=f_.ap,
                                                             op0=ALU.mult, op1=ALU.mult), reads=R(bA, cols, f_), writes=R(Qn))
                P.op("dve", lambda h: h.scalar_tensor_tensor(out=tb_.ap, in0=bB.ap, scalar=col(COL_GQAUG), in1=f_.ap,
                                                             op0=ALU.mult, op1=ALU.mult), reads=R(bB, cols, f_), writes=R(tb_))
                P.op("dve", lambda h: h.tensor_tensor(out=Qr.ap[:, qs], in0=tb_.ap, in1=CS_fm.ap[:, qs], op=ALU.mult),
                     reads=R(tb_, CS_fm), writes=R(Qr))

            for tb in range(4):
                st += [lambda tb=tb: K_a(tb), lambda tb=tb: K_b(tb), lambda tb=tb: K_c(tb)]
            st += [K_d, K_e]
            for rnd in range(4):
                st += [lambda rnd=rnd: V_a(rnd), lambda rnd=rnd: V_b(rnd)]
            for qb in range(2):
                st += [lambda qb=qb: Q_a(qb), lambda qb=qb: Q_b(qb), lambda qb=qb: Q_c(qb), lambda qb=qb: Q_d(qb), lambda qb=qb: Q_e(qb)]
            return st

        pt_i = 0
        gstep = 0
        pending = []
        for st_ in upproj_stages(0):
            st_()
        for hh in range(NH):
            hp = hh % 2
            Kn, Vv, Qn, Qr = KnT[hp], Vsb[hp], QnT[hp], QrT[hp]
            sk_h = sk[hp]
            nxt = upproj_stages(hh + 1) if hh + 1 < NH else []
            step = 0
            for qb in range(2):
                qs = slice(qb * 512, (qb + 1) * 512)
                bo, ac = bO[qb], acc[qb]

                def emit_S(kt, Kn=Kn, Qn=Qn, Qr=Qr, qs=qs):
                    bs = bS[kt % 2]
                    P.op("pe", lambda h: h.matmul(bs.ap, lhsT=Kn.ap[:, kt * 128:(kt + 1) * 128], rhs=Qn.ap[:, qs], start=True, stop=False),
                         reads=R(Kn, Qn), writes=R(bs))
                    P.op("pe", lambda h: h.matmul(bs.ap, lhsT=KrT.ap[:, kt * 128:(kt + 1) * 128], rhs=Qr.ap[:, qs], start=False, stop=True),
                         reads=R(KrT, Qr), writes=R(bs))

                emit_S(0)
                for kt in range(16):
                    for due_, fn_ in [p_ for p_ in pending if p_[0] <= gstep]:
                        fn_()
                    pending[:] = [p_ for p_ in pending if p_[0] > gstep]
                    gstep += 1
                    pt = PT[pt_i % 4]
                    pt_i += 1
                    bs = bS[kt % 2]
                    P.op("act", lambda h, pt=pt, bs=bs, kt=kt, sk_h=sk_h: h.activation(out=pt.ap, in_=bs.ap, func=AF.Exp, scale=sk_h.t[:, kt:kt + 1]),
                         reads=R(bs, sk_h), writes=R(pt))
                    if kt + 1 < 16:
                        emit_S(kt + 1)
                    if kt == 0:
                        P.op("dve", lambda h, pt=pt, ac=ac: h.tensor_copy(out=ac.ap, in_=pt.ap), reads=R(pt), writes=R(ac))
                    else:
                        P.op("dve", lambda h, pt=pt, ac=ac: h.tensor_tensor(out=ac.ap, in0=ac.ap, in1=pt.ap, op=ALU.add), reads=R(ac, pt), writes=R(ac))
                    if step < len(nxt):
                        nxt[step]()
                    step += 1
                    P.op("pe", lambda h, pt=pt, kt=kt, Vv=Vv, bo=bo: h.matmul(bo.ap, lhsT=Vv.ap[:, kt, :], rhs=pt.ap, start=(kt == 0), stop=(kt == 15)),
                         reads=R(Vv, pt), writes=R(bo))
                r_ = rl[qb]

                def tail1(ac=ac):
                    P.op("pe", lambda h: h.matmul(bL.ap, lhsT=onesf.t[:], rhs=ac.ap, start=True, stop=True), reads=R(onesf, ac), writes=R(bL))

                def tail2(r_=r_):
                    P.op("act", lambda h: h.activation(out=r_.ap, in_=bL.ap, func=AF.Ln), reads=R(bL), writes=R(r_))
                    P.op("act", lambda h: h.activation(out=r_.ap, in_=r_.ap, func=AF.Exp, scale=-1.0), reads=R(r_), writes=R(r_))

                def tail3(r_=r_, hh=hh, qs=qs, bo=bo):
                    P.op("dve", lambda h: h.tensor_tensor(out=AT.ap[:, hh, qs], in0=bo.ap, in1=r_.ap, op=ALU.mult),
                         reads=R(bo, r_), writes=R(AT))

                pending.extend([(gstep + 1, tail1), (gstep + 2, tail2), (gstep + 4, tail3)])
            assert step >= len(nxt)
        for due_, fn_ in sorted(pending, key=lambda p_: p_[0]):
            fn_()

        if debug:
            dbg_outs["AT"] = (AT, [128, 16 * 1024], BF16)
            dump_now(["AT"])
        maybe_stop("p1b")

        P.barrier()
        slab = [view(L0 + i * 28672, BF16, [14336], f"slab{i}") for i in range(2)]
        cvg = view(L0 + 57344, BF16, [8, 1024], "cvg")
        merged = view(L0 + 73728, BF16, [16, 1024], "merged")
        Fs = [view(65536 + i * 2048, F32, [512], f"F{i}") for i in range(8)]
        ubuf = view(65536 + 16384, F32, [1026], "ubuf")
        ybuf = Buf(arena[:, (65536 + 6 * 2048) // 4:(65536 + 8 * 2048) // 4], "ybuf")
        ybuf.rl = [Fs[6].res, Fs[7].res]
        slab_i = 0

        def next_slab():
            nonlocal slab_i
            s_ = slab[slab_i % 2]
            slab_i += 1
            return s_

        for sgrp in range(4):
            sl = next_slab()
            slv = sl.ap[:, 0:8192].rearrange("p (c n) -> p c n", c=16)
            ld("pool", slv, w_in_v[:, :, C_GA + sgrp * 512:C_GA + (sgrp + 1) * 512], sl)
            for mm_ in range(4):
                m = sgrp * 4 + mm_
                for tb in range(2):
                    bk = banks[(m * 2 + tb) % 4]
                    ts_ = slice(tb * 512, (tb + 1) * 512)
                    for kc in range(16):
                        P.op("pe", lambda h, kc=kc, bk=bk, slv=slv, mm_=mm_, ts_=ts_: h.matmul(bk.ap, lhsT=slv[:, kc, mm_ * 128:(mm_ + 1) * 128],
                                                                                           rhs=hT_own.ap[:, kc, ts_], start=(kc == 0), stop=(kc == 15)),
                             reads=R(sl, hT_own), writes=R(bk))
                    sg = Fs[tb]
                    sgv = sg.ap.bitcast(BF16)[:, 0:512]
                    P.op("act", lambda h, bk=bk, sgv=sgv: h.activation(out=sgv, in_=bk.ap, func=AF.Silu), reads=R(bk), writes=R(sg))
                    P.op("dve", lambda h, m=m, ts_=ts_, sgv=sgv: h.tensor_tensor(out=AT.ap[:, m, ts_], in0=AT.ap[:, m, ts_], in1=sgv, op=ALU.mult),
                         reads=R(AT, sg), writes=R(AT))

        for c in range(8):
            sl = next_slab()
            slv = sl.ap[:, 0:8192].rearrange("p (c g n) -> p c g n", c=16, g=4)
            for g_ in range(4):
                c_lo = C_CB + g_ * 1024 + c * 128
                ld("pool", slv[:, :, g_, :], w_in_v[:, :, c_lo:c_lo + 128], sl)

            def proj(bk, g, rhs_fn, ncols, slv=slv, sl=sl):
                for kc in range(16):
                    P.op("pe", lambda h, kc=kc: h.matmul(bk.ap[:, 0:ncols] if ncols < 512 else bk.ap, lhsT=slv[:, kc, g, :], rhs=rhs_fn(kc),
                                                       start=(kc == 0), stop=(kc == 15)), reads=R(sl, hT_own, hT_halo), writes=R(bk))

            for tb in range(2):
                ts_ = slice(tb * 512, (tb + 1) * 512)
                bc_, bx_ = banks[tb], banks[2 + tb]
                proj(bc_, 1, lambda kc, ts_=ts_: hT_own.ap[:, kc, ts_], 512)
                proj(bx_, 2, lambda kc, ts_=ts_: hT_own.ap[:, kc, ts_], 512)
                cc = Fs[tb]
                P.op("act", lambda h, cc=cc, bc_=bc_: h.copy(out=cc.ap, in_=bc_.ap), reads=R(bc_), writes=R(cc))
                P.op("dve", lambda h, cc=cc, bx_=bx_, tb=tb: h.tensor_tensor(out=ubuf.ap[:, 1 + tb * 512:1 + (tb + 1) * 512], in0=bx_.ap, in1=cc.ap, op=ALU.mult),
                     reads=R(bx_, cc), writes=R(ubuf))
            bh = banks[4]
            for kc in range(16):
                P.op("pe", lambda h, kc=kc, slv=slv: h.matmul(bh.ap[:, 0:2], lhsT=slv[:, kc, 1, :], rhs=hT_halo.t[:, kc, :], start=(kc == 0), stop=(kc == 15)),
                     reads=R(sl, hT_halo), writes=R(bh))
            for kc in range(16):
                P.op("pe", lambda h, kc=kc, slv=slv: h.matmul(bh.ap[:, 2:4], lhsT=slv[:, kc, 2, :], rhs=hT_halo.t[:, kc, :], start=(kc == 0), stop=(kc == 15),
                                                            skip_group_check=True), reads=R(sl, hT_halo), writes=R(bh))
            P.op("act", lambda h: h.copy(out=hcx.t[:, 0:2], in_=bh.ap[:, 0:2]), reads=R(bh), writes=R(hcx))
            P.op("dve", lambda h: h.tensor_tensor(out=ubuf.ap[:, 0:1], in0=bh.ap[:, 2:3], in1=hcx.t[:, 0:1], op=ALU.mult),
                 reads=R(bh, hcx), writes=R(ubuf))
            P.op("dve", lambda h: h.tensor_tensor(out=ubuf.ap[:, 1025:1026], in0=bh.ap[:, 3:4], in1=hcx.t[:, 1:2], op=ALU.mult),
                 reads=R(bh, hcx), writes=R(ubuf))
            P.op("dve", lambda h, c=c: h.tensor_scalar(out=ybuf.ap, in0=ubuf.ap[:, 0:1024], scalar1=col(COL_CW + 3 * c), scalar2=col(COL_CBIAS + c),
                                                      op0=ALU.mult, op1=ALU.add), reads=R(ubuf, cols), writes=R(ybuf))
            P.op("dve", lambda h, c=c: h.scalar_tensor_tensor(out=ybuf.ap, in0=ubuf.ap[:, 1:1025], scalar=col(COL_CW + 3 * c + 1), in1=ybuf.ap,
                                                             op0=ALU.mult, op1=ALU.add), reads=R(ubuf, cols, ybuf), writes=R(ybuf))
            P.op("dve", lambda h, c=c: h.scalar_tensor_tensor(out=ybuf.ap, in0=ubuf.ap[:, 2:1026], scalar=col(COL_CW + 3 * c + 2), in1=ybuf.ap,
                                                             op0=ALU.mult, op1=ALU.add), reads=R(ubuf, cols, ybuf), writes=R(ybuf))
            for tb in range(2):
                ts_ = slice(tb * 512, (tb + 1) * 512)
                bb_, bg_ = banks[5 + tb], banks[(7 + tb) if tb == 0 else 4]
                proj(bb_, 0, lambda kc, ts_=ts_: hT_own.ap[:, kc, ts_], 512)
                proj(bg_, 3, lambda kc, ts_=ts_: hT_own.ap[:, kc, ts_], 512)
                sgc, t1 = Fs[2 + tb], Fs[4 + tb]
                P.op("act", lambda h, sgc=sgc, bg_=bg_: h.activation(out=sgc.ap, in_=bg_.ap, func=AF.Silu), reads=R(bg_), writes=R(sgc))
                P.op("dve", lambda h, t1=t1, bb_=bb_, ts_=ts_: h.tensor_tensor(out=t1.ap, in0=bb_.ap, in1=ybuf.ap[:, ts_], op=ALU.mult),
                     reads=R(bb_, ybuf), writes=R(t1))
                P.op("dve", lambda h, t1=t1, sgc=sgc, c=c, ts_=ts_: h.tensor_tensor(out=cvg.ap[:, c, ts_], in0=t1.ap, in1=sgc.ap, op=ALU.mult),
                     reads=R(t1, sgc), writes=R(cvg))

        for g in range(8):
            sl = next_slab()
            s_ma = sl.ap[:, 0:4096].rearrange("p (c n) -> p c n", c=16)
            s_mc = sl.ap[:, 4096:8192].rearrange("p (c n) -> p c n", c=16)
            s_ba = sl.ap[:, 8192:12288].rearrange("p (c n) -> p c n", c=16)
            s_bc = sl.ap[:, 12288:14336].rearrange("p (c n) -> p c n", c=8)
            cs_ = slice(g * 256, (g + 1) * 256)
            sl.slot = getattr(sl, "slot", None) or P.slot("ld_" + sl.res.name)
            ld("pool", s_ma, w_in_v[:, :, C_MA + g * 256:C_MA + (g + 1) * 256], sl)
            ld("pool", s_mc, w_in_v[:, :, C_MC + g * 256:C_MC + (g + 1) * 256], sl)
            ld("pool", s_ba, w_ba_v[:, :, cs_], sl)
            ld("pool", s_bc, w_bc_v[:, :, cs_], sl)
            for jj in range(2):
                j = g * 2 + jj
                js = slice(jj * 128, (jj + 1) * 128)
                for tb in range(2):
                    ts_ = slice(tb * 512, (tb + 1) * 512)
                    par = (j * 2 + tb) % 2
                    b_ma, b_mc, b_ya, b_yc = banks[par * 4], banks[par * 4 + 1], banks[par * 4 + 2], banks[par * 4 + 3]
                    for kc in range(16):
                        P.op("pe", lambda h, kc=kc, b_ma=b_ma, s_ma=s_ma, js=js, ts_=ts_: h.matmul(b_ma.ap, lhsT=s_ma[:, kc, js], rhs=hT_own.ap[:, kc, ts_],
                                                                                                start=(kc == 0), stop=(kc == 15)), reads=R(sl, hT_own), writes=R(b_ma))
                    for kc in range(16):
                        P.op("pe", lambda h, kc=kc, b_mc=b_mc, s_mc=s_mc, js=js, ts_=ts_: h.matmul(b_mc.ap, lhsT=s_mc[:, kc, js], rhs=hT_own.ap[:, kc, ts_],
                                                                                                start=(kc == 0), stop=(kc == 15)), reads=R(sl, hT_own), writes=R(b_mc))
                    for kc in range(16):
                        P.op("pe", lambda h, kc=kc, b_ya=b_ya, s_ba=s_ba, js=js, ts_=ts_: h.matmul(b_ya.ap, lhsT=s_ba[:, kc, js], rhs=AT.ap[:, kc, ts_],
                                                                                                start=(kc == 0), stop=(kc == 15)), reads=R(sl, AT), writes=R(b_ya))
                    for kc in range(8):
                        P.op("pe", lambda h, kc=kc, b_yc=b_yc, s_bc=s_bc, js=js, ts_=ts_: h.matmul(b_yc.ap, lhsT=s_bc[:, kc, js], rhs=cvg.ap[:, kc, ts_],
                                                                                                start=(kc == 0), stop=(kc == 7)), reads=R(sl, cvg), writes=R(b_yc))
                    ta, tcg, m1, m2 = Fs[par], Fs[2 + par], Fs[4 + par], Fs[6 + par]
                    P.op("act", lambda h, ta=ta, b_ma=b_ma: h.activation(out=ta.ap, in_=b_ma.ap, func=AF.Tanh, scale=0.5), reads=R(b_ma), writes=R(ta))
                    P.op("act", lambda h, tcg=tcg, b_mc=b_mc: h.activation(out=tcg.ap, in_=b_mc.ap, func=AF.Tanh, scale=0.5), reads=R(b_mc), writes=R(tcg))
                    P.op("dve", lambda h, ta=ta, m1=m1, b_ya=b_ya: h.scalar_tensor_tensor(out=m1.ap, in0=ta.ap, scalar=1.0, in1=b_ya.ap, op0=ALU.add, op1=ALU.mult),
                         reads=R(ta, b_ya), writes=R(m1))
                    P.op("dve", lambda h, tcg=tcg, m2=m2, b_yc=b_yc: h.scalar_tensor_tensor(out=m2.ap, in0=tcg.ap, scalar=1.0, in1=b_yc.ap, op0=ALU.add, op1=ALU.mult),
                         reads=R(tcg, b_yc), writes=R(m2))
                    P.op("dve", lambda h, m1=m1, m2=m2, j=j, ts_=ts_: h.tensor_tensor(out=merged.ap[:, j, ts_], in0=m1.ap, in1=m2.ap, op=ALU.add),
                         reads=R(m1, m2), writes=R(merged))

        if debug:
            dbg_outs["cvg"] = (cvg, [128, 8 * 1024], BF16)
            dbg_outs["merged"] = (merged, [128, 16 * 1024], BF16)
            dbg_outs["AT2"] = (AT, [128, 16 * 1024], BF16)
            dump_now(["cvg", "merged", "AT2"])
        maybe_stop("p2c")

        x2 = view(0, F32, [8, 2048], "x2")
        x2t = []
        for t in range(8):
            x2t.append(Buf(x2.ap[:, t, :], f"x2_{t}"))

        def res_deps(res):
            return ([res.w] if res.w is not None else []) + list(res.r.values()) + list(res.rd)
        S_ALL = [f_.res for f_ in Fs] + [ubuf.res]
        h2T = view(L0 + 73728, BF16, [16, 1024], "h2T")
        h2T.rl = [h2T.res, merged.res]
        pleg = view(L0 + 57344, F32, [2048], "pleg")
        wpp = view(196608, BF16, [2, 2048], "wpp")
        c3 = Carver(65536, 90112)

        def take_s(dt, shape, name):
            b_ = c3.take(dt, shape, name)
            b_.rl = [b_.res] + S_ALL
            return b_

        pT = take_s(BF16, [2, 1024], "pT")
        ptm = [take_s(F32, [256], f"ptm{i}") for i in range(2)]
        pbb = [take_s(BF16, [256], f"pbb{i}") for i in range(2)]
        h2b = [take_s(BF16, [2048], f"h2b{i}") for i in range(2)]
        tg = [take_s(F32, [512], f"tg{i}") for i in range(2)]
        e1 = [take_s(F32, [512], f"e1{i}") for i in range(2)]

        ld("pool", wpp.ap, w_pp_v, wpp)
        old_deps = res_deps(hT_own.res) + res_deps(AT.res)
        for t in range(8):
            ld("sp", x2t[t].ap, x_d[t * 128:(t + 1) * 128, :], x2t[t], deps=old_deps)
        ld("sp", pleg.ap, pleg_d, pleg, deps=res_deps(cvg.res))

        def norm2_stats(t):
            hb2 = h2b[t % 2]
            P.op("act", lambda h: h.activation(out=hb2.ap, in_=x2t[t].ap, func=AF.Square, accum_out=ss2.t[:, t:t + 1]),
                 reads=R(x2t[t]), writes=R(hb2, ss2))
            P.op("act", lambda h: h.activation(out=lnt.t[:, t:t + 1], in_=ss2.t[:, t:t + 1], func=AF.Ln, bias=col(COL_EPS), scale=1.0 / D),
                 reads=R(ss2, cols), writes=R(lnt))
            P.op("act", lambda h: h.activation(out=rs2.t[:, t:t + 1], in_=lnt.t[:, t:t + 1], func=AF.Exp, scale=-0.5), reads=R(lnt), writes=R(rs2))
            P.op("dve", lambda h: h.scalar_tensor_tensor(out=hb2.ap, in0=x2t[t].ap, scalar=rs2.t[:, t:t + 1], in1=pleg.ap, op0=ALU.mult, op1=ALU.mult),
                 reads=R(x2t[t], rs2, pleg), writes=R(hb2))
            pm, pb_ = ptm[t % 2], pbb[t % 2]
            ld("sp", pm.ap, p_d[t * 128:(t + 1) * 128, :], pm)
            P.op("dve", lambda h: h.tensor_copy(out=pb_.ap, in_=pm.ap), reads=R(pm), writes=R(pb_))

        def norm2_transposes(t):
            hb2, pb_ = h2b[t % 2], pbb[t % 2]
            bviews = [banks[4 + c // 8].bf[:, (c % 8) * 128:(c % 8 + 1) * 128] for c in range(16)]
            transposes(hb2.ap, 16, bviews, R(hb2), [banks[4 + c // 8] for c in range(16)])
            P.op("act", lambda h: h.copy(out=h2T.ap[:, 0:8, t * 128:(t + 1) * 128], in_=banks[4].bf.rearrange("p (a b) -> p a b", a=8)),
                 reads=R(banks[4]), writes=R(h2T))
            P.op("dve", lambda h: h.tensor_copy(out=h2T.ap[:, 8:16, t * 128:(t + 1) * 128], in_=banks[5].bf.rearrange("p (a b) -> p a b", a=8)),
                 reads=R(banks[5]), writes=R(h2T))
            transposes(pb_.ap, 2, [banks[6].bf[:, c * 128:(c + 1) * 128] for c in range(2)], R(pb_), [banks[6]] * 2)
            P.op("act", lambda h: h.copy(out=pT.ap[:, :, t * 128:(t + 1) * 128], in_=banks[6].bf[:, 0:256].rearrange("p (a b) -> p a b", a=2)),
                 reads=R(banks[6]), writes=R(pT))

        for n in range(4):
            sl = next_slab()
            w_ap = sl.ap[:, 0:8192].rearrange("p (c n) -> p c n", c=16)
            ns = slice(n * 512, (n + 1) * 512)
            ld("pool", w_ap, w_out_v[:, :, ns], sl)
            for t in range(8):
                bk = banks[(n * 8 + t) % 4]
                for kc in range(16):
                    P.op("pe", lambda h, kc=kc, bk=bk, w_ap=w_ap, t=t: h.matmul(bk.ap, lhsT=merged.ap[:, kc, t * 128:(t + 1) * 128], rhs=w_ap[:, kc, :],
                                                                            start=(kc == 0), stop=(kc == 15)), reads=R(merged, sl), writes=R(bk))
                P.op("dve", lambda h, bk=bk, t=t, ns=ns: h.scalar_tensor_tensor(out=x2t[t].ap[:, ns], in0=bk.ap, scalar=0.5, in1=x2t[t].ap[:, ns],
                                                                             op0=ALU.mult, op1=ALU.add), reads=R(bk, x2t[t]), writes=R(x2t[t]))
                if n == 3:
                    if t >= 2:
                        norm2_transposes(t - 2)
                    norm2_stats(t)
        norm2_transposes(6)
        norm2_transposes(7)
        so = P.slot("out")
        for n in range(4):
            sl = next_slab()
            w_ap = sl.ap[:, 0:8192].rearrange("p (c n) -> p c n", c=16)
            ns = slice(n * 512, (n + 1) * 512)
            ld("pool", w_ap, w_pg_v[:, :, ns], sl)
            for t in range(8):
                par = (n * 8 + t) % 2
                bg_, bp_ = banks[par * 2], banks[par * 2 + 1]
                for kc in range(16):
                    P.op("pe", lambda h, kc=kc, bg_=bg_, w_ap=w_ap, t=t: h.matmul(bg_.ap, lhsT=h2T.ap[:, kc, t * 128:(t + 1) * 128], rhs=w_ap[:, kc, :],
                                                                              start=(kc == 0), stop=(kc == 15)), reads=R(h2T, sl), writes=R(bg_))
                for kc in range(2):
                    P.op("pe", lambda h, kc=kc, bp_=bp_, t=t, ns=ns: h.matmul(bp_.ap, lhsT=pT.ap[:, kc, t * 128:(t + 1) * 128], rhs=wpp.ap[:, kc, ns],
                                                                          start=(kc == 0), stop=(kc == 1)), reads=R(pT, wpp), writes=R(bp_))
                tg_, e1_ = tg[par], e1[par]
                P.op("act", lambda h, tg_=tg_, bg_=bg_: h.activation(out=tg_.ap, in_=bg_.ap, func=AF.Tanh, scale=0.5), reads=R(bg_), writes=R(tg_))
                P.op("dve", lambda h, tg_=tg_, e1_=e1_, bp_=bp_: h.scalar_tensor_tensor(out=e1_.ap, in0=tg_.ap, scalar=1.0, in1=bp_.ap, op0=ALU.add, op1=ALU.mult),
                     reads=R(tg_, bp_), writes=R(e1_))
                P.op("dve", lambda h, e1_=e1_, t=t, ns=ns: h.scalar_tensor_tensor(out=x2t[t].ap[:, ns], in0=e1_.ap, scalar=0.5, in1=x2t[t].ap[:, ns],
                                                                               op0=ALU.mult, op1=ALU.add), reads=R(e1_, x2t[t]), writes=R(x2t[t]))
                if n == 3:
                    P.dma("sp", lambda h, t=t: h.dma_start(out=out_d[t * 128:(t + 1) * 128, :], in_=x2t[t].ap), so, reads=R(x2t[t]))
        fin_deps = [o for o in P.ops["sp"] if o.slot is so][-1:]
        if dbg_names:
            fin_deps += [o for o in P.ops["sp"] if o.slot is dq["sp"]][-1:]
        P.op("sp", lambda h: h.nop(), deps=fin_deps, real=False)

        P.finalize()
        print("ops", {e: len(P.ops[e]) for e in P.ENGS}, "signals", {e: sum(1 for o in P.ops[e] if o.signal) for e in P.ENGS}, "slots", len(P.slots))
        for s_ in P.slots:
            s_.sem = es.enter_context(nc.semaphore("d_" + s_.name))
        with nc.Block() as block:
            @block.tensor
            def _(h):
                P.emit("pe", h, sems)

            @block.scalar
            def _(h):
                P.emit("act", h, sems)

            @block.vector
            def _(h):
                P.emit("dve", h, sems)

            @block.gpsimd
            def _(h):
                P.emit("pool", h, sems)

            @block.sync
            def _(h):
                P.emit("sp", h, sems)
    nc._dbg_names = dbg_names
    return nc


def _host_inputs(x, p, positions, norm_g, w_in, q_lat_g, kv_lat_g, w_uq, w_ukv, q_norm_g, k_norm_g,
                 conv_w, conv_b, w_branch_attn, w_branch_conv, w_out, ple_norm_g, w_ple_gate, w_ple_proj):
    f32 = np.float32
    x = np.asarray(x, f32)
    p = np.asarray(p, f32)[0]
    positions = np.asarray(positions, np.int32)
    qg = np.asarray(q_norm_g, f32)[0]
    kg = np.asarray(k_norm_g, f32)[0]
    cw = np.asarray(conv_w, f32)[0]
    cb = np.asarray(conv_b, f32)[0]
    inv_freq = (1.0 / (10000.0 ** (np.arange(0, 64, 2, dtype=np.float32) / 64.0))).astype(f32)

    def rep(v):
        return np.ascontiguousarray(np.broadcast_to(np.asarray(v, f32)[None, :], (128, len(v))))

    shared = {
        "rows_a": np.ascontiguousarray(np.concatenate([rep(np.asarray(q_lat_g, f32)[0]), rep(np.asarray(kv_lat_g, f32)[0]), rep(kg[128:192])], axis=1)),
        "normg_row": rep(np.asarray(norm_g, f32)[0]),
        "pleg_row": rep(np.asarray(ple_norm_g, f32)[0]),
        "invf4": rep(np.tile(inv_freq, 4)),
        "ph_row": rep(np.concatenate([np.full(64, np.pi / 2, f32), np.zeros(64, f32)])),
        "w_in": np.ascontiguousarray(np.asarray(w_in, f32)[0]),
        "w_uq": np.ascontiguousarray(np.asarray(w_uq, f32)[0]),
        "w_ukv": np.ascontiguousarray(np.asarray(w_ukv, f32)[0]),
        "w_ba": np.ascontiguousarray(np.asarray(w_branch_attn, f32)[0]),
        "w_bc": np.ascontiguousarray(np.asarray(w_branch_conv, f32)[0]),
        "w_out": np.ascontiguousarray(np.asarray(w_out, f32)[0]),
        "w_pg": np.ascontiguousarray(np.asarray(w_ple_gate, f32)[0]),
        "w_pp": np.ascontiguousarray(np.asarray(w_ple_proj, f32)[0]),
    }
    cols0 = np.zeros((128, NCOL), f32)
    cols0[:, COL_GQN] = qg[0:128]
    cols0[:, COL_GQAUG] = np.concatenate([qg[128:192], qg[160:192], qg[128:160]])
    cols0[:, COL_GKN] = kg[0:128]
    cols0[:, COL_INVF] = np.tile(inv_freq, 4)
    cols0[:, COL_PH] = np.concatenate([np.full(64, np.pi / 2, f32), np.zeros(64, f32)])
    cols0[:, COL_EPS] = EPS
    cols0[:, COL_LNS] = -0.5 * math.log(192.0)
    for c in range(8):
        for j in range(3):
            cols0[:, COL_CW + 3 * c + j] = cw[j, c * 128:(c + 1) * 128]
        cols0[:, COL_CBIAS + c] = cb[c * 128:(c + 1) * 128]
    in_maps = []
    for core in range(8):
        b, hq = core // 2, core % 2
        own = slice(hq * T, (hq + 1) * T)
        oth = slice((1 - hq) * T, (2 - hq) * T)
        cols = cols0.copy()
        cols[:, COL_ML] = 1.0 if hq == 1 else 0.0
        cols[:, COL_MR] = 1.0 if hq == 0 else 0.0
        pos_loc = np.concatenate([positions[b, own], positions[b, oth]])
        m = dict(shared)
        m["x_loc"] = np.ascontiguousarray(np.concatenate([x[b, own], x[b, oth]], axis=0))
        m["p_loc"] = np.ascontiguousarray(p[b, own])
        m["pos_tm"] = np.ascontiguousarray(pos_loc.reshape(16, 128).T)
        m["pos_row"] = np.ascontiguousarray(np.broadcast_to(positions[b, own][None, :], (128, T)))
        m["cols"] = cols
        in_maps.append(m)
    return in_maps


_NC_CACHE = {}


def kernel(**inputs):
    in_maps = _host_inputs(**inputs)
    if "nc" not in _NC_CACHE:
        _NC_CACHE["nc"] = build_nc(debug=False)
    nc = _NC_CACHE["nc"]
    res = run_bass_kernel_spmd(nc, in_maps, core_ids=list(range(8)))
    out = np.empty((4, S, D), np.float32)
    for core in range(8):
        b, hq = core // 2, core % 2
        out[b, hq * T:(hq + 1) * T, :] = res.results[core]["out"]
    return out
```

```python
import math
from contextlib import ExitStack

import numpy as np
import concourse.bass as bass
import concourse.mybir as mybir
from concourse.bass_utils import run_bass_kernel_spmd

F32 = mybir.dt.float32
BF16 = mybir.dt.bfloat16
I32 = mybir.dt.int32
ALU = mybir.AluOpType
AF = mybir.ActivationFunctionType

D = 2048
S = 2048
T = 1024
NH = 16
EPS = 1e-6
IN_W = 11072
C_GA, C_CB, C_CC, C_CX, C_GC, C_MA, C_MC = 832, 2880, 3904, 4928, 5952, 6976, 9024
NCOL = 48
COL_GQN, COL_GQAUG, COL_GKN, COL_INVF, COL_PH, COL_ML, COL_MR, COL_EPS, COL_LNS, COL_CW, COL_CBIAS = 0, 1, 2, 3, 4, 5, 6, 7, 8, 9, 33

MAGIC = 12582912.0
INV2PI = 0.15915494309189535
CW1 = 6.28125
CW2 = 2.0 * math.pi - 6.28125
PI_LO = 3.1415925


class Res:
    def __init__(self, name, excl=False):
        self.name = name
        self.w = None
        self.r = {}
        self.rd = []
        self.excl = excl


class Slot:
    def __init__(self, name):
        self.name = name
        self.total = 0
        self.sem = None


class Op:
    __slots__ = ("eng", "fn", "deps", "signal", "count", "slot", "val")

    def __init__(self, eng, fn, deps, slot=None):
        self.eng = eng
        self.fn = fn
        self.deps = deps
        self.signal = False
        self.count = None
        self.slot = slot
        self.val = None


class Prog:
    ENGS = ("pe", "act", "dve", "pool", "sp")

    def __init__(self):
        self.ops = {e: [] for e in self.ENGS}
        self.slots = []
        self.last_real = {}
        self.bar = []

    def slot(self, name):
        s = Slot(name)
        self.slots.append(s)
        return s

    def _deps(self, eng, reads, writes, deps):
        d = [x for x in deps if x is not None]
        for r in reads:
            if r.w is not None:
                d.append(r.w)
            if r.excl:
                d.extend(o for e, o in r.r.items() if e != eng)
        for w in writes:
            if w.w is not None:
                d.append(w.w)
            d.extend(w.r.values())
            d.extend(w.rd)
        return d

    def _track(self, o, reads, writes):
        for r in reads:
            if o.slot is not None:
                r.rd.append(o)
            else:
                r.r[o.eng] = o
        for w in writes:
            w.w = o
            w.r = {}
            w.rd = []

    def op(self, eng, fn, reads=(), writes=(), deps=(), real=True):
        o = Op(eng, fn, self._deps(eng, reads, writes, deps))
        self._track(o, reads, writes)
        self.ops[eng].append(o)
        if real:
            self.last_real[eng] = o
        return o

    def dma(self, eng, fn, slot, reads=(), writes=(), deps=()):
        dl = [d for d in self._deps(eng, reads, writes, list(deps) + self.bar) if d.slot is not slot]
        o = Op(eng, fn, dl, slot=slot)
        slot.total += 16
        o.val = slot.total
        self._track(o, reads, writes)
        self.ops[eng].append(o)
        return o

    def barrier(self):
        last = [self.last_real[e] for e in ("pe", "act", "dve", "pool") if e in self.last_real]
        for e in self.ENGS:
            self.op(e, lambda h: h.nop(), deps=last, real=False)
        self.bar = last

    def finalize(self):
        for e in self.ENGS:
            for o in self.ops[e]:
                for d in o.deps:
                    if d.slot is None and not (d.eng == "pe" and e == "pe"):
                        d.signal = True
        for e in self.ENGS:
            c = 0
            for o in self.ops[e]:
                if o.slot is None and o.signal:
                    c += 1
                    o.count = c

    def emit(self, eng, h, sems):
        waited = {}
        cut = getattr(self, "cut", None)
        ops = self.ops[eng] if cut is None else self.ops[eng][:cut[eng]]
        for o in ops:
            for d in o.deps:
                if d.slot is not None:
                    key = id(d.slot)
                    if waited.get(key, 0) < d.val:
                        h.wait_ge(d.slot.sem, d.val)
                        waited[key] = d.val
                else:
                    if d.eng == "pe" and eng == "pe":
                        continue
                    if waited.get(d.eng, 0) < d.count:
                        h.wait_ge(sems[d.eng], d.count)
                        waited[d.eng] = d.count
            inst = o.fn(h)
            if o.slot is not None:
                inst.then_inc(o.slot.sem, 16)
            elif o.signal:
                inst.then_inc(sems[eng], 1)


class Buf:
    def __init__(self, ap, name):
        self.ap = ap
        self.res = Res(name)
        self.rl = None


def build_nc(debug=False, stop=None):
    nc = bass.Bass("TRN2", target_bir_lowering=False)
    P = Prog()

    def din(name, shape, dt=F32):
        return nc.dram_tensor(name, list(shape), dt, kind="ExternalInput").ap()

    x_d = din("x_loc", [S, D])
    p_d = din("p_loc", [T, 256])
    postm_d = din("pos_tm", [128, 16], I32)
    posrow_d = din("pos_row", [128, T], I32)
    cols_d = din("cols", [128, NCOL])
    rowsa_d = din("rows_a", [128, 832])
    normg_d = din("normg_row", [128, D])
    pleg_d = din("pleg_row", [128, D])
    invf4_d = din("invf4", [128, 128])
    phrow_d = din("ph_row", [128, 128])
    w_in_d = din("w_in", [D, IN_W])
    w_uq_d = din("w_uq", [512, 3072])
    w_ukv_d = din("w_ukv", [256, 4096])
    w_ba_d = din("w_ba", [D, D])
    w_bc_d = din("w_bc", [1024, D])
    w_out_d = din("w_out", [D, D])
    w_pg_d = din("w_pg", [D, D])
    w_pp_d = din("w_pp", [256, D])
    out_d = nc.dram_tensor("out", [T, D], F32, kind="ExternalOutput").ap()
    dbg_outs = {}

    w_in_v = w_in_d.rearrange("(c p) n -> p c n", p=128)
    w_uq_v = w_uq_d.rearrange("(c p) n -> p c n", p=128)
    w_ukv_v = w_ukv_d.rearrange("(c p) n -> p c n", p=128)
    w_ba_v = w_ba_d.rearrange("(c p) n -> p c n", p=128)
    w_bc_v = w_bc_d.rearrange("(c p) n -> p c n", p=128)
    w_out_v = w_out_d.rearrange("(c p) n -> p c n", p=128)
    w_pg_v = w_pg_d.rearrange("(c p) n -> p c n", p=128)
    w_pp_v = w_pp_d.rearrange("(c p) n -> p c n", p=128)

    with ExitStack() as es:
        ARENA_BYTES = 204800
        arena = es.enter_context(nc.sbuf_tensor("arena", [128, ARENA_BYTES // 4], F32))

        def view(off, dt, shape, name):
            n = 1
            for s_ in shape:
                n *= s_
            el = 2 if dt == BF16 else 4
            assert off % 4 == 0 and (n * el) % 4 == 0, (name, off, n)
            assert off + n * el <= ARENA_BYTES, (name, off, n * el)
            ap = arena[:, off // 4:(off + n * el) // 4]
            if dt != F32:
                ap = ap.bitcast(dt)
            if len(shape) == 2:
                ap = ap.rearrange("p (a b) -> p a b", a=shape[0])
            elif len(shape) == 3:
                ap = ap.rearrange("p (a b c) -> p a b c", a=shape[0], b=shape[1])
            return Buf(ap, name)

        class Carver:
            def __init__(self, start, end):
                self.off = start
                self.end = end

            def take(self, dt, shape, name):
                n = 1
                for s_ in shape:
                    n *= s_
                nb = n * (2 if dt == BF16 else 4)
                nb_al = (nb + 31) // 32 * 32
                b = view(self.off, dt, shape, name)
                self.off += nb_al
                assert self.off <= self.end, (name, self.off, self.end)
                return b

        def sb(name, shape, dt=F32):
            t = es.enter_context(nc.sbuf_tensor("sb_" + name, list(shape), dt))
            b = Buf(None, name)
            b.t = t
            return b

        cols = sb("cols", [128, NCOL])
        ident = sb("ident", [128, 128], BF16)
        identf = sb("identf", [128, 128])
        ones = sb("ones", [128, 128], BF16)
        ss = sb("ss", [128, 16])
        rs = sb("rs", [128, 16])
        ssq = sb("ssq", [128, 16])
        rq = sb("rq", [128, 16])
        sskv = sb("sskv", [128, 16])
        rkv = sb("rkv", [128, 16])
        sspe = sb("sspe", [128, 16])
        lnt = sb("lnt", [128, 16])
        skt = [sb(f"skt{i}", [128, 16]) for i in range(2)]
        sk = [sb(f"sk{i}", [128, 16]) for i in range(2)]
        ss2 = sb("ss2", [128, 8])
        rs2 = sb("rs2", [128, 8])
        posf = sb("posf", [128, 16])
        postm = sb("postm", [128, 16], I32)
        hT_halo = sb("hT_halo", [128, 16, 2], BF16)
        hcx = sb("hcx", [128, 4])

        def col(i):
            return cols.t[:, i:i + 1]

        banks = []
        for i in range(8):
            t = es.enter_context(nc.psum_tensor(f"bank{i}", [128, 512], F32))
            b = Buf(t[:], f"bank{i}")
            b.res.excl = True
            b.bf = t[:].bitcast(BF16)
            banks.append(b)

        sems = {e: es.enter_context(nc.semaphore(f"s_{e}")) for e in Prog.ENGS}

        hT_own = view(0, BF16, [16, 1024], "hT_own")
        AT = view(32768, BF16, [16, 1024], "AT")
        qaT = view(65536, BF16, [4, 1024], "qaT")
        kvaT = view(73728, BF16, [2, 2048], "kvaT")
        KrT = view(81920, BF16, [2048], "KrT")
        CS_fm = view(86016, F32, [1024], "CS_fm")
        L0 = 90112

        dq = {"sp": P.slot("sp_misc")}

        def maybe_stop(tag):
            if stop == tag and getattr(P, "cut", None) is None:
                P.barrier()
                fd = [o for o in P.ops["sp"] if o.slot is dq["sp"]][-1:]
                P.op("sp", lambda h: h.nop(), deps=fd + P.bar, real=False)
                P.cut = {e: len(P.ops[e]) for e in P.ENGS}
                print("STOP at", tag, {e: len(P.ops[e]) for e in P.ENGS})

        def ld(eng, out_ap, in_ap, wbuf, slot=None, reads=(), deps=()):
            if slot is None:
                if not hasattr(wbuf, "slot"):
                    wbuf.slot = P.slot("ld_" + wbuf.res.name)
                slot = wbuf.slot
            return P.dma(eng, lambda h: h.dma_start(out=out_ap, in_=in_ap), slot,
                         reads=R(*reads), writes=R(wbuf), deps=deps)

        def R(*bufs):
            out_ = []
            for b in bufs:
                out_.extend(b.rl if b.rl is not None else [b.res])
            return out_

        ld("sp", cols.t[:], cols_d, cols)
        ld("sp", postm.t[:], postm_d, postm)

        P.op("dve", lambda h: h.memset(identf.t[:], 1.0), writes=R(identf))
        P.op("pool", lambda h: h.affine_select(out=identf.t[:], in_=identf.t[:], pattern=[[-1, 128]],
                                               compare_op=ALU.is_equal, fill=0.0, base=0, channel_multiplier=1),
             reads=R(identf), writes=R(identf))
        P.op("dve", lambda h: h.tensor_copy(out=ident.t[:], in_=identf.t[:]), reads=R(identf), writes=R(ident))
        P.op("dve", lambda h: h.memset(ones.t[:], 1.0), writes=R(ones))
        for b_ in (ss, ssq, sskv, sspe, ss2):
            P.op("dve", lambda h, b_=b_: h.memset(b_.t[:], 0.0), writes=R(b_))

        W_BASE = 147456
        c0 = Carver(L0, W_BASE)
        xs = [c0.take(F32, [2048], f"xs{i}") for i in range(2)]
        hb = [c0.take(BF16, [2048], f"hb{i}") for i in range(2)]
        hTt = [c0.take(BF16, [16, 128], f"hTt{i}") for i in range(2)]
        normg = c0.take(F32, [2048], "normg")
        CS_tm = c0.take(F32, [16, 128], "CS_tm")
        rows_a = c0.take(F32, [832], "rows_a")
        qa_n = c0.take(BF16, [512], "qa_n")
        kva_n = c0.take(BF16, [256], "kva_n")
        kpe_g = c0.take(F32, [64], "kpe_g")
        tcb = c0.take(F32, [64], "tcb")
        tsb = c0.take(F32, [64], "tsb")
        Kr_tm = c0.take(BF16, [128], "Kr_tm")
        c0 = Carver(W_BASE, ARENA_BYTES)
        trA = c0.take(F32, [2048], "trA")
        trB = c0.take(F32, [2048], "trB")
        trC = c0.take(F32, [2048], "trC")
        posrow = c0.take(I32, [1024], "posrow")
        invf4 = c0.take(F32, [128], "invf4")
        phrow = c0.take(F32, [128], "phrow")
        Wlat = view(32768, BF16, [16, 832], "Wlat")
        junk = view(32768 + 26624, BF16, [2048], "junk")

        ld("sp", rows_a.ap, rowsa_d, rows_a)
        ld("sp", normg.ap, normg_d, normg)
        ld("sp", invf4.ap, invf4_d, invf4)
        ld("sp", phrow.ap, phrow_d, phrow)
        ld("sp", posrow.ap, posrow_d, posrow)
        ld("pool", Wlat.ap, w_in_v[:, :, 0:832], Wlat)

        def range_reduce(ang, kf, n):
            a, k = ang.ap[:, 0:n], kf.ap[:, 0:n]
            P.op("dve", lambda h: h.tensor_scalar(out=k, in0=a, scalar1=INV2PI, scalar2=MAGIC, op0=ALU.mult, op1=ALU.add),
                 reads=R(ang), writes=R(kf))
            P.op("dve", lambda h: h.tensor_scalar(out=k, in0=k, scalar1=-MAGIC, scalar2=None, op0=ALU.add),
                 reads=R(kf), writes=R(kf))
            P.op("dve", lambda h: h.scalar_tensor_tensor(out=a, in0=k, scalar=-CW1, in1=a, op0=ALU.mult, op1=ALU.add),
                 reads=R(kf, ang), writes=R(ang))
            P.op("dve", lambda h: h.scalar_tensor_tensor(out=a, in0=k, scalar=-CW2, in1=a, op0=ALU.mult, op1=ALU.add),
                 reads=R(kf, ang), writes=R(ang))

        def wrap_clamp(ang, m, n):
            a, k = ang.ap[:, 0:n], m.ap[:, 0:n]
            P.op("dve", lambda h: h.tensor_scalar(out=k, in0=a, scalar1=math.pi, scalar2=-2.0 * math.pi, op0=ALU.is_gt, op1=ALU.mult),
                 reads=R(ang), writes=R(m))
            P.op("dve", lambda h: h.tensor_tensor(out=a, in0=a, in1=k, op=ALU.add), reads=R(ang, m), writes=R(ang))
            P.op("dve", lambda h: h.tensor_scalar(out=a, in0=a, scalar1=PI_LO, scalar2=-PI_LO, op0=ALU.min, op1=ALU.max),
                 reads=R(ang), writes=R(ang))

        P.op("dve", lambda h: h.tensor_copy(out=trA.ap[:, 0:1024], in_=posrow.ap), reads=R(posrow), writes=R(trA))
        P.op("dve", lambda h: h.tensor_scalar(out=trA.ap[:, 0:1024], in0=trA.ap[:, 0:1024], scalar1=col(COL_INVF), scalar2=None, op0=ALU.mult),
             reads=R(trA, cols), writes=R(trA))
        range_reduce(trA, trB, 1024)
        P.op("dve", lambda h: h.tensor_scalar(out=trA.ap[:, 0:1024], in0=trA.ap[:, 0:1024], scalar1=col(COL_PH), scalar2=None, op0=ALU.add),
             reads=R(trA, cols), writes=R(trA))
        wrap_clamp(trA, trB, 1024)
        P.op("act", lambda h: h.activation(out=CS_fm.ap, in_=trA.ap[:, 0:1024], func=AF.Sin), reads=R(trA), writes=R(CS_fm))

        P.op("dve", lambda h: h.tensor_copy(out=posf.t[:], in_=postm.t[:]), reads=R(postm), writes=R(posf))
        for t in range(16):
            P.op("dve", lambda h, t=t: h.tensor_scalar(out=trC.ap[:, t * 128:(t + 1) * 128], in0=invf4.ap, scalar1=posf.t[:, t:t + 1],
                                                      scalar2=None, op0=ALU.mult), reads=R(invf4, posf), writes=R(trC))
        range_reduce(trC, trB, 2048)
        for t in range(16):
            P.op("dve", lambda h, t=t: h.tensor_tensor(out=trC.ap[:, t * 128:(t + 1) * 128], in0=trC.ap[:, t * 128:(t + 1) * 128],
                                                      in1=phrow.ap, op=ALU.add), reads=R(trC, phrow), writes=R(trC))
        wrap_clamp(trC, trB, 2048)
        P.op("act", lambda h: h.activation(out=CS_tm.ap.rearrange("p a b -> p (a b)"), in_=trC.ap, func=AF.Sin),
             reads=R(trC), writes=R(CS_tm))

        trig_done = [P.last_real["dve"], P.last_real["act"]]
        wukv = view(W_BASE, BF16, [2, 4096], "wukv")
        wuq = view(W_BASE + 16384, BF16, [4, 3072], "wuq")
        Bmat = view(W_BASE + 40960, BF16, [4, 16, 128], "Bmat")
        ld("pool", wukv.ap, w_ukv_v, wukv, deps=trig_done)
        ld("pool", wuq.ap, w_uq_v, wuq, deps=trig_done)

        def transposes(src_ap, nblk, bank_views, reads, bank_bufs):
            for c in range(nblk):
                bv = bank_views[c]
                P.op("pe", lambda h, c=c, bv=bv: h.transpose(out=bv, in_=src_ap[:, c * 128:(c + 1) * 128], identity=ident.t[:]),
                     reads=reads + R(ident), writes=[bank_bufs[c].res])

        maybe_stop("p0")
        tile_dst = {}

        def stA(t):
            par = t % 2
            xb_, hb_ = xs[par], hb[par]
            ld("sp", xb_.ap, x_d[t * 128:(t + 1) * 128, :], xb_)
            P.op("act", lambda h: h.activation(out=junk.ap, in_=xb_.ap, func=AF.Square, accum_out=ss.t[:, t:t + 1]),
                 reads=R(xb_), writes=R(junk, ss))
            P.op("act", lambda h: h.activation(out=lnt.t[:, 0:1], in_=ss.t[:, t:t + 1], func=AF.Ln, bias=col(COL_EPS), scale=1.0 / D),
                 reads=R(ss, cols), writes=R(lnt))
            P.op("act", lambda h: h.activation(out=rs.t[:, t:t + 1], in_=lnt.t[:, 0:1], func=AF.Exp, scale=-0.5),
                 reads=R(lnt), writes=R(rs))
            P.op("dve", lambda h: h.scalar_tensor_tensor(out=hb_.ap, in0=xb_.ap, scalar=rs.t[:, t:t + 1], in1=normg.ap,
                                                         op0=ALU.mult, op1=ALU.mult),
                 reads=R(xb_, rs, normg), writes=R(hb_))

        def stB(t):
            own = t < 8
            par = t % 2
            hb_ = hb[par]
            bx0, bx1 = banks[0], banks[1]
            bb = [bx0, bx1]
            bviews = [bb[c // 8].bf[:, (c % 8) * 128:(c % 8 + 1) * 128] for c in range(16)]
            transposes(hb_.ap, 16, bviews, R(hb_), [bb[c // 8] for c in range(16)])
            if own:
                dst = hT_own
                d0 = hT_own.ap[:, 0:8, t * 128:(t + 1) * 128]
                d1 = hT_own.ap[:, 8:16, t * 128:(t + 1) * 128]
            else:
                dst = hTt[par]
                d0 = dst.ap[:, 0:8, :]
                d1 = dst.ap[:, 8:16, :]
            tile_dst[t] = dst
            P.op("act", lambda h: h.copy(out=d0, in_=bx0.bf.rearrange("p (a b) -> p a b", a=8)),
                 reads=R(bx0), writes=R(dst))
            P.op("dve", lambda h: h.tensor_copy(out=d1, in_=bx1.bf.rearrange("p (a b) -> p a b", a=8)),
                 reads=R(bx1), writes=R(dst))
            if t == 8:
                P.op("dve", lambda h: h.tensor_scalar(out=hT_halo.t[:, :, 1:2], in0=dst.ap[:, :, 0:1], scalar1=col(COL_MR),
                                                      scalar2=None, op0=ALU.mult), reads=R(dst, cols), writes=R(hT_halo))
            if t == 15:
                P.op("dve", lambda h: h.tensor_scalar(out=hT_halo.t[:, :, 0:1], in0=dst.ap[:, :, 127:128], scalar1=col(COL_ML),
                                                      scalar2=None, op0=ALU.mult), reads=R(dst, cols), writes=R(hT_halo))

        def zbanks(t):
            par = t % 2
            return banks[2 + 2 * par], banks[3 + 2 * par]

        def stC(t):
            own = t < 8
            dst = tile_dst[t]
            bq, bk = zbanks(t)

            def hsrc(kc):
                if own:
                    return dst.ap[:, kc, t * 128:(t + 1) * 128]
                return dst.ap[:, kc, :]

            if own:
                for kc in range(16):
                    P.op("pe", lambda h, kc=kc: h.matmul(bq.ap, lhsT=hsrc(kc), rhs=Wlat.ap[:, kc, 0:512],
                                                        start=(kc == 0), stop=(kc == 15)),
                         reads=R(dst, Wlat), writes=R(bq))
            for kc in range(16):
                P.op("pe", lambda h, kc=kc: h.matmul(bk.ap[:, 0:320], lhsT=hsrc(kc), rhs=Wlat.ap[:, kc, 512:832],
                                                    start=(kc == 0), stop=(kc == 15)),
                     reads=R(dst, Wlat), writes=R(bk))

        qa_n2 = [qa_n, Buf(junk.ap[:, 512:1024], "qa_n_b")]

        def stD(t):
            own = t < 8
            bq, bk = zbanks(t)
            if own:
                P.op("act", lambda h: h.activation(out=junk.ap[:, 0:512], in_=bq.ap, func=AF.Square, accum_out=ssq.t[:, t:t + 1]),
                     reads=R(bq), writes=R(junk, ssq))
            P.op("act", lambda h: h.activation(out=junk.ap[:, 0:256], in_=bk.ap[:, 0:256], func=AF.Square, accum_out=sskv.t[:, t:t + 1]),
                 reads=R(bk), writes=R(junk, sskv))
            P.op("act", lambda h: h.activation(out=junk.ap[:, 0:64], in_=bk.ap[:, 256:320], func=AF.Square, accum_out=sspe.t[:, t:t + 1]),
                 reads=R(bk), writes=R(junk, sspe))
            if own:
                P.op("act", lambda h: h.activation(out=lnt.t[:, 1:2], in_=ssq.t[:, t:t + 1], func=AF.Ln, bias=col(COL_EPS), scale=1.0 / 512),
                     reads=R(ssq, cols), writes=R(lnt))
                P.op("act", lambda h: h.activation(out=rq.t[:, t:t + 1], in_=lnt.t[:, 1:2], func=AF.Exp, scale=-0.5),
                     reads=R(lnt), writes=R(rq))
            P.op("act", lambda h: h.activation(out=lnt.t[:, 2:3], in_=sskv.t[:, t:t + 1], func=AF.Ln, bias=col(COL_EPS), scale=1.0 / 256),
                 reads=R(sskv, cols), writes=R(lnt))
            P.op("act", lambda h: h.activation(out=rkv.t[:, t:t + 1], in_=lnt.t[:, 2:3], func=AF.Exp, scale=-0.5),
                 reads=R(lnt), writes=R(rkv))
            if own:
                P.op("dve", lambda h: h.scalar_tensor_tensor(out=qa_n.ap, in0=bq.ap, scalar=rq.t[:, t:t + 1], in1=rows_a.ap[:, 0:512],
                                                             op0=ALU.mult, op1=ALU.mult), reads=R(bq, rq, rows_a), writes=R(qa_n))
            P.op("dve", lambda h: h.scalar_tensor_tensor(out=kva_n.ap, in0=bk.ap[:, 0:256], scalar=rkv.t[:, t:t + 1],
                                                         in1=rows_a.ap[:, 512:768], op0=ALU.mult, op1=ALU.mult),
                 reads=R(bk, rkv, rows_a), writes=R(kva_n))
            P.op("dve", lambda h: h.tensor_tensor(out=kpe_g.ap, in0=bk.ap[:, 256:320], in1=rows_a.ap[:, 768:832], op=ALU.mult),
                 reads=R(bk, rows_a), writes=R(kpe_g))
            P.op("dve", lambda h: h.tensor_tensor(out=tcb.ap, in0=kpe_g.ap, in1=CS_tm.ap[:, t, 0:64], op=ALU.mult),
                 reads=R(kpe_g, CS_tm), writes=R(tcb))
            P.op("dve", lambda h: h.tensor_tensor(out=tsb.ap, in0=kpe_g.ap, in1=CS_tm.ap[:, t, 64:128], op=ALU.mult),
                 reads=R(kpe_g, CS_tm), writes=R(tsb))
            for rep in range(2):
                o_ = rep * 64
                P.op("dve", lambda h, o_=o_: h.tensor_tensor(out=Kr_tm.ap[:, o_:o_ + 32], in0=tcb.ap[:, 0:32], in1=tsb.ap[:, 32:64], op=ALU.subtract),
                     reads=R(tcb, tsb), writes=R(Kr_tm))
                P.op("dve", lambda h, o_=o_: h.tensor_tensor(out=Kr_tm.ap[:, o_ + 32:o_ + 64], in0=tcb.ap[:, 32:64], in1=tsb.ap[:, 0:32], op=ALU.add),
                     reads=R(tcb, tsb), writes=R(Kr_tm))

        def stE(t):
            own = t < 8
            par = t % 2
            b4 = banks[6 + par]
            if own:
                transposes(qa_n.ap, 4, [b4.bf[:, c * 128:(c + 1) * 128] for c in range(4)], R(qa_n), [b4] * 4)
            transposes(kva_n.ap, 2, [b4.bf[:, (4 + c) * 128:(5 + c) * 128] for c in range(2)], R(kva_n), [b4] * 2)
            transposes(Kr_tm.ap, 1, [b4.bf[:, 768:896]], R(Kr_tm), [b4])
            if own:
                P.op("act", lambda h: h.copy(out=qaT.ap[:, :, t * 128:(t + 1) * 128], in_=b4.bf[:, 0:512].rearrange("p (a b) -> p a b", a=4)),
                     reads=R(b4), writes=R(qaT))
            P.op("dve", lambda h: h.tensor_copy(out=kvaT.ap[:, :, t * 128:(t + 1) * 128], in_=b4.bf[:, 512:768].rearrange("p (a b) -> p a b", a=2)),
                 reads=R(b4), writes=R(kvaT))
            P.op("dve", lambda h: h.tensor_copy(out=KrT.ap[:, t * 128:(t + 1) * 128], in_=b4.bf[:, 768:896]),
                 reads=R(b4), writes=R(KrT))

        stages_1a = [stA, stB, stC, stD, stE]
        for s_ in range(16 + len(stages_1a) - 1):
            for k_ in reversed(range(len(stages_1a))):
                t_ = s_ - k_
                if 0 <= t_ < 16:
                    stages_1a[k_](t_)

        if debug:
            dbg_outs["hT_own"] = (hT_own, [128, 16 * 1024], BF16)
            dbg_outs["qaT"] = (qaT, [128, 4 * 1024], BF16)
            dbg_outs["kvaT"] = (kvaT, [128, 2 * 2048], BF16)
            dbg_outs["KrT"] = (KrT, [128, 2048], BF16)
            dbg_outs["CS_fm"] = (CS_fm, [128, 1024], F32)

        def dump_now(names):
            P.barrier()
            for nm in names:
                buf, shape, dt = dbg_outs.pop(nm)
                dd = nc.dram_tensor("dbg_" + nm, shape, dt, kind="ExternalOutput").ap()
                src = buf.ap
                if len(src.shape) == 3:
                    src = src.rearrange("p a b -> p (a b)")
                P.dma("sp", lambda h, dd=dd, src=src: h.dma_start(out=dd, in_=src), dq["sp"], reads=R(buf))
                dbg_names.append("dbg_" + nm)

        dbg_names = []

        if debug:
            dump_now(["hT_own", "qaT", "kvaT", "KrT", "CS_fm"])
        maybe_stop("p1a")

        P.barrier()
        c1 = Carver(L0, W_BASE)
        KnT = [c1.take(BF16, [2048], f"KnT{i}") for i in range(2)]
        Vsb = [c1.take(BF16, [16, 128], f"V{i}") for i in range(2)]
        QnT = [c1.take(BF16, [1024], f"QnT{i}") for i in range(2)]
        QrT = [c1.take(BF16, [1024], f"QrT{i}") for i in range(2)]
        PT = [c1.take(BF16, [512], f"PT{i}") for i in range(6)]
        sqA = [c1.take(BF16, [512], f"sqA{i}") for i in range(2)]
        sqB = [c1.take(BF16, [512], f"sqB{i}") for i in range(2)]
        sqK = [c1.take(BF16, [512], f"sqK{i}") for i in range(2)]
        lnv = [c1.take(F32, [512], f"lnv{i}") for i in range(2)]
        fq = [c1.take(F32, [512], f"fq{i}") for i in range(2)]
        tmpB = [c1.take(F32, [512], f"tmpB{i}") for i in range(2)]
        rl = [c1.take(F32, [512], f"rl{i}") for i in range(2)]
        acc = [c1.take(F32, [512], f"acc{i}") for i in range(2)]
        onesf = identf
        P.op("dve", lambda h: h.memset(onesf.t[:], 1.0), reads=R(ident), writes=R(onesf))

        wuq4 = wuq.ap.rearrange("p c (h d) -> p c h d", h=16)
        P.op("dve", lambda h: h.tensor_copy(out=Bmat.ap[:, :, :, 0:64], in_=wuq4[:, :, :, 128:192]), reads=R(wuq), writes=R(Bmat))
        P.op("dve", lambda h: h.tensor_scalar(out=Bmat.ap[:, :, :, 64:96], in0=wuq4[:, :, :, 160:192], scalar1=-1.0, scalar2=None, op0=ALU.mult),
             reads=R(wuq), writes=R(Bmat))
        P.op("dve", lambda h: h.tensor_copy(out=Bmat.ap[:, :, :, 96:128], in_=wuq4[:, :, :, 128:160]), reads=R(wuq), writes=R(Bmat))

        bS, bO, bL, bB, bSS, bKV = [banks[0], banks[1]], [banks[2], banks[3]], banks[4], banks[5], banks[6], banks[7]
        bA = bKV

        def upproj_stages(hh):
            hp = hh % 2
            Kn, Vv, Qn, Qr = KnT[hp], Vsb[hp], QnT[hp], QrT[hp]
            skt_h, sk_h = skt[hp], sk[hp]
            st = []

            def K_a(tb):
                for kc in range(2):
                    P.op("pe", lambda h, kc=kc: h.matmul(bKV.ap, lhsT=wukv.ap[:, kc, hh * 256:hh * 256 + 128],
                                                        rhs=kvaT.ap[:, kc, tb * 512:(tb + 1) * 512], start=(kc == 0), stop=(kc == 1)),
                         reads=R(wukv, kvaT), writes=R(bKV))

            def K_b(tb):
                sq_ = sqK[tb % 2]
                P.op("act", lambda h: h.activation(out=sq_.ap, in_=bKV.ap, func=AF.Square), reads=R(bKV), writes=R(sq_))
                P.op("dve", lambda h: h.tensor_scalar(out=Kn.ap[:, tb * 512:(tb + 1) * 512], in0=bKV.ap, scalar1=col(COL_GKN),
                                                      scalar2=None, op0=ALU.mult), reads=R(bKV, cols), writes=R(Kn))

            def K_c(tb):
                sq_ = sqK[tb % 2]
                for j in range(4):
                    kt = tb * 4 + j
                    P.op("pe", lambda h, j=j, kt=kt: h.matmul(bSS.ap[:, kt:kt + 1], lhsT=sq_.ap[:, j * 128:(j + 1) * 128],
                                                             rhs=ones.t[:, 0:1], start=True, stop=True),
                         reads=R(sq_, ones), writes=R(bSS))

            def K_d():
                P.op("dve", lambda h: h.tensor_tensor(out=skt_h.t[:], in0=bSS.ap[:, 0:16], in1=sspe.t[:], op=ALU.add),
                     reads=R(bSS, sspe), writes=R(skt_h))

            def K_e():
                P.op("act", lambda h: h.activation(out=skt_h.t[:], in_=skt_h.t[:], func=AF.Ln, bias=col(COL_EPS), scale=1.0 / 192),
                     reads=R(skt_h, cols), writes=R(skt_h))
                P.op("act", lambda h: h.activation(out=sk_h.t[:], in_=skt_h.t[:], func=AF.Exp, bias=col(COL_LNS), scale=-0.5),
                     reads=R(skt_h, cols), writes=R(sk_h))

            def V_a(rnd):
                for j in range(4):
                    kt = rnd * 4 + j
                    for kc in range(2):
                        P.op("pe", lambda h, kc=kc, kt=kt, j=j: h.matmul(bKV.ap[:, j * 128:(j + 1) * 128], lhsT=kvaT.ap[:, kc, kt * 128:(kt + 1) * 128],
                                                                       rhs=wukv.ap[:, kc, hh * 256 + 128:hh * 256 + 256],
                                                                       start=(kc == 0), stop=(kc == 1), skip_group_check=True),
                             reads=R(kvaT, wukv), writes=R(bKV))

            def V_b(rnd):
                eng = "act" if rnd % 2 == 0 else "dve"
                if eng == "act":
                    P.op("act", lambda h: h.copy(out=Vv.ap[:, rnd * 4:(rnd + 1) * 4, :], in_=bKV.ap.rearrange("p (a b) -> p a b", a=4)),
                         reads=R(bKV), writes=R(Vv))
                else:
                    P.op("dve", lambda h: h.tensor_copy(out=Vv.ap[:, rnd * 4:(rnd + 1) * 4, :], in_=bKV.ap.rearrange("p (a b) -> p a b", a=4)),
                         reads=R(bKV), writes=R(Vv))

            def Q_a(qb):
                qs = slice(qb * 512, (qb + 1) * 512)
                for kc in range(4):
                    P.op("pe", lambda h, kc=kc: h.matmul(bA.ap, lhsT=wuq.ap[:, kc, hh * 192:hh * 192 + 128], rhs=qaT.ap[:, kc, qs],
                                                        start=(kc == 0), stop=(kc == 3)), reads=R(wuq, qaT), writes=R(bA))
                for kc in range(4):
                    P.op("pe", lambda h, kc=kc: h.matmul(bB.ap, lhsT=Bmat.ap[:, kc, hh, :], rhs=qaT.ap[:, kc, qs],
                                                        start=(kc == 0), stop=(kc == 3)), reads=R(Bmat, qaT), writes=R(bB))

            def Q_b(qb):
                sa, sb_ = sqA[qb], sqB[qb]
                P.op("act", lambda h: h.activation(out=sa.ap, in_=bA.ap, func=AF.Square), reads=R(bA), writes=R(sa))
                P.op("act", lambda h: h.activation(out=sb_.ap[0:64, :], in_=bB.ap[0:64, :], func=AF.Square), reads=R(bB), writes=R(sb_))

            def Q_c(qb):
                sa, sb_ = sqA[qb], sqB[qb]
                P.op("pe", lambda h: h.matmul(bSS.ap, lhsT=ones.t[:], rhs=sa.ap, start=True, stop=False), reads=R(ones, sa), writes=R(bSS))
                P.op("pe", lambda h: h.matmul(bSS.ap, lhsT=ones.t[0:64, :], rhs=sb_.ap[0:64, :], start=False, stop=True),
                     reads=R(ones, sb_), writes=R(bSS))

            def Q_d(qb):
                lv, f_ = lnv[qb], fq[qb]
                P.op("act", lambda h: h.activation(out=lv.ap, in_=bSS.ap, func=AF.Ln, bias=col(COL_EPS), scale=1.0 / 192),
                     reads=R(bSS, cols), writes=R(lv))
                P.op("act", lambda h: h.activation(out=f_.ap, in_=lv.ap, func=AF.Exp, scale=-0.5), reads=R(lv), writes=R(f_))

            def Q_e(qb):
                qs = slice(qb * 512, (qb + 1) * 512)
                f_, tb_ = fq[qb], tmpB[qb]
                P.op("dve", lambda h: h.scalar_tensor_tensor(out=Qn.ap[:, qs], in0=bA.ap, scalar=col(COL_GQN), in1=f_.ap,
                                                             op0=ALU.mult, op1=ALU.mult), reads=R(bA, cols, f_), writes=R(Qn))
                P.op("dve", lambda h: h.scalar_tensor_tensor(out=tb_.ap, in0=bB.ap, scalar=col(COL_GQAUG), in1=f_.ap,
                                                             op0=ALU.mult, op1=ALU.mult), reads=R(bB, cols, f_), writes=R(tb_))
                P.op("dve", lambda h: h.tensor_tensor(out=Qr.ap[:, qs], in0=tb_.ap, in1=CS_fm.ap[:, qs], op=ALU.mult),
                     reads=R(tb_, CS_fm), writes=R(Qr))

            for tb in range(4):
                st += [lambda tb=tb: K_a(tb), lambda tb=tb: K_b(tb), lambda tb=tb: K_c(tb)]
            st += [K_d, K_e]
            for rnd in range(4):
                st += [lambda rnd=rnd: V_a(rnd), lambda rnd=rnd: V_b(rnd)]
            for qb in range(2):
                st += [lambda qb=qb: Q_a(qb), lambda qb=qb: Q_b(qb), lambda qb=qb: Q_c(qb), lambda qb=qb: Q_d(qb), lambda qb=qb: Q_e(qb)]
            return st

        pt_i = 0
        gstep = 0
        pending = []
        for st_ in upproj_stages(0):
            st_()
        for hh in range(NH):
            hp = hh % 2
            Kn, Vv, Qn, Qr = KnT[hp], Vsb[hp], QnT[hp], QrT[hp]
            sk_h = sk[hp]
            nxt = upproj_stages(hh + 1) if hh + 1 < NH else []
            step = 0
            for qb in range(2):
                qs = slice(qb * 512, (qb + 1) * 512)
                bo, ac = bO[qb], acc[qb]

                def emit_S(kt, Kn=Kn, Qn=Qn, Qr=Qr, qs=qs):
                    bs = bS[kt % 2]
                    P.op("pe", lambda h: h.matmul(bs.ap, lhsT=Kn.ap[:, kt * 128:(kt + 1) * 128], rhs=Qn.ap[:, qs], start=True, stop=False),
                         reads=R(Kn, Qn), writes=R(bs))
                    P.op("pe", lambda h: h.matmul(bs.ap, lhsT=KrT.ap[:, kt * 128:(kt + 1) * 128], rhs=Qr.ap[:, qs], start=False, stop=True),
                         reads=R(KrT, Qr), writes=R(bs))

                emit_S(0)
                for kt in range(16):
                    for due_, fn_ in [p_ for p_ in pending if p_[0] <= gstep]:
                        fn_()
                    pending[:] = [p_ for p_ in pending if p_[0] > gstep]
                    gstep += 1
                    pt = PT[pt_i % 6]
                    pt_i += 1
                    bs = bS[kt % 2]
                    P.op("act", lambda h, pt=pt, bs=bs, kt=kt, sk_h=sk_h: h.activation(out=pt.ap, in_=bs.ap, func=AF.Exp, scale=sk_h.t[:, kt:kt + 1]),
                         reads=R(bs, sk_h), writes=R(pt))
                    if kt + 1 < 16:
                        emit_S(kt + 1)
                    if kt == 0:
                        P.op("dve", lambda h, pt=pt, ac=ac: h.tensor_copy(out=ac.ap, in_=pt.ap), reads=R(pt), writes=R(ac))
                    else:
                        P.op("dve", lambda h, pt=pt, ac=ac: h.tensor_tensor(out=ac.ap, in0=ac.ap, in1=pt.ap, op=ALU.add), reads=R(ac, pt), writes=R(ac))
                    if step < len(nxt):
                        nxt[step]()
                    step += 1
                    P.op("pe", lambda h, pt=pt, kt=kt, Vv=Vv, bo=bo: h.matmul(bo.ap, lhsT=Vv.ap[:, kt, :], rhs=pt.ap, start=(kt == 0), stop=(kt == 15)),
                         reads=R(Vv, pt), writes=R(bo))
                r_ = rl[qb]

                def tail1(ac=ac):
                    P.op("pe", lambda h: h.matmul(bL.ap, lhsT=onesf.t[:], rhs=ac.ap, start=True, stop=True), reads=R(onesf, ac), writes=R(bL))

                def tail2(r_=r_):
                    P.op("act", lambda h: h.activation(out=r_.ap, in_=bL.ap, func=AF.Ln), reads=R(bL), writes=R(r_))
                    P.op("act", lambda h: h.activation(out=r_.ap, in_=r_.ap, func=AF.Exp, scale=-1.0), reads=R(r_), writes=R(r_))

                def tail3(r_=r_, hh=hh, qs=qs, bo=bo):
                    P.op("dve", lambda h: h.tensor_tensor(out=AT.ap[:, hh, qs], in0=bo.ap, in1=r_.ap, op=ALU.mult),
                         reads=R(bo, r_), writes=R(AT))

                pending.extend([(gstep + 1, tail1), (gstep + 2, tail2), (gstep + 4, tail3)])
            assert step >= len(nxt)
        for due_, fn_ in sorted(pending, key=lambda p_: p_[0]):
            fn_()

        if debug:
            dbg_outs["AT"] = (AT, [128, 16 * 1024], BF16)
            dump_now(["AT"])
        maybe_stop("p1b")

        P.barrier()
        slab = [view(L0 + i * 28672, BF16, [14336], f"slab{i}") for i in range(2)]
        cvg = view(L0 + 57344, BF16, [8, 1024], "cvg")
        merged = view(L0 + 73728, BF16, [16, 1024], "merged")
        Fs = [view(65536 + i * 2048, F32, [512], f"F{i}") for i in range(8)]
        ubuf = view(65536 + 16384, F32, [1026], "ubuf")
        ybuf = Buf(arena[:, (65536 + 6 * 2048) // 4:(65536 + 8 * 2048) // 4], "ybuf")
        ybuf.rl = [Fs[6].res, Fs[7].res]
        slab_i = 0

        def next_slab():
            nonlocal slab_i
            s_ = slab[slab_i % 2]
            slab_i += 1
            return s_

        for sgrp in range(4):
            sl = next_slab()
            slv = sl.ap[:, 0:8192].rearrange("p (c n) -> p c n", c=16)
            ld("pool", slv, w_in_v[:, :, C_GA + sgrp * 512:C_GA + (sgrp + 1) * 512], sl)
            for mm_ in range(4):
                m = sgrp * 4 + mm_
                for tb in range(2):
                    bk = banks[(m * 2 + tb) % 4]
                    ts_ = slice(tb * 512, (tb + 1) * 512)
                    for kc in range(16):
                        P.op("pe", lambda h, kc=kc, bk=bk, slv=slv, mm_=mm_, ts_=ts_: h.matmul(bk.ap, lhsT=slv[:, kc, mm_ * 128:(mm_ + 1) * 128],
                                                                                           rhs=hT_own.ap[:, kc, ts_], start=(kc == 0), stop=(kc == 15)),
                             reads=R(sl, hT_own), writes=R(bk))
                    sg = Fs[tb]
                    sgv = sg.ap.bitcast(BF16)[:, 0:512]
                    P.op("act", lambda h, bk=bk, sgv=sgv: h.activation(out=sgv, in_=bk.ap, func=AF.Silu), reads=R(bk), writes=R(sg))
                    P.op("dve", lambda h, m=m, ts_=ts_, sgv=sgv: h.tensor_tensor(out=AT.ap[:, m, ts_], in0=AT.ap[:, m, ts_], in1=sgv, op=ALU.mult),
                         reads=R(AT, sg), writes=R(AT))

        for c in range(8):
            sl = next_slab()
            slv = sl.ap[:, 0:8192].rearrange("p (c g n) -> p c g n", c=16, g=4)
            for g_ in range(4):
                c_lo = C_CB + g_ * 1024 + c * 128
                ld("pool", slv[:, :, g_, :], w_in_v[:, :, c_lo:c_lo + 128], sl)

            def proj(bk, g, rhs_fn, ncols, slv=slv, sl=sl):
                for kc in range(16):
                    P.op("pe", lambda h, kc=kc: h.matmul(bk.ap[:, 0:ncols] if ncols < 512 else bk.ap, lhsT=slv[:, kc, g, :], rhs=rhs_fn(kc),
                                                       start=(kc == 0), stop=(kc == 15)), reads=R(sl, hT_own, hT_halo), writes=R(bk))

            for tb in range(2):
                ts_ = slice(tb * 512, (tb + 1) * 512)
                bc_, bx_ = banks[tb], banks[2 + tb]
                proj(bc_, 1, lambda kc, ts_=ts_: hT_own.ap[:, kc, ts_], 512)
                proj(bx_, 2, lambda kc, ts_=ts_: hT_own.ap[:, kc, ts_], 512)
                cc = Fs[tb]
                P.op("act", lambda h, cc=cc, bc_=bc_: h.copy(out=cc.ap, in_=bc_.ap), reads=R(bc_), writes=R(cc))
                P.op("dve", lambda h, cc=cc, bx_=bx_, tb=tb: h.tensor_tensor(out=ubuf.ap[:, 1 + tb * 512:1 + (tb + 1) * 512], in0=bx_.ap, in1=cc.ap, op=ALU.mult),
                     reads=R(bx_, cc), writes=R(ubuf))
            bh = banks[4]
            for kc in range(16):
                P.op("pe", lambda h, kc=kc, slv=slv: h.matmul(bh.ap[:, 0:2], lhsT=slv[:, kc, 1, :], rhs=hT_halo.t[:, kc, :], start=(kc == 0), stop=(kc == 15)),
                     reads=R(sl, hT_halo), writes=R(bh))
            for kc in range(16):
                P.op("pe", lambda h, kc=kc, slv=slv: h.matmul(bh.ap[:, 2:4], lhsT=slv[:, kc, 2, :], rhs=hT_halo.t[:, kc, :], start=(kc == 0), stop=(kc == 15),
                                                            skip_group_check=True), reads=R(sl, hT_halo), writes=R(bh))
            P.op("act", lambda h: h.copy(out=hcx.t[:, 0:2], in_=bh.ap[:, 0:2]), reads=R(bh), writes=R(hcx))
            P.op("dve", lambda h: h.tensor_tensor(out=ubuf.ap[:, 0:1], in0=bh.ap[:, 2:3], in1=hcx.t[:, 0:1], op=ALU.mult),
                 reads=R(bh, hcx), writes=R(ubuf))
            P.op("dve", lambda h: h.tensor_tensor(out=ubuf.ap[:, 1025:1026], in0=bh.ap[:, 3:4], in1=hcx.t[:, 1:2], op=ALU.mult),
                 reads=R(bh, hcx), writes=R(ubuf))
            P.op("dve", lambda h, c=c: h.tensor_scalar(out=ybuf.ap, in0=ubuf.ap[:, 0:1024], scalar1=col(COL_CW + 3 * c), scalar2=col(COL_CBIAS + c),
                                                      op0=ALU.mult, op1=ALU.add), reads=R(ubuf, cols), writes=R(ybuf))
            P.op("dve", lambda h, c=c: h.scalar_tensor_tensor(out=ybuf.ap, in0=ubuf.ap[:, 1:1025], scalar=col(COL_CW + 3 * c + 1), in1=ybuf.ap,
                                                             op0=ALU.mult, op1=ALU.add), reads=R(ubuf, cols, ybuf), writes=R(ybuf))
            P.op("dve", lambda h, c=c: h.scalar_tensor_tensor(out=ybuf.ap, in0=ubuf.ap[:, 2:1026], scalar=col(COL_CW + 3 * c + 2), in1=ybuf.ap,
                                                             op0=ALU.mult, op1=ALU.add), reads=R(ubuf, cols, ybuf), writes=R(ybuf))
            for tb in range(2):
                ts_ = slice(tb * 512, (tb + 1) * 512)
                bb_, bg_ = banks[5 + tb], banks[(7 + tb) if tb == 0 else 4]
                proj(bb_, 0, lambda kc, ts_=ts_: hT_own.ap[:, kc, ts_], 512)
                proj(bg_, 3, lambda kc, ts_=ts_: hT_own.ap[:, kc, ts_], 512)
                sgc, t1 = Fs[2 + tb], Fs[4 + tb]
                P.op("act", lambda h, sgc=sgc, bg_=bg_: h.activation(out=sgc.ap, in_=bg_.ap, func=AF.Silu), reads=R(bg_), writes=R(sgc))
                P.op("dve", lambda h, t1=t1, bb_=bb_, ts_=ts_: h.tensor_tensor(out=t1.ap, in0=bb_.ap, in1=ybuf.ap[:, ts_], op=ALU.mult),
                     reads=R(bb_, ybuf), writes=R(t1))
                P.op("dve", lambda h, t1=t1, sgc=sgc, c=c, ts_=ts_: h.tensor_tensor(out=cvg.ap[:, c, ts_], in0=t1.ap, in1=sgc.ap, op=ALU.mult),
                     reads=R(t1, sgc), writes=R(cvg))

        for g in range(8):
            sl = next_slab()
            s_ma = sl.ap[:, 0:4096].rearrange("p (c n) -> p c n", c=16)
            s_mc = sl.ap[:, 4096:8192].rearrange("p (c n) -> p c n", c=16)
            s_ba = sl.ap[:, 8192:12288].rearrange("p (c n) -> p c n", c=16)
            s_bc = sl.ap[:, 12288:14336].rearrange("p (c n) -> p c n", c=8)
            cs_ = slice(g * 256, (g + 1) * 256)
            sl.slot = getattr(sl, "slot", None) or P.slot("ld_" + sl.res.name)
            ld("pool", s_ma, w_in_v[:, :, C_MA + g * 256:C_MA + (g + 1) * 256], sl)
            ld("pool", s_mc, w_in_v[:, :, C_MC + g * 256:C_MC + (g + 1) * 256], sl)
            ld("pool", s_ba, w_ba_v[:, :, cs_], sl)
            ld("pool", s_bc, w_bc_v[:, :, cs_], sl)
            for jj in range(2):
                j = g * 2 + jj
                js = slice(jj * 128, (jj + 1) * 128)
                for tb in range(2):
                    ts_ = slice(tb * 512, (tb + 1) * 512)
                    par = (j * 2 + tb) % 2
                    b_ma, b_mc, b_ya, b_yc = banks[par * 4], banks[par * 4 + 1], banks[par * 4 + 2], banks[par * 4 + 3]
                    for kc in range(16):
                        P.op("pe", lambda h, kc=kc, b_ma=b_ma, s_ma=s_ma, js=js, ts_=ts_: h.matmul(b_ma.ap, lhsT=s_ma[:, kc, js], rhs=hT_own.ap[:, kc, ts_],
                                                                                                start=(kc == 0), stop=(kc == 15)), reads=R(sl, hT_own), writes=R(b_ma))
                    for kc in range(16):
                        P.op("pe", lambda h, kc=kc, b_mc=b_mc, s_mc=s_mc, js=js, ts_=ts_: h.matmul(b_mc.ap, lhsT=s_mc[:, kc, js], rhs=hT_own.ap[:, kc, ts_],
                                                                                                start=(kc == 0), stop=(kc == 15)), reads=R(sl, hT_own), writes=R(b_mc))
                    for kc in range(16):
                        P.op("pe", lambda h, kc=kc, b_ya=b_ya, s_ba=s_ba, js=js, ts_=ts_: h.matmul(b_ya.ap, lhsT=s_ba[:, kc, js], rhs=AT.ap[:, kc, ts_],
                                                                                                start=(kc == 0), stop=(kc == 15)), reads=R(sl, AT), writes=R(b_ya))
                    for kc in range(8):
                        P.op("pe", lambda h, kc=kc, b_yc=b_yc, s_bc=s_bc, js=js, ts_=ts_: h.matmul(b_yc.ap, lhsT=s_bc[:, kc, js], rhs=cvg.ap[:, kc, ts_],
                                                                                                start=(kc == 0), stop=(kc == 7)), reads=R(sl, cvg), writes=R(b_yc))
                    ta, tcg, m1, m2 = Fs[par], Fs[2 + par], Fs[4 + par], Fs[6 + par]
                    P.op("act", lambda h, ta=ta, b_ma=b_ma: h.activation(out=ta.ap, in_=b_ma.ap, func=AF.Tanh, scale=0.5), reads=R(b_ma), writes=R(ta))
                    P.op("act", lambda h, tcg=tcg, b_mc=b_mc: h.activation(out=tcg.ap, in_=b_mc.ap, func=AF.Tanh, scale=0.5), reads=R(b_mc), writes=R(tcg))
                    P.op("dve", lambda h, ta=ta, m1=m1, b_ya=b_ya: h.scalar_tensor_tensor(out=m1.ap, in0=ta.ap, scalar=1.0, in1=b_ya.ap, op0=ALU.add, op1=ALU.mult),
                         reads=R(ta, b_ya), writes=R(m1))
                    P.op("dve", lambda h, tcg=tcg, m2=m2, b_yc=b_yc: h.scalar_tensor_tensor(out=m2.ap, in0=tcg.ap, scalar=1.0, in1=b_yc.ap, op0=ALU.add, op1=ALU.mult),
                         reads=R(tcg, b_yc), writes=R(m2))
                    P.op("dve", lambda h, m1=m1, m2=m2, j=j, ts_=ts_: h.tensor_tensor(out=merged.ap[:, j, ts_], in0=m1.ap, in1=m2.ap, op=ALU.add),
                         reads=R(m1, m2), writes=R(merged))

        if debug:
            dbg_outs["cvg"] = (cvg, [128, 8 * 1024], BF16)
            dbg_outs["merged"] = (merged, [128, 16 * 1024], BF16)
            dbg_outs["AT2"] = (AT, [128, 16 * 1024], BF16)
            dump_now(["cvg", "merged", "AT2"])
        maybe_stop("p2c")

        x2 = view(0, F32, [8, 2048], "x2")
        x2t = []
        for t in range(8):
            x2t.append(Buf(x2.ap[:, t, :], f"x2_{t}"))

        def res_deps(res):
            return ([res.w] if res.w is not None else []) + list(res.r.values()) + list(res.rd)
        S_ALL = [f_.res for f_ in Fs] + [ubuf.res]
        h2T = view(L0 + 73728, BF16, [16, 1024], "h2T")
        h2T.rl = [h2T.res, merged.res]
        pleg = view(L0 + 57344, F32, [2048], "pleg")
        wpp = view(196608, BF16, [2, 2048], "wpp")
        c3 = Carver(65536, 90112)

        def take_s(dt, shape, name):
            b_ = c3.take(dt, shape, name)
            b_.rl = [b_.res] + S_ALL
            return b_

        pT = take_s(BF16, [2, 1024], "pT")
        ptm = [take_s(F32, [256], f"ptm{i}") for i in range(2)]
        pbb = [take_s(BF16, [256], f"pbb{i}") for i in range(2)]
        h2b = [take_s(BF16, [2048], f"h2b{i}") for i in range(2)]
        tg = [take_s(F32, [512], f"tg{i}") for i in range(2)]
        e1 = [take_s(F32, [512], f"e1{i}") for i in range(2)]

        ld("pool", wpp.ap, w_pp_v, wpp)
        old_deps = res_deps(hT_own.res) + res_deps(AT.res)
        for t in range(8):
            ld("sp", x2t[t].ap, x_d[t * 128:(t + 1) * 128, :], x2t[t], deps=old_deps)
        ld("sp", pleg.ap, pleg_d, pleg, deps=res_deps(cvg.res))

        def norm2_stats(t):
            hb2 = h2b[t % 2]
            P.op("act", lambda h: h.activation(out=hb2.ap, in_=x2t[t].ap, func=AF.Square, accum_out=ss2.t[:, t:t + 1]),
                 reads=R(x2t[t]), writes=R(hb2, ss2))
            P.op("act", lambda h: h.activation(out=lnt.t[:, t:t + 1], in_=ss2.t[:, t:t + 1], func=AF.Ln, bias=col(COL_EPS), scale=1.0 / D),
                 reads=R(ss2, cols), writes=R(lnt))
            P.op("act", lambda h: h.activation(out=rs2.t[:, t:t + 1], in_=lnt.t[:, t:t + 1], func=AF.Exp, scale=-0.5), reads=R(lnt), writes=R(rs2))
            P.op("dve", lambda h: h.scalar_tensor_tensor(out=hb2.ap, in0=x2t[t].ap, scalar=rs2.t[:, t:t + 1], in1=pleg.ap, op0=ALU.mult, op1=ALU.mult),
                 reads=R(x2t[t], rs2, pleg), writes=R(hb2))
            pm, pb_ = ptm[t % 2], pbb[t % 2]
            ld("sp", pm.ap, p_d[t * 128:(t + 1) * 128, :], pm)
            P.op("dve", lambda h: h.tensor_copy(out=pb_.ap, in_=pm.ap), reads=R(pm), writes=R(pb_))

        def norm2_transposes(t):
            hb2, pb_ = h2b[t % 2], pbb[t % 2]
            bviews = [banks[4 + c // 8].bf[:, (c % 8) * 128:(c % 8 + 1) * 128] for c in range(16)]
            transposes(hb2.ap, 16, bviews, R(hb2), [banks[4 + c // 8] for c in range(16)])
            P.op("act", lambda h: h.copy(out=h2T.ap[:, 0:8, t * 128:(t + 1) * 128], in_=banks[4].bf.rearrange("p (a b) -> p a b", a=8)),
                 reads=R(banks[4]), writes=R(h2T))
            P.op("dve", lambda h: h.tensor_copy(out=h2T.ap[:, 8:16, t * 128:(t + 1) * 128], in_=banks[5].bf.rearrange("p (a b) -> p a b", a=8)),
                 reads=R(banks[5]), writes=R(h2T))
            transposes(pb_.ap, 2, [banks[6].bf[:, c * 128:(c + 1) * 128] for c in range(2)], R(pb_), [banks[6]] * 2)
            P.op("act", lambda h: h.copy(out=pT.ap[:, :, t * 128:(t + 1) * 128], in_=banks[6].bf[:, 0:256].rearrange("p (a b) -> p a b", a=2)),
                 reads=R(banks[6]), writes=R(pT))

        for n in range(4):
            sl = next_slab()
            w_ap = sl.ap[:, 0:8192].rearrange("p (c n) -> p c n", c=16)
            ns = slice(n * 512, (n + 1) * 512)
            ld("pool", w_ap, w_out_v[:, :, ns], sl)
            for t in range(8):
                bk = banks[(n * 8 + t) % 4]
                for kc in range(16):
                    P.op("pe", lambda h, kc=kc, bk=bk, w_ap=w_ap, t=t: h.matmul(bk.ap, lhsT=merged.ap[:, kc, t * 128:(t + 1) * 128], rhs=w_ap[:, kc, :],
                                                                            start=(kc == 0), stop=(kc == 15)), reads=R(merged, sl), writes=R(bk))
                P.op("dve", lambda h, bk=bk, t=t, ns=ns: h.scalar_tensor_tensor(out=x2t[t].ap[:, ns], in0=bk.ap, scalar=0.5, in1=x2t[t].ap[:, ns],
                                                                             op0=ALU.mult, op1=ALU.add), reads=R(bk, x2t[t]), writes=R(x2t[t]))
                if n == 3:
                    if t >= 2:
                        norm2_transposes(t - 2)
                    norm2_stats(t)
        norm2_transposes(6)
        norm2_transposes(7)
        so = P.slot("out")
        for n in range(4):
            sl = next_slab()
            w_ap = sl.ap[:, 0:8192].rearrange("p (c n) -> p c n", c=16)
            ns = slice(n * 512, (n + 1) * 512)
            ld("pool", w_ap, w_pg_v[:, :, ns], sl)
            for t in range(8):
                par = (n * 8 + t) % 2
                bg_, bp_ = banks[par * 2], banks[par * 2 + 1]
                for kc in range(16):
                    P.op("pe", lambda h, kc=kc, bg_=bg_, w_ap=w_ap, t=t: h.matmul(bg_.ap, lhsT=h2T.ap[:, kc, t * 128:(t + 1) * 128], rhs=w_ap[:, kc, :],
                                                                              start=(kc == 0), stop=(kc == 15)), reads=R(h2T, sl), writes=R(bg_))
                for kc in range(2):
                    P.op("pe", lambda h, kc=kc, bp_=bp_, t=t, ns=ns: h.matmul(bp_.ap, lhsT=pT.ap[:, kc, t * 128:(t + 1) * 128], rhs=wpp.ap[:, kc, ns],
                                                                          start=(kc == 0), stop=(kc == 1)), reads=R(pT, wpp), writes=R(bp_))
                tg_, e1_ = tg[par], e1[par]
                P.op("act", lambda h, tg_=tg_, bg_=bg_: h.activation(out=tg_.ap, in_=bg_.ap, func=AF.Tanh, scale=0.5), reads=R(bg_), writes=R(tg_))
                P.op("dve", lambda h, tg_=tg_, e1_=e1_, bp_=bp_: h.scalar_tensor_tensor(out=e1_.ap, in0=tg_.ap, scalar=1.0, in1=bp_.ap, op0=ALU.add, op1=ALU.mult),
                     reads=R(tg_, bp_), writes=R(e1_))
                P.op("dve", lambda h, e1_=e1_, t=t, ns=ns: h.scalar_tensor_tensor(out=x2t[t].ap[:, ns], in0=e1_.ap, scalar=0.5, in1=x2t[t].ap[:, ns],
                                                                               op0=ALU.mult, op1=ALU.add), reads=R(e1_, x2t[t]), writes=R(x2t[t]))
                if n == 3:
                    P.dma("sp", lambda h, t=t: h.dma_start(out=out_d[t * 128:(t + 1) * 128, :], in_=x2t[t].ap), so, reads=R(x2t[t]))
        fin_deps = [o for o in P.ops["sp"] if o.slot is so][-1:]
        if dbg_names:
            fin_deps += [o for o in P.ops["sp"] if o.slot is dq["sp"]][-1:]
        P.op("sp", lambda h: h.nop(), deps=fin_deps, real=False)

        P.finalize()
        print("ops", {e: len(P.ops[e]) for e in P.ENGS}, "signals", {e: sum(1 for o in P.ops[e] if o.signal) for e in P.ENGS}, "slots", len(P.slots))
        for s_ in P.slots:
            s_.sem = es.enter_context(nc.semaphore("d_" + s_.name))
        with nc.Block() as block:
            @block.tensor
            def _(h):
                P.emit("pe", h, sems)

            @block.scalar
            def _(h):
                P.emit("act", h, sems)

            @block.vector
            def _(h):
                P.emit("dve", h, sems)

            @block.gpsimd
            def _(h):
                P.emit("pool", h, sems)

            @block.sync
            def _(h):
                P.emit("sp", h, sems)
    nc._dbg_names = dbg_names
    return nc


def _host_inputs(x, p, positions, norm_g, w_in, q_lat_g, kv_lat_g, w_uq, w_ukv, q_norm_g, k_norm_g,
                 conv_w, conv_b, w_branch_attn, w_branch_conv, w_out, ple_norm_g, w_ple_gate, w_ple_proj):
    f32 = np.float32
    x = np.asarray(x, f32)
    p = np.asarray(p, f32)[0]
    positions = np.asarray(positions, np.int32)
    qg = np.asarray(q_norm_g, f32)[0]
    kg = np.asarray(k_norm_g, f32)[0]
    cw = np.asarray(conv_w, f32)[0]
    cb = np.asarray(conv_b, f32)[0]
    inv_freq = (1.0 / (10000.0 ** (np.arange(0, 64, 2, dtype=np.float32) / 64.0))).astype(f32)

    def rep(v):
        return np.ascontiguousarray(np.broadcast_to(np.asarray(v, f32)[None, :], (128, len(v))))

    shared = {
        "rows_a": np.ascontiguousarray(np.concatenate([rep(np.asarray(q_lat_g, f32)[0]), rep(np.asarray(kv_lat_g, f32)[0]), rep(kg[128:192])], axis=1)),
        "normg_row": rep(np.asarray(norm_g, f32)[0]),
        "pleg_row": rep(np.asarray(ple_norm_g, f32)[0]),
        "invf4": rep(np.tile(inv_freq, 4)),
        "ph_row": rep(np.concatenate([np.full(64, np.pi / 2, f32), np.zeros(64, f32)])),
        "w_in": np.ascontiguousarray(np.asarray(w_in, f32)[0]),
        "w_uq": np.ascontiguousarray(np.asarray(w_uq, f32)[0]),
        "w_ukv": np.ascontiguousarray(np.asarray(w_ukv, f32)[0]),
        "w_ba": np.ascontiguousarray(np.asarray(w_branch_attn, f32)[0]),
        "w_bc": np.ascontiguousarray(np.asarray(w_branch_conv, f32)[0]),
        "w_out": np.ascontiguousarray(np.asarray(w_out, f32)[0]),
        "w_pg": np.ascontiguousarray(np.asarray(w_ple_gate, f32)[0]),
        "w_pp": np.ascontiguousarray(np.asarray(w_ple_proj, f32)[0]),
    }
    cols0 = np.zeros((128, NCOL), f32)
    cols0[:, COL_GQN] = qg[0:128]
    cols0[:, COL_GQAUG] = np.concatenate([qg[128:192], qg[160:192], qg[128:160]])
    cols0[:, COL_GKN] = kg[0:128]
    cols0[:, COL_INVF] = np.tile(inv_freq, 4)
    cols0[:, COL_PH] = np.concatenate([np.full(64, np.pi / 2, f32), np.zeros(64, f32)])
    cols0[:, COL_EPS] = EPS
    cols0[:, COL_LNS] = -0.5 * math.log(192.0)
    for c in range(8):
        for j in range(3):
            cols0[:, COL_CW + 3 * c + j] = cw[j, c * 128:(c + 1) * 128]
        cols0[:, COL_CBIAS + c] = cb[c * 128:(c + 1) * 128]
    in_maps = []
    for core in range(8):
        b, hq = core // 2, core % 2
        own = slice(hq * T, (hq + 1) * T)
        oth = slice((1 - hq) * T, (2 - hq) * T)
        cols = cols0.copy()
        cols[:, COL_ML] = 1.0 if hq == 1 else 0.0
        cols[:, COL_MR] = 1.0 if hq == 0 else 0.0
        pos_loc = np.concatenate([positions[b, own], positions[b, oth]])
        m = dict(shared)
        m["x_loc"] = np.ascontiguousarray(np.concatenate([x[b, own], x[b, oth]], axis=0))
        m["p_loc"] = np.ascontiguousarray(p[b, own])
        m["pos_tm"] = np.ascontiguousarray(pos_loc.reshape(16, 128).T)
        m["pos_row"] = np.ascontiguousarray(np.broadcast_to(positions[b, own][None, :], (128, T)))
        m["cols"] = cols
        in_maps.append(m)
    return in_maps


_NC_CACHE = {}


def kernel(**inputs):
    in_maps = _host_inputs(**inputs)
    if "nc" not in _NC_CACHE:
        _NC_CACHE["nc"] = build_nc(debug=False)
    nc = _NC_CACHE["nc"]
    res = run_bass_kernel_spmd(nc, in_maps, core_ids=list(range(8)))
    out = np.empty((4, S, D), np.float32)
    for core in range(8):
        b, hq = core // 2, core % 2
        out[b, hq * T:(hq + 1) * T, :] = res.results[core]["out"]
    return out
```
